# Optimizing a Trainium2 kernel written in Bass

```python
import jax, jax.numpy as jnp
from jax import lax
import numpy as np

D_MODEL = 1024
BATCH = 8
SEQ = 2048
DEPTH = 4
DEC_BATCH = 128
DEC_SEQ = 4
PAST_LEN = 16384
PAGE_SIZE = 128

POOL_WINDOWS = (2, 4, 8, 16)
POOL_GROUPS = len(POOL_WINDOWS)
POOL_WIDTH = D_MODEL
POOL_GROUP_DIM = POOL_WIDTH // POOL_GROUPS
POOL_HIST = max(POOL_WINDOWS) - 1
MLSTM_WIDTH = 2 * D_MODEL
N_HEADS = 4
HEAD_DIM = MLSTM_WIDTH // N_HEADS
CONV_W = 4
PLE_DIM = 256
CHUNK = 128
EPS = 1e-6
F_BIAS_LO = 3.0
F_BIAS_HI = 6.0
SPLIT_SIZES = (POOL_WIDTH, POOL_WIDTH, 2 * MLSTM_WIDTH, MLSTM_WIDTH, MLSTM_WIDTH, MLSTM_WIDTH, 2 * N_HEADS, 2 * D_MODEL)
N_IN = sum(SPLIT_SIZES)
SPLIT_OFFSETS = tuple(int(v) for v in np.cumsum(SPLIT_SIZES)[:-1])

kernel_name = 'hybrid_pool_mlstm_decode_step'


def rmsnorm(x, w):
    xf = x.astype(jnp.float32)
    y = xf * lax.rsqrt(jnp.mean(xf * xf, axis=-1, keepdims=True) + EPS)
    return (y * w.astype(jnp.float32)).astype(x.dtype)


def head_rmsnorm(h, w):
    y = h * lax.rsqrt(jnp.mean(h * h, axis=-1, keepdims=True) + EPS)
    return y * w.astype(jnp.float32).reshape(N_HEADS, HEAD_DIM)


def causal_conv_silu(z, hist, w, b):
    S = z.shape[1]
    full = jnp.concatenate([hist.astype(z.dtype), z], axis=1)
    out = sum(full[:, j:j + S] * w[j] for j in range(CONV_W)) + b
    return jax.nn.silu(out), full[:, -(CONV_W - 1):]


def multiscale_pool(u, hist, pos0, pool_w, pool_scale):
    B, S, _ = u.shape
    full = jnp.concatenate([hist.astype(u.dtype), u], axis=1)
    cs = jnp.pad(jnp.cumsum(full.astype(jnp.float32), axis=1), ((0, 0), (1, 0), (0, 0)))
    pos = pos0 + jnp.arange(S)
    outs = []
    for g, win in enumerate(POOL_WINDOWS):
        sl = slice(g * POOL_GROUP_DIM, (g + 1) * POOL_GROUP_DIM)
        end = cs[:, POOL_HIST + 1:POOL_HIST + 1 + S, sl]
        start = cs[:, POOL_HIST + 1 - win:POOL_HIST + 1 - win + S, sl]
        cnt = jnp.minimum(pos + 1, win).astype(jnp.float32)[None, :, None]
        outs.append((end - start) / cnt)
    pooled = jnp.concatenate(outs, axis=-1) - u.astype(jnp.float32)
    mixed = jnp.einsum('bsgc,gcd->bsgd', pooled.reshape(B, S, POOL_GROUPS, POOL_GROUP_DIM),
                       pool_w.astype(jnp.float32)).reshape(B, S, POOL_WIDTH)
    return (mixed * pool_scale.astype(jnp.float32)).astype(u.dtype), full[:, -POOL_HIST:]


def mlstm_chunkwise(q, k, v, i_pre, f_pre, c0, n0, m0):
    B, H, S, DH = q.shape
    L = CHUNK if S % CHUNK == 0 else S
    nc = S // L

    def to_chunks(a):
        return jnp.moveaxis(a.reshape(a.shape[:2] + (nc, L) + a.shape[3:]), 2, 0)

    mask = jnp.tril(jnp.ones((L, L), dtype=bool))

    def step(carry, inp):
        c, n, m = carry
        qc, kc, vc, ic, fc = inp
        b = jnp.cumsum(jax.nn.log_sigmoid(fc), axis=-1)
        log_d = jnp.where(mask, b[..., :, None] - b[..., None, :] + ic[..., None, :], -jnp.inf)
        m_inter = b + m[..., None]
        m_t = jnp.maximum(m_inter, jnp.max(log_d, axis=-1))
        d = jnp.exp(log_d - m_t[..., None])
        s = jnp.einsum('bhtd,bhsd->bhts', qc, kc) * d
        g = jnp.exp(m_inter - m_t)
        num = jnp.einsum('bhts,bhse->bhte', s, vc) + g[..., None] * jnp.einsum('bhtd,bhde->bhte', qc, c)
        den = jnp.sum(s, axis=-1) + g * jnp.einsum('bhtd,bhd->bht', qc, n)
        h = num / jnp.maximum(jnp.abs(den), jnp.exp(-m_t))[..., None]
        b_last = b[..., -1]
        m_new = m_t[..., -1]
        decay = jnp.exp(b_last + m - m_new)
        w_s = jnp.exp(b_last[..., None] - b + ic - m_new[..., None])
        kw = kc * w_s[..., None]
        c_new = decay[..., None, None] * c + jnp.einsum('bhsd,bhse->bhde', kw, vc)
        n_new = decay[..., None] * n + jnp.sum(kw, axis=2)
        return (c_new, n_new, m_new), h

    (c, n, m), hs = lax.scan(step, (c0, n0, m0), tuple(to_chunks(a) for a in (q, k, v, i_pre, f_pre)))
    h = jnp.moveaxis(hs, 0, 2).reshape(B, H, S, DH)
    return h, c, n, m


def mixer_layer(x, pe, pool_hist, conv_hist, c0, n0, m0, pos0, lw):
    (norm_w, w_in, b_if, conv_w, conv_b, pool_w, pool_scale, w_pool_down, mlstm_norm_w,
     w_mlstm_down, w_out, w_ple, ple_norm_w, w_ple_gate) = lw
    B, S, _ = x.shape
    h = rmsnorm(x, norm_w)
    pool_in, pool_z, qk_pre, v, o_pre, m_z, if_pre, gate_pre = jnp.split(h @ w_in, SPLIT_OFFSETS, axis=-1)
    a, pool_hist_new = multiscale_pool(pool_in, pool_hist, pos0, pool_w, pool_scale)
    a = (a * jax.nn.silu(pool_z)) @ w_pool_down
    qk, conv_hist_new = causal_conv_silu(qk_pre, conv_hist, conv_w, conv_b)
    q, k = jnp.split(qk, 2, axis=-1)

    def heads(t):
        return t.astype(jnp.float32).reshape(B, S, N_HEADS, HEAD_DIM).transpose(0, 2, 1, 3)

    gates = (if_pre + b_if).astype(jnp.float32).transpose(0, 2, 1)
    hc, c, n, m = mlstm_chunkwise(heads(q), heads(k) * (HEAD_DIM ** -0.5), heads(v),
                                  gates[:, :N_HEADS], gates[:, N_HEADS:],
                                  c0.astype(jnp.float32), n0.astype(jnp.float32), m0.astype(jnp.float32))
    hc = hc.transpose(0, 2, 1, 3) * jax.nn.sigmoid(o_pre.astype(jnp.float32)).reshape(B, S, N_HEADS, HEAD_DIM)
    hc = head_rmsnorm(hc, mlstm_norm_w).reshape(B, S, MLSTM_WIDTH).astype(x.dtype)
    bm = (hc * jax.nn.silu(m_z)) @ w_mlstm_down
    g_a, g_b = jnp.split(jax.nn.sigmoid(gate_pre), 2, axis=-1)
    x = x + (g_a * a + g_b * bm) @ w_out
    x = x + (pe @ w_ple) * jax.nn.sigmoid(rmsnorm(x, ple_norm_w) @ w_ple_gate)
    return x, pool_hist_new, conv_hist_new, c, n, m


def setup_inputs(seed: int = 0) -> dict:
    key = jax.random.key(seed)
    ks = jax.random.split(key, 32)

    def nrm(k, shape, s):
        return jax.random.normal(k, shape, jnp.float32) * s

    f_bias = jnp.linspace(F_BIAS_LO, F_BIAS_HI, N_HEADS, dtype=jnp.float32)[None, :] + nrm(ks[21], (DEPTH, N_HEADS), 0.1)
    i_bias = nrm(ks[22], (DEPTH, N_HEADS), 0.1)
    return {
        'x_prompt': nrm(ks[0], (BATCH, SEQ, D_MODEL), 1.0),
        'x_sample': nrm(ks[1], (DEC_BATCH, DEC_SEQ, D_MODEL), 1.0),
        'state_pool': nrm(ks[2], (DEPTH, DEC_BATCH, POOL_HIST, POOL_WIDTH), 1.0),
        'state_conv': nrm(ks[3], (DEPTH, DEC_BATCH, CONV_W - 1, 2 * MLSTM_WIDTH), 1.0),
        'state_mlstm_C': nrm(ks[4], (DEPTH, DEC_BATCH, N_HEADS, HEAD_DIM, HEAD_DIM), HEAD_DIM ** -0.5),
        'state_mlstm_n': nrm(ks[5], (DEPTH, DEC_BATCH, N_HEADS, HEAD_DIM), 0.1),
        'state_mlstm_m': nrm(ks[6], (DEPTH, DEC_BATCH, N_HEADS), 1.0),
        'p_prompt': nrm(ks[7], (DEPTH, BATCH, SEQ, PLE_DIM), 1.0),
        'p_sample': nrm(ks[8], (DEPTH, DEC_BATCH, DEC_SEQ, PLE_DIM), 1.0),
        'norm_w': 1.0 + nrm(ks[9], (DEPTH, D_MODEL), 0.05),
        'w_in': nrm(ks[10], (DEPTH, D_MODEL, N_IN), D_MODEL ** -0.5),
        'b_if': jnp.concatenate([i_bias, f_bias], axis=-1),
        'conv_w': nrm(ks[11], (DEPTH, CONV_W, 2 * MLSTM_WIDTH), CONV_W ** -0.5),
        'conv_b': nrm(ks[12], (DEPTH, 2 * MLSTM_WIDTH), 0.01),
        'pool_w': nrm(ks[13], (DEPTH, POOL_GROUPS, POOL_GROUP_DIM, POOL_GROUP_DIM), POOL_GROUP_DIM ** -0.5),
        'pool_scale': 1.0 + nrm(ks[14], (DEPTH, POOL_WIDTH), 0.05),
        'w_pool_down': nrm(ks[15], (DEPTH, POOL_WIDTH, D_MODEL), POOL_WIDTH ** -0.5),
        'mlstm_norm_w': 1.0 + nrm(ks[16], (DEPTH, MLSTM_WIDTH), 0.05),
        'w_mlstm_down': nrm(ks[17], (DEPTH, MLSTM_WIDTH, D_MODEL), MLSTM_WIDTH ** -0.5),
        'w_out': nrm(ks[18], (DEPTH, D_MODEL, D_MODEL), D_MODEL ** -0.5),
        'w_ple': nrm(ks[19], (DEPTH, PLE_DIM, D_MODEL), PLE_DIM ** -0.5),
        'ple_norm_w': 1.0 + nrm(ks[20], (DEPTH, D_MODEL), 0.05),
        'w_ple_gate': nrm(ks[23], (DEPTH, D_MODEL, D_MODEL), D_MODEL ** -0.5),
        'final_norm_w': 1.0 + nrm(ks[24], (D_MODEL,), 0.05),
    }


def reference(x_prompt, x_sample, state_pool, state_conv, state_mlstm_C, state_mlstm_n, state_mlstm_m,
              p_prompt, p_sample, norm_w, w_in, b_if, conv_w, conv_b, pool_w, pool_scale, w_pool_down,
              mlstm_norm_w, w_mlstm_down, w_out, w_ple, ple_norm_w, w_ple_gate, final_norm_w):
    B = x_prompt.shape[0]
    dt = x_prompt.dtype
    pool0 = jnp.zeros((B, POOL_HIST, POOL_WIDTH), dt)
    conv0 = jnp.zeros((B, CONV_W - 1, 2 * MLSTM_WIDTH), dt)
    c0 = jnp.zeros((B, N_HEADS, HEAD_DIM, HEAD_DIM), jnp.float32)
    n0 = jnp.zeros((B, N_HEADS, HEAD_DIM), jnp.float32)
    m0 = jnp.zeros((B, N_HEADS), jnp.float32)
    xp, xs = x_prompt, x_sample
    pool_p, pool_s, conv_p, conv_s, c_p, c_s, n_p, n_s, m_p, m_s = ([] for _ in range(10))
    for l in range(DEPTH):
        lw = (norm_w[l], w_in[l], b_if[l], conv_w[l], conv_b[l], pool_w[l], pool_scale[l], w_pool_down[l],
              mlstm_norm_w[l], w_mlstm_down[l], w_out[l], w_ple[l], ple_norm_w[l], w_ple_gate[l])
        xp, ph, ch, c, n, m = mixer_layer(xp, p_prompt[l], pool0, conv0, c0, n0, m0, 0, lw)
        pool_p.append(ph); conv_p.append(ch); c_p.append(c); n_p.append(n); m_p.append(m)
        xs, ph, ch, c, n, m = mixer_layer(xs, p_sample[l], state_pool[l], state_conv[l], state_mlstm_C[l],
                                          state_mlstm_n[l], state_mlstm_m[l], PAST_LEN, lw)
        pool_s.append(ph); conv_s.append(ch); c_s.append(c); n_s.append(n); m_s.append(m)
    y_prompt = rmsnorm(xp, final_norm_w)
    y_sample = rmsnorm(xs, final_norm_w)
    return (y_prompt, y_sample, jnp.stack(pool_p), jnp.stack(pool_s), jnp.stack(conv_p), jnp.stack(conv_s),
            jnp.stack(c_p), jnp.stack(c_s), jnp.stack(n_p), jnp.stack(n_s), jnp.stack(m_p), jnp.stack(m_s))
```

```python
import math
from contextlib import ExitStack

import numpy as np
import ml_dtypes

import concourse.bass as bass
import concourse.mybir as mybir
from concourse.bass_utils import run_bass_kernel_spmd

F32 = mybir.dt.float32
BF16 = mybir.dt.bfloat16
AF = mybir.ActivationFunctionType
ALU = mybir.AluOpType
AX = mybir.AxisListType

D = 1024
NIN = 14344
O_PIN, O_PZ, O_QK, O_V, O_O, O_MZ, O_IF, O_GA, O_GB = 0, 1024, 2048, 6144, 8192, 10240, 12288, 12296, 13320
EPS = 1e-6
LNSCALE = math.log(512 ** -0.5)
NEG = -30000.0
NSEQ = 16
TT = 512
NPRM = 8 + 8 + 8 + 32 + 128
DEBUG = False
DBG_TILE = 0


class Buf:
    __slots__ = ("name", "t", "w", "r", "sem", "dcount")

    def __init__(self, name, t):
        self.name = name
        self.t = t
        self.w = {}
        self.r = {}
        self.sem = None
        self.dcount = 0


class Op:
    __slots__ = ("eng", "fn", "deps", "dma", "tok", "sig", "idx")


class Prog:
    ENG = ("pe", "act", "dve", "pool", "sp")

    def __init__(self, nc, es):
        self.nc = nc
        self.es = es
        self.ops = []
        self.by_eng = {e: [] for e in self.ENG}
        self.dma_bufs = []

    def op(self, eng, fn, reads=(), writes=(), dma_owner=None):
        o = Op()
        o.eng = eng
        o.fn = fn
        o.idx = len(self.ops)
        o.dma = dma_owner
        o.sig = dma_owner is not None
        deps = set()
        for b in reads:
            deps.update(b.w.values())
        for b in writes:
            deps.update(b.w.values())
            deps.update(b.r.values())
        o.deps = deps
        if dma_owner is not None:
            if dma_owner.sem is None:
                dma_owner.sem = self.es.enter_context(self.nc.semaphore("d_" + dma_owner.name))
                self.dma_bufs.append(dma_owner)
            dma_owner.dcount += 1
            o.tok = (dma_owner.sem, 16 * dma_owner.dcount)
            key = ("dma", id(dma_owner))
        else:
            o.tok = None
            key = eng
        for b in reads:
            b.r[key] = o.idx
        for b in writes:
            b.w = {key: o.idx}
            b.r = {}
        self.ops.append(o)
        self.by_eng[eng].append(o)
        return o

    def emit(self):
        nc = self.nc
        ops = self.ops
        for o in ops:
            for d in o.deps:
                p = ops[d]
                if p.dma is None and not (p.eng == "pe" and o.eng == "pe" and o.dma is None):
                    p.sig = True
        esem = {e: self.es.enter_context(nc.semaphore("c_" + e)) for e in ("pe", "act", "dve", "pool")}
        for e in ("pe", "act", "dve", "pool"):
            cnt = 0
            for o in self.by_eng[e]:
                if o.dma is None and o.sig:
                    cnt += 1
                    o.tok = (esem[e], cnt)
        block = self.es.enter_context(nc.Block())
        final = [(b.sem, 16 * b.dcount) for b in self.dma_bufs]

        def run(engname, engobj):
            waited = {}
            for o in self.by_eng[engname]:
                for d in sorted(o.deps):
                    p = ops[d]
                    if p.dma is None and p.eng == "pe" and engname == "pe" and o.dma is None:
                        continue
                    sem, val = p.tok
                    k = id(sem)
                    if waited.get(k, 0) >= val:
                        continue
                    waited[k] = val
                    engobj.wait_ge(sem, val)
                ins = o.fn(engobj)
                if o.sig:
                    ins.then_inc(o.tok[0], 16 if o.dma is not None else 1)
            if engname == "sp":
                for sem, val in final:
                    engobj.wait_ge(sem, val)

        block.tensor(lambda e: run("pe", e))
        block.scalar(lambda e: run("act", e))
        block.vector(lambda e: run("dve", e))
        block.gpsimd(lambda e: run("pool", e))
        block.sync(lambda e: run("sp", e))


def build(NP, DEPTH):
    NTOK = NP + 4 * NSEQ
    nc = bass.Bass("TRN2", target_bir_lowering=False)
    es = ExitStack()
    P = Prog(nc, es)

    def din(name, shape, dt=F32):
        return nc.dram_tensor(name, list(shape), dt, kind="ExternalInput").ap()

    def dout(name, shape):
        return nc.dram_tensor(name, list(shape), F32, kind="ExternalOutput").ap()

    xT_d = din("xT", [D, NTOK])
    peT_d = din("peT", [DEPTH, 256, NTOK])
    spT_d = din("spT", [DEPTH, D, NSEQ, 15])
    scT_d = din("scT", [DEPTH, 4096, NSEQ, 3])
    Cs_d = din("Cs", [DEPTH, NSEQ, 4, 512, 512])
    nT_d = din("nT", [DEPTH, 128, 4, 4, NSEQ])
    mT_d = din("mT", [DEPTH, 4, NSEQ])
    w_in_d = din("w_in", [DEPTH, D, NIN])
    pool_w_d = din("pool_w", [DEPTH, 4, 256, 256])
    wpd_d = din("w_pool_down", [DEPTH, D, D])
    wmd_d = din("w_mlstm_down", [DEPTH, 2048, D])
    wout_d = din("w_out", [DEPTH, D, D])
    wple_d = din("w_ple", [DEPTH, 256, D])
    wpg_d = din("w_ple_gate", [DEPTH, D, D])
    prm_d = din("prm", [DEPTH, 128, NPRM])
    fnw_d = din("fnw", [128, 8])
    bif_d = din("bif", [DEPTH, 4, 2])
    nwB_d = din("nwB", [DEPTH, 128, 2048])
    c_ident = din("c_ident", [128, 128])
    c_maskP = din("c_maskP", [128, 128])
    c_maskS = din("c_maskS", [64, 64])
    c_sel = din("c_sel", [4, 512])
    c_invc = din("c_invc", [128, 4, 16])
    c_seg = din("c_seg", [4, 128])
    c_zero = din("c_zero", [128, 544])
    c_ones = din("c_ones", [128, 128])
    c_eps = din("c_eps", [128, 1])
    c_one1 = din("c_one1", [128, 1])

    yT_d = dout("yT", [D, NTOK])
    npp_d = dout("npp", [DEPTH, D, 15])
    nps_d = dout("nps", [DEPTH, D, NSEQ, 15])
    ncp_d = dout("ncp", [DEPTH, 4096, 3])
    ncs_d = dout("ncs", [DEPTH, 4096, NSEQ, 3])
    nCp_d = dout("nCp", [DEPTH, 4, 512, 512])
    nCs_d = dout("nCs", [DEPTH, NSEQ, 4, 512, 512])
    nnp_d = dout("nnp", [DEPTH, 128, 4, 4, 1])
    nns_d = dout("nns", [DEPTH, 128, 4, 4, NSEQ])
    nmp_d = dout("nmp", [DEPTH, 4, 1])
    nms_d = dout("nms", [DEPTH, 4, NSEQ])
    xs_t = nc.dram_tensor("xs", [D, NTOK], F32, kind="Internal").ap()
    xs_buf = Buf("xs", None)

    def sb(name, shape, dt=F32):
        t = es.enter_context(nc.sbuf_tensor("s_" + name, list(shape), dt))
        return Buf(name, t)

    def ps(name, shape, dt=F32):
        t = es.enter_context(nc.psum_tensor("q_" + name, list(shape), dt))
        return Buf(name, t)

    xt = sb("xt", [128, 8, TT])
    hT = sb("hT", [128, 8, TT], BF16)
    scrA = sb("scrA", [128, 8, TT], BF16)
    gaT = sb("gaT", [128, 8, TT], BF16)
    Fb = [sb(f"F{i}", [128, 528]) for i in range(6)]
    pm = sb("pm", [128, 2, TT], BF16)
    qT = sb("qT", [128, 4, TT], BF16)
    kT = sb("kT", [128, 4, TT], BF16)
    qg = sb("qg", [128, 4, TT], BF16)
    vtok = sb("vtok", [128, 4, 512], BF16)
    sigo = sb("sigo", [128, 4, 512], BF16)
    nwsz = sb("nwsz", [128, 4, 512], BF16)
    g_ig = sb("g_ig", [4, TT])
    g_ls = sb("g_ls", [4, TT])
    g_m = sb("g_m", [4, TT])
    g_B = sb("g_B", [4, TT])
    g_R = sb("g_R", [4, TT])
    g_G = sb("g_G", [4, TT])
    g_small = sb("g_small", [4, 64])
    Atok = sb("Atok", [128, 4, 4])
    emtt = sb("emtt", [128, 4, 4])
    RBs = sb("RBs", [128, TT])
    gBe = sb("gBe", [128, TT])
    tmpD = [sb(f"tmpD{i}", [128, 128]) for i in range(2)]
    SDT = [sb(f"SDT{i}", [128, 128], BF16) for i in range(2)]
    WSBb = [sb(f"WSBb{i}", [128, 16], BF16) for i in range(2)]
    ktok = [sb(f"ktok{i}", [128, 512], BF16) for i in range(2)]
    vw = [sb(f"vw{i}", [128, 512], BF16) for i in range(2)]
    hc1 = sb("hc1", [128, 512])
    hc2 = [sb(f"hc2{i}", [128, 512], BF16) for i in range(1)]
    junk = sb("junk", [128, 512], BF16)
    szb = [sb(f"sz{i}", [128, 512], BF16) for i in range(2)]
    fix15 = sb("fix15", [128, 16])
    onesf = sb("onesf", [128, 1])
    epsf = sb("epsf", [128, 1])
    tiny = [sb(f"tiny{i}", [128, 8]) for i in range(2)]
    hcT = sb("hcT", [128, 16, TT], BF16)
    Cb = [sb(f"C{i}", [128, 4, 512]) for i in range(4)]
    Cbf = [sb(f"Cbf{i}", [128, 4, 512], BF16) for i in range(2)]
    Z = sb("Z", [128, 4, 1088], BF16)
    NSL = 5
    wslot = [sb(f"w{i}", [128, 2048], BF16) for i in range(NSL)]
    ones_bf = sb("ones_bf", [128, 128], BF16)
    maskP = sb("maskP", [128, 128])
    maskS = sb("maskS", [64, 64])
    sel = sb("sel", [4, 512])
    identb = sb("identb", [128, 128], BF16)
    identf = sb("identf", [128, 128])
    invc = sb("invc", [128, 4, 16])
    segc = sb("segc", [4, 128])
    zt = sb("zt", [128, 544])
    prm = sb("prm", [128, NPRM])
    fnw = sb("fnw", [128, 8])
    bif = sb("bif", [4, 2])
    nwh = sb("nwh", [128, 512])
    peT = sb("peT", [128, 2, TT], BF16)
    rstd = sb("rstd", [128, TT])
    phist = sb("phist", [128, 8, 15])
    chist = sb("chist", [128, 32, 3])
    nP = sb("nP", [128, 4, 4, 1])
    nS = sb("nS", [128, 4, 4, NSEQ])
    nbf = sb("nbf", [128, 4, 4, NSEQ], BF16)
    ntmp = sb("ntmp", [128, 4, NSEQ])

    NROT = 4
    prot = [ps(f"prot{i}", [128, 512]) for i in range(NROT)]
    p_st = ps("p_st", [128, 128])
    p_tr = ps("p_tr", [128, 512], BF16)
    p_num = ps("p_num", [128, 512])
    p_sm = ps("p_sm", [128, 512])
    p_sm_at = Buf("p_sm_at", p_sm.t)
    p_sm_den = Buf("p_sm_den", p_sm.t)
    p_sm_n = Buf("p_sm_n", p_sm.t)

    rot_i = [0]

    def PR():
        b = prot[rot_i[0] % NROT]
        rot_i[0] += 1
        return b

    cnt = {"F": 0, "w": 0, "d": 0, "k": 0, "v": 0, "h2": 0, "C": 0, "cbf": 0, "tiny": 0}

    def nxt(key, lst):
        b = lst[cnt[key] % len(lst)]
        cnt[key] += 1
        return b

    def MM(out_b, out_ap, lb, l_ap, rb, r_ap, start, stop):
        P.op("pe", lambda e: e.matmul(out_ap, l_ap, r_ap, start=start, stop=stop),
             reads=list(lb) + list(rb), writes=[out_b])

    def TR(out_b, out_ap, ib, in_ap, idb, id_ap):
        P.op("pe", lambda e: e.transpose(out_ap, in_ap, id_ap), reads=[ib, idb], writes=[out_b])

    def ACT(out_b, out_ap, in_bs, in_ap, func, bias=None, scale=None, accum=None, extra_w=()):
        kw = {}
        if bias is not None:
            kw["bias"] = bias
        if scale is not None:
            kw["scale"] = scale
        if accum is not None:
            kw["accum_out"] = accum
        P.op("act", lambda e: e.activation(out_ap, in_ap, func, **kw), reads=list(in_bs),
             writes=[out_b] + list(extra_w))

    def V(eng, method, args, reads, writes, kw=None):
        kw = kw or {}
        P.op(eng, lambda e: getattr(e, method)(*args, **kw), reads=reads, writes=writes)

    def DMA(eng, out_ap, in_ap, reads, writes, owner):
        P.op(eng, lambda e: e.dma_start(out=out_ap, in_=in_ap), reads=reads, writes=writes, dma_owner=owner)

    dbg_names = []

    def DBG(name, buf, ap, shape, bf=False):
        if not DEBUG or name in dbg_names:
            return
        dbg_names.append(name)
        d = nc.dram_tensor("dbg_" + name, list(shape), F32, kind="ExternalOutput").ap()
        DMA("pool" if bf else "sp", d, ap, [buf], [], buf)

    def wload(src_ap, kc, n):
        s = nxt("w", wslot)
        view = s.t[:, 0:kc * n].rearrange("p (k n) -> p k n", k=kc)
        DMA("pool", view, src_ap, [], [s], s)
        return s, view

    def win_blk(l, col0, n):
        return wload(w_in_d[l, :, col0:col0 + n].rearrange("(k p) n -> p k n", p=128), 8, n)

    def sq_blk(wd, l, j0, n, kc):
        return wload(wd[l, :, j0:j0 + n].rearrange("(k p) n -> p k n", p=128), kc, n)

    DMA("sp", identf.t[:], c_ident, [], [identf], identf)
    DMA("pool", identb.t[:], c_ident, [], [identb], identb)
    DMA("sp", maskP.t[:], c_maskP, [], [maskP], maskP)
    DMA("sp", maskS.t[:], c_maskS, [], [maskS], maskS)
    DMA("sp", sel.t[:], c_sel, [], [sel], sel)
    DMA("sp", invc.t[:], c_invc, [], [invc], invc)
    DMA("sp", segc.t[:], c_seg, [], [segc], segc)
    DMA("sp", fnw.t[:], fnw_d, [], [fnw], fnw)
    DMA("sp", zt.t[:], c_zero, [], [zt], zt)
    DMA("pool", ones_bf.t[:], c_ones, [], [ones_bf], ones_bf)
    DMA("sp", onesf.t[:], c_one1, [], [onesf], onesf)
    DMA("sp", epsf.t[:], c_eps, [], [epsf], epsf)
    V("dve", "tensor_copy", (Z.t[:, :, :].rearrange("p c (a u) -> p (c a) u", a=2), zt.t[:, :].unsqueeze(1).to_broadcast([128, 8, 544])), [zt], [Z])
    V("dve", "tensor_copy", (nbf.t[:], zt.t[:, 0:256].rearrange("p (a b c) -> p a b c", a=4, b=4)), [zt], [nbf])

    def rmsnorm_to(dst, NT, wcol0, wbuf):
        ACT(scrA, scrA.t[:, :, 0:NT], [xt], xt.t[:, :, 0:NT], AF.Square)
        pb = PR()
        for kc in range(8):
            MM(pb, pb.t[:, 0:NT], [ones_bf], ones_bf.t[:, :], [scrA], scrA.t[:, kc, 0:NT], kc == 0, kc == 7)
        ACT(rstd, rstd.t[:, 0:NT], [pb, epsf], pb.t[:, 0:NT], AF.Ln, bias=epsf.t[:, 0:1], scale=1.0 / D)
        ACT(rstd, rstd.t[:, 0:NT], [rstd], rstd.t[:, 0:NT], AF.Exp, scale=-0.5)
        for kc in range(8):
            V("dve", "scalar_tensor_tensor",
              (dst.t[:, kc, 0:NT], xt.t[:, kc, 0:NT], wbuf.t[:, wcol0 + kc:wcol0 + kc + 1], rstd.t[:, 0:NT],
               ALU.mult, ALU.mult), [xt, wbuf, rstd], [dst])

    def proj_fm(wb, wv, j, NT, src):
        pb = PR()
        kcn = wv.shape[1]
        for kc in range(kcn):
            MM(pb, pb.t[:, 0:NT], [wb], wv[:, kc, j * 128:(j + 1) * 128], [src], src.t[:, kc, 0:NT],
               kc == 0, kc == kcn - 1)
        return pb

    C_NW, C_PNW, C_PSC, C_CB, C_CW = 0, 8, 16, 24, 56

    for l in range(DEPTH):
        last_layer = l == DEPTH - 1
        DMA("sp", prm.t[:], prm_d[l], [], [prm], prm)
        DMA("sp", bif.t[:], bif_d[l], [], [bif], bif)
        V("dve", "tensor_copy", (g_small.t[:, 0:1], zt.t[0:4, 0:1]), [zt], [g_small])
        V("dve", "tensor_scalar", (g_small.t[:, 1:2], bif.t[:, 1:2], -1.0, None, ALU.mult), [bif], [g_small])
        V("dve", "tensor_copy", (nP.t[:], zt.t[:, 0:16].rearrange("p (a b c) -> p a b c", a=4, b=4)), [zt], [nP])
        DMA("sp", nS.t[:], nT_d[l], [], [nS], nS)
        DMA("sp", g_small.t[:, 8:24], mT_d[l], [], [g_small], g_small)
        V("dve", "tensor_copy", (phist.t[:], zt.t[:, 0:120].rearrange("p (a b) -> p a b", a=8)), [zt], [phist])
        V("dve", "tensor_copy", (chist.t[:], zt.t[:, 0:96].rearrange("p (a b) -> p a b", a=32)), [zt], [chist])

        ntiles = NP // TT
        for ti in range(ntiles + 1):
            samp = ti == ntiles
            NT = 4 * NSEQ if samp else TT
            c0 = ti * TT
            nseg = NSEQ if samp else 1
            Ls = 4 if samp else TT
            L = 64 if samp else 128
            nch = NT // L
            nseq = NSEQ if samp else 1
            ls_ = L // nseq
            first_tile = ti == 0
            last_prompt = ti == ntiles - 1
            mask = maskS if samp else maskP
            xsrc = xT_d if l == 0 else xs_t

            DMA("sp", xt.t[:, :, 0:NT], xsrc[:, c0:c0 + NT].rearrange("(k p) n -> p k n", p=128),
                [xs_buf], [xt], xt)
            DMA("pool", peT.t[:, :, 0:NT], peT_d[l, :, c0:c0 + NT].rearrange("(k p) n -> p k n", p=128),
                [], [peT], peT)

            rmsnorm_to(hT, NT, C_NW, prm)

            def hview(fb, H):
                return fb.t[:, 0:nseg * (H + Ls)].rearrange("p (s w) -> p s w", s=nseg)

            def p3(pb):
                return pb.t[:, 0:NT].rearrange("p (s w) -> p s w", s=nseg)

            for g in range(4):
                win = 2 << g
                wb_u, wv_u = win_blk(l, O_PIN + g * 256, 256)
                wb_z, wv_z = win_blk(l, O_PZ + g * 256, 256)
                wb_p, wv_p = wload(pool_w_d[l, g].rearrange("(k p) n -> p k n", p=128), 2, 256)
                szs = []
                for j in range(2):
                    c = 2 * g + j
                    pb = proj_fm(wb_u, wv_u, j, NT, hT)
                    U = nxt("F", Fb)
                    Uv = hview(U, 15)
                    if samp:
                        DMA("sp", Uv[:, :, 0:15], spT_d[l, c * 128:(c + 1) * 128], [], [U], U)
                    elif first_tile:
                        V("dve", "tensor_copy", (Uv[:, 0, 0:15], zt.t[:, 0:15]), [zt], [U])
                    else:
                        V("dve", "tensor_copy", (Uv[:, 0, 0:15], phist.t[:, c, :]), [phist], [U])
                    ACT(U, Uv[:, :, 15:15 + Ls], [pb], p3(pb), AF.Copy)
                    if samp:
                        DMA("sp", nps_d[l, c * 128:(c + 1) * 128], Uv[:, :, 4:19], [U], [], U)
                    else:
                        V("dve", "tensor_copy", (phist.t[:, c, :], Uv[:, 0, Ls:Ls + 15]), [U], [phist])
                    cur = U
                    sh = 1
                    lo = 0
                    for _ in range(g + 1):
                        nb = nxt("F", Fb)
                        cv = hview(cur, 15)
                        nv = hview(nb, 15)
                        lo += sh
                        V("dve", "tensor_tensor", (nv[:, :, lo:15 + Ls], cv[:, :, lo:15 + Ls],
                                                   cv[:, :, lo - sh:15 + Ls - sh], ALU.add), [cur], [nb])
                        cur = nb
                        sh *= 2
                    sv = hview(cur, 15)
                    pmv = pm.t[:, j, 0:NT].rearrange("p (s w) -> p s w", s=nseg)
                    V("dve", "scalar_tensor_tensor", (pmv, sv[:, :, 15:15 + Ls], 1.0 / win, Uv[:, :, 15:15 + Ls],
                                                      ALU.mult, ALU.subtract), [cur, U], [pm])
                    if first_tile and not samp:
                        V("dve", "tensor_tensor", (fix15.t[:, 0:15], sv[:, 0, 15:30], invc.t[:, g, 0:15], ALU.mult),
                          [cur, invc], [fix15])
                        V("dve", "tensor_tensor", (pm.t[:, j, 0:15], fix15.t[:, 0:15], Uv[:, 0, 15:30], ALU.subtract),
                          [fix15, U], [pm])
                    pz = proj_fm(wb_z, wv_z, j, NT, hT)
                    sz = szb[j]
                    ACT(sz, sz.t[:, 0:NT], [pz], pz.t[:, 0:NT], AF.Silu)
                    szs.append(sz)
                for j in range(2):
                    c = 2 * g + j
                    pb = PR()
                    for ci in range(2):
                        MM(pb, pb.t[:, 0:NT], [wb_p], wv_p[:, ci, j * 128:(j + 1) * 128], [pm], pm.t[:, ci, 0:NT],
                           ci == 0, ci == 1)
                    V("dve", "scalar_tensor_tensor",
                      (scrA.t[:, c, 0:NT], pb.t[:, 0:NT], prm.t[:, C_PSC + c:C_PSC + c + 1], szs[j].t[:, 0:NT],
                       ALU.mult, ALU.mult), [pb, prm, szs[j]], [scrA])
            for jb in range(4):
                wb_d, wv_d = sq_blk(wpd_d, l, jb * 256, 256, 8)
                wb_g, wv_g = win_blk(l, O_GA + jb * 256, 256)
                for jj in range(2):
                    j = jb * 2 + jj
                    pa = proj_fm(wb_d, wv_d, jj, NT, scrA)
                    pg = proj_fm(wb_g, wv_g, jj, NT, hT)
                    sg = nxt("F", Fb)
                    ACT(sg, sg.t[:, 0:NT], [pg], pg.t[:, 0:NT], AF.Sigmoid)
                    V("dve", "tensor_tensor", (gaT.t[:, j, 0:NT], pa.t[:, 0:NT], sg.t[:, 0:NT], ALU.mult),
                      [pa, sg], [gaT])

            wb_if, wv_if = win_blk(l, O_IF, 8)
            pi = PR()
            pf = PR()
            for kc in range(8):
                MM(pi, pi.t[0:4, 0:NT], [wb_if], wv_if[:, kc, 0:4], [hT], hT.t[:, kc, 0:NT], kc == 0, kc == 7)
            for kc in range(8):
                MM(pf, pf.t[0:4, 0:NT], [wb_if], wv_if[:, kc, 4:8], [hT], hT.t[:, kc, 0:NT], kc == 0, kc == 7)
            ACT(g_ig, g_ig.t[:, 0:NT], [pi, bif], pi.t[0:4, 0:NT], AF.Identity, bias=bif.t[:, 0:1])
            ACT(g_ls, g_ls.t[:, 0:NT], [pf, g_small], pf.t[0:4, 0:NT], AF.Exp, bias=g_small.t[:, 1:2], scale=-1.0)
            ACT(g_ls, g_ls.t[:, 0:NT], [g_ls, onesf], g_ls.t[:, 0:NT], AF.Ln, bias=onesf.t[0:4, 0:1])
            V("dve", "tensor_scalar", (g_ls.t[:, 0:NT], g_ls.t[:, 0:NT], -1.0, None, ALU.mult), [g_ls], [g_ls])
            V("dve", "tensor_tensor_scan", (g_B.t[:, 0:TT], g_ls.t[:, 0:TT], zt.t[0:4, 0:TT], 0.0, ALU.add, ALU.add),
              [g_ls, zt], [g_B])
            if samp:
                ls3 = g_ls.t[:, 0:NT].rearrange("p (b t) -> p b t", t=4)
                ig3 = g_ig.t[:, 0:NT].rearrange("p (b t) -> p b t", t=4)
                R3 = g_R.t[:, 0:NT].rearrange("p (b t) -> p b t", t=4)
                G3 = g_G.t[:, 0:NT].rearrange("p (b t) -> p b t", t=4)
                V("dve", "tensor_copy", (g_R.t[:, 0:NT], g_ig.t[:, 0:NT]), [g_ig], [g_R])
                V("dve", "tensor_tensor", (g_small.t[:, 24:40], g_small.t[:, 8:24], ls3[:, :, 0], ALU.add),
                  [g_small, g_ls], [g_small])
                V("dve", "tensor_tensor", (R3[:, :, 0], g_small.t[:, 24:40], ig3[:, :, 0], ALU.max),
                  [g_small, g_ig], [g_R])
                V("dve", "tensor_tensor", (g_G.t[:, 0:NT], g_ls.t[:, 0:NT], segc.t[:, 0:64], ALU.mult), [g_ls, segc], [g_G])
                V("dve", "tensor_tensor", (g_G.t[:, 0:NT], g_G.t[:, 0:NT], segc.t[:, 64:128], ALU.add), [g_G, segc], [g_G])
                V("dve", "tensor_tensor_scan", (g_m.t[:, 0:NT], g_G.t[:, 0:NT], g_R.t[:, 0:NT], 0.0, ALU.add, ALU.max),
                  [g_G, g_R], [g_m])
            else:
                V("dve", "tensor_tensor_scan", (g_m.t[:, 0:NT], g_ls.t[:, 0:NT], g_ig.t[:, 0:NT], g_small.t[:, 0:1],
                                                ALU.add, ALU.max), [g_ls, g_ig, g_small], [g_m])
            V("dve", "tensor_tensor", (g_R.t[:, 0:NT], g_B.t[:, 0:NT], g_m.t[:, 0:NT], ALU.subtract), [g_B, g_m], [g_R])
            V("dve", "tensor_tensor", (g_ig.t[:, 0:NT], g_ig.t[:, 0:NT], g_B.t[:, 0:NT], ALU.subtract), [g_ig, g_B], [g_ig])
            if samp:
                B3 = g_B.t[:, 0:NT].rearrange("p (b t) -> p b t", t=4)
                R3 = g_R.t[:, 0:NT].rearrange("p (b t) -> p b t", t=4)
                G3 = g_G.t[:, 0:NT].rearrange("p (b t) -> p b t", t=4)
                V("dve", "tensor_copy", (g_small.t[:, 24:25], g_small.t[:, 8:9]), [g_small], [g_small])
                V("dve", "tensor_tensor", (g_small.t[:, 25:40], g_small.t[:, 9:24], B3[:, 0:15, 3], ALU.subtract),
                  [g_small, g_B], [g_small])
                V("dve", "tensor_tensor", (G3, R3, g_small.t[:, 24:40].unsqueeze(2).to_broadcast([4, NSEQ, 4]), ALU.add),
                  [g_R, g_small], [g_G])
                m3 = g_m.t[:, 0:NT].rearrange("p (b t) -> p b t", t=4)
                V("dve", "tensor_copy", (g_small.t[:, 40:56], m3[:, :, 3]), [g_m], [g_small])
                DMA("sp", nms_d[l], g_small.t[:, 40:56], [g_small], [], g_small)
            else:
                for c in range(nch):
                    cs = slice(c * L, (c + 1) * L)
                    if c == 0:
                        V("dve", "tensor_scalar", (g_G.t[:, cs], g_R.t[:, cs], g_small.t[:, 0:1], None, ALU.add),
                          [g_R, g_small], [g_G])
                    else:
                        V("dve", "tensor_scalar", (g_G.t[:, cs], g_R.t[:, cs], g_R.t[:, c * L - 1:c * L], None, ALU.subtract),
                          [g_R], [g_G])
                V("dve", "tensor_copy", (g_small.t[:, 0:1], g_m.t[:, NT - 1:NT]), [g_m], [g_small])
                if last_prompt:
                    DMA("sp", nmp_d[l], g_small.t[:, 0:1], [g_small], [], g_small)
            ACT(g_m, g_m.t[:, 0:NT], [g_m], g_m.t[:, 0:NT], AF.Exp, scale=-1.0)
            for c in range(nch):
                cs = slice(c * L, (c + 1) * L)
                TR(p_sm_at, p_sm.t[0:L, c * 4:c * 4 + 4], g_ig, g_ig.t[0:4, cs], identf, identf.t[0:4, 0:4])
                TR(p_sm_at, p_sm.t[0:L, 16 + c * 4:16 + c * 4 + 4], g_m, g_m.t[0:4, cs], identf, identf.t[0:4, 0:4])
            V("dve", "tensor_scalar", (Atok.t[0:L, 0:nch, :], p_sm.t[0:L, 0:nch * 4].rearrange("p (c h) -> p c h", h=4),
                                       LNSCALE, None, ALU.add), [p_sm_at], [Atok])
            V("dve", "tensor_copy", (emtt.t[0:L, 0:nch, :], p_sm.t[0:L, 16:16 + nch * 4].rearrange("p (c h) -> p c h", h=4)),
              [p_sm_at], [emtt])

            if samp:
                V("dve", "tensor_copy", (nbf.t[:], nS.t[:]), [nS], [nbf])
            for h in range(4):
                for qk in range(2):
                    dstb = qT if qk == 0 else kT
                    for half in range(2):
                        wb, wv = win_blk(l, O_QK + qk * 2048 + h * 512 + half * 256, 256)
                        for jj in range(2):
                            dc = half * 2 + jj
                            cch = qk * 16 + h * 4 + dc
                            pb = proj_fm(wb, wv, jj, NT, hT)
                            cb = nxt("F", Fb)
                            cv = hview(cb, 3)
                            if samp:
                                DMA("sp", cv[:, :, 0:3], scT_d[l, cch * 128:(cch + 1) * 128], [], [cb], cb)
                            elif first_tile:
                                V("dve", "tensor_copy", (cv[:, 0, 0:3], zt.t[:, 0:3]), [zt], [cb])
                            else:
                                V("dve", "tensor_copy", (cv[:, 0, 0:3], chist.t[:, cch, :]), [chist], [cb])
                            ACT(cb, cv[:, :, 3:3 + Ls], [pb], p3(pb), AF.Copy)
                            if samp:
                                DMA("sp", ncs_d[l, cch * 128:(cch + 1) * 128], cv[:, :, 4:7], [cb], [], cb)
                            else:
                                V("dve", "tensor_copy", (chist.t[:, cch, :], cv[:, 0, Ls:Ls + 3]), [cb], [chist])
                            acc = nxt("F", Fb)
                            av = acc.t[:, 0:NT].rearrange("p (s w) -> p s w", s=nseg)
                            wc = C_CW + cch * 4
                            V("dve", "tensor_scalar", (av, cv[:, :, 0:Ls], prm.t[:, wc:wc + 1],
                                                       prm.t[:, C_CB + cch:C_CB + cch + 1], ALU.mult, ALU.add),
                              [cb, prm], [acc])
                            for tap in range(1, 4):
                                V("dve", "scalar_tensor_tensor", (av, cv[:, :, tap:tap + Ls], prm.t[:, wc + tap:wc + tap + 1],
                                                                  av, ALU.mult, ALU.add), [cb, prm, acc], [acc])
                            ACT(dstb, dstb.t[:, dc, 0:NT], [acc], acc.t[:, 0:NT], AF.Silu)
                DMA("sp", nwh.t[:], nwB_d[l, :, h * 512:(h + 1) * 512], [], [nwh], nwh)
                for which, off in ((0, O_V), (1, O_O), (2, O_MZ)):
                    blks = [win_blk(l, off + h * 512 + half * 256, 256) for half in range(2)]
                    for c in range(nch):
                        pb = PR()
                        for half in range(2):
                            wb, wv = blks[half]
                            for kc in range(8):
                                MM(pb, pb.t[0:L, half * 256:(half + 1) * 256], [hT], hT.t[:, kc, c * L:(c + 1) * L],
                                   [wb], wv[:, kc, :], kc == 0, kc == 7)
                        if which == 0:
                            ACT(vtok, vtok.t[0:L, c, :], [pb], pb.t[0:L, :], AF.Copy)
                        elif which == 1:
                            ACT(sigo, sigo.t[0:L, c, :], [pb], pb.t[0:L, :], AF.Sigmoid)
                        else:
                            ztmp = nxt("F", Fb)
                            ACT(ztmp, ztmp.t[0:L, 0:512], [pb], pb.t[0:L, :], AF.Silu)
                            V("dve", "tensor_tensor", (nwsz.t[0:L, c, :], ztmp.t[0:L, 0:512], nwh.t[0:L, :], ALU.mult),
                              [ztmp, nwh], [nwsz])
                prb = PR()
                MM(prb, prb.t[:, 0:NT], [sel], sel.t[:, h * 128:(h + 1) * 128], [g_R], g_R.t[:, 0:NT], True, True)
                pgb = PR()
                MM(pgb, pgb.t[:, 0:NT], [sel], sel.t[:, h * 128:(h + 1) * 128], [g_G], g_G.t[:, 0:NT], True, True)
                ACT(RBs, RBs.t[:, 0:NT], [prb], prb.t[:, 0:NT], AF.Copy)
                ACT(gBe, gBe.t[:, 0:NT], [pgb], pgb.t[:, 0:NT], AF.Exp)
                V("dve", "tensor_tensor", (qg.t[:, :, 0:NT], qT.t[:, :, 0:NT],
                                           gBe.t[:, 0:NT].unsqueeze(1).to_broadcast([128, 4, NT]), ALU.mult),
                  [qT, gBe], [qg])
                if samp:
                    zd = Z.t[:, :, :].rearrange("p c (b u) -> p c b u", u=68)[:, :, :, 0:4]
                    V("dve", "tensor_copy", (zd, qg.t[:, :, 0:NT].rearrange("p c (b t) -> p c b t", t=4)), [qg], [Z])
                    nst = nS
                else:
                    nst = nP

                for c in range(nch):
                    cs = slice(c * L, (c + 1) * L)
                    state_zero = first_tile and c == 0 and not samp
                    td = nxt("d", tmpD)
                    dt_ = td
                    sdt = SDT[(cnt["d"] - 1) % 2]
                    wsb = WSBb[(cnt["d"] - 1) % 2]
                    V("dve", "tensor_tensor", (td.t[0:L, 0:L], RBs.t[0:L, cs], mask.t[0:L, 0:L], ALU.add), [RBs, mask], [td])
                    ACT(dt_, dt_.t[0:L, 0:L], [td, Atok], td.t[0:L, 0:L], AF.Exp, bias=Atok.t[0:L, c, h:h + 1])
                    lastv = dt_.t[0:L, 0:L].rearrange("p (b t) -> p b t", t=ls_)[:, :, ls_ - 1]
                    V("dve", "tensor_copy", (wsb.t[0:L, 0:nseq], lastv), [dt_], [wsb])
                    for dc in range(4):
                        MM(p_st, p_st.t[0:L, 0:L], [kT], kT.t[:, dc, cs], [qT], qT.t[:, dc, cs], dc == 0, dc == 3)
                    V("dve", "tensor_tensor", (sdt.t[0:L, 0:L], p_st.t[0:L, 0:L], dt_.t[0:L, 0:L], ALU.mult), [p_st, dt_], [sdt])
                    for dc in range(4):
                        TR(p_tr, p_tr.t[0:L, dc * 128:(dc + 1) * 128], kT, kT.t[:, dc, cs], identb, identb.t[:, :])
                    kt = nxt("k", ktok)
                    ACT(kt, kt.t[0:L, :], [p_tr], p_tr.t[0:L, :], AF.Copy)
                    den_ap = p_sm.t[0:L, 32:33]
                    MM(p_num, p_num.t[0:L, :], [sdt], sdt.t[0:L, 0:L], [vtok], vtok.t[0:L, c, :], True, state_zero)
                    MM(p_sm_den, den_ap, [sdt], sdt.t[0:L, 0:L], [ones_bf], ones_bf.t[0:L, 0:1], True, state_zero)
                    decay = gBe.t[:, c * L + L - 1:c * L + L]
                    if not samp:
                        Ch = Cb[h]
                        if not state_zero:
                            if c == 0:
                                cbf = Cbf[0]
                                ACT(cbf, cbf.t[:], [Ch], Ch.t[:], AF.Copy)
                                cur_cbf[0] = 0
                            cbf = Cbf[cur_cbf[0]]
                            for dc in range(4):
                                MM(p_num, p_num.t[0:L, :], [qg], qg.t[:, dc, cs], [cbf], cbf.t[:, dc, :], False, dc == 3)
                            for dc in range(4):
                                MM(p_sm_den, den_ap, [qg], qg.t[:, dc, cs], [nbf], nbf.t[:, h, dc, 0:1], False, dc == 3)
                            nxi = 1 - cur_cbf[0]
                        else:
                            nxi = 0
                        vwb = nxt("v", vw)
                        V("dve", "tensor_scalar", (vwb.t[0:L, :], vtok.t[0:L, c, :], lastv, None, ALU.mult),
                          [vtok, dt_], [vwb])
                        newcbf = Cbf[nxi]
                        for dc in range(4):
                            pu = PR()
                            MM(pu, pu.t[:, :], [kt], kt.t[0:L, dc * 128:(dc + 1) * 128], [vwb], vwb.t[0:L, :], True, True)
                            if state_zero:
                                V("dve", "tensor_copy", (Ch.t[:, dc, :], pu.t[:, :]), [pu], [Ch])
                            else:
                                V("dve", "scalar_tensor_tensor", (Ch.t[:, dc, :], Ch.t[:, dc, :], decay, pu.t[:, :],
                                                                  ALU.mult, ALU.add), [Ch, gBe, pu], [Ch])
                            ACT(newcbf, newcbf.t[:, dc, :], [Ch], Ch.t[:, dc, :], AF.Copy)
                        cur_cbf[0] = nxi
                    else:
                        def cload(b):
                            cbuf_ = Cb[b % 4]
                            DMA("sp", cbuf_.t[:], Cs_d[l, b, h].rearrange("(k p) e -> p k e", p=128), [], [cbuf_], cbuf_)
                        cload(0)
                        cload(1)
                        for b in range(NSEQ):
                            if b + 2 < NSEQ:
                                cload(b + 2)
                            Cq = Cb[b % 4]
                            cbf = nxt("cbf", Cbf)
                            ACT(cbf, cbf.t[:], [Cq], Cq.t[:], AF.Copy)
                            for dc in range(4):
                                MM(p_num, p_num.t[0:L, :], [Z], Z.t[:, dc, b * 64:(b + 1) * 64], [cbf], cbf.t[:, dc, :],
                                   False, b == NSEQ - 1 and dc == 3)
                            for dc in range(4):
                                MM(p_sm_den, den_ap, [Z], Z.t[:, dc, b * 64:(b + 1) * 64], [nbf], nbf.t[:, h, dc, b:b + 1],
                                   False, b == NSEQ - 1 and dc == 3)
                            vwb = nxt("v", vw)
                            lv_b = dt_.t[0:L, 4 * b + 3:4 * b + 4]
                            V("dve", "tensor_scalar", (vwb.t[0:L, :], vtok.t[0:L, c, :], lv_b, None, ALU.mult),
                              [vtok, dt_], [vwb])
                            dec_b = gBe.t[:, 4 * b + 3:4 * b + 4]
                            for dc in range(4):
                                pu = PR()
                                MM(pu, pu.t[:, :], [kt], kt.t[0:L, dc * 128:(dc + 1) * 128], [vwb], vwb.t[0:L, :], True, True)
                                V("dve", "scalar_tensor_tensor", (Cq.t[:, dc, :], Cq.t[:, dc, :], dec_b, pu.t[:, :],
                                                                  ALU.mult, ALU.add), [Cq, gBe, pu], [Cq])
                            DMA("sp", nCs_d[l, b, h].rearrange("(k p) e -> p k e", p=128), Cq.t[:], [Cq], [], Cq)
                    pn = p_sm.t[:, 64:64 + 4 * nseq].rearrange("p (k b) -> p k b", k=4)
                    for dc in range(4):
                        MM(p_sm_n, pn[:, dc, :], [kt], kt.t[0:L, dc * 128:(dc + 1) * 128], [wsb], wsb.t[0:L, 0:nseq], True, True)
                    decv = gBe.t[:, cs].rearrange("p (b t) -> p b t", t=ls_)[:, :, ls_ - 1]
                    ty = nxt("tiny", tiny)
                    V("dve", "tensor_copy", (ty.t[0:L, 0:1], den_ap), [p_sm_den], [ty])
                    V("dve", "scalar_tensor_tensor", (ty.t[0:L, 1:2], ty.t[0:L, 0:1], -1.0, ty.t[0:L, 0:1], ALU.mult, ALU.max),
                      [ty], [ty])
                    V("dve", "tensor_tensor", (ty.t[0:L, 2:3], ty.t[0:L, 1:2], emtt.t[0:L, c, h:h + 1], ALU.max), [ty, emtt], [ty])
                    V("dve", "reciprocal", (ty.t[0:L, 3:4], ty.t[0:L, 2:3]), [ty], [ty])
                    V("dve", "scalar_tensor_tensor", (hc1.t[0:L, :], p_num.t[0:L, :], ty.t[0:L, 3:4], sigo.t[0:L, c, :],
                                                      ALU.mult, ALU.mult), [p_num, ty, sigo], [hc1])
                    V("dve", "tensor_tensor", (ntmp.t[:, :, 0:nseq], nst.t[:, h, :, :],
                                               decv.unsqueeze(1).to_broadcast([128, 4, nseq]), ALU.mult), [nst, gBe], [ntmp])
                    V("dve", "tensor_tensor", (nst.t[:, h, :, :], ntmp.t[:, :, 0:nseq], pn, ALU.add), [ntmp, p_sm_n], [nst])
                    V("dve", "tensor_copy", (nbf.t[:, h, :, 0:nseq], nst.t[:, h, :, :]), [nst], [nbf])
                    V("dve", "tensor_copy", (ty.t[0:L, 4:5], zt.t[0:L, 0:1]), [zt], [ty])
                    ACT(junk, junk.t[0:L, :], [hc1], hc1.t[0:L, :], AF.Square, accum=ty.t[0:L, 4:5], extra_w=[ty])
                    ACT(ty, ty.t[0:L, 5:6], [ty, epsf], ty.t[0:L, 4:5], AF.Ln, bias=epsf.t[0:L, 0:1], scale=1.0 / 512)
                    ACT(ty, ty.t[0:L, 5:6], [ty], ty.t[0:L, 5:6], AF.Exp, scale=-0.5)
                    h2 = nxt("h2", hc2)
                    V("dve", "scalar_tensor_tensor", (h2.t[0:L, :], hc1.t[0:L, :], ty.t[0:L, 5:6], nwsz.t[0:L, c, :],
                                                      ALU.mult, ALU.mult), [hc1, ty, nwsz], [h2])
                    if l == 0 and ti == DBG_TILE and h == 0 and c == 0:
                        DBG("A", g_ig, g_ig.t[:, 0:NT], [4, NT]); DBG("emt", g_m, g_m.t[:, 0:NT], [4, NT])
                        DBG("R", g_R, g_R.t[:, 0:NT], [4, NT]); DBG("G", g_G, g_G.t[:, 0:NT], [4, NT])
                        DBG("ls", g_ls, g_ls.t[:, 0:NT], [4, NT]); DBG("B", g_B, g_B.t[:, 0:NT], [4, NT])
                        DBG("Atok", Atok, Atok.t[:], [128, 4, 4]); DBG("emtt", emtt, emtt.t[:], [128, 4, 4])
                        DBG("RBs", RBs, RBs.t[:], [128, TT]); DBG("gBe", gBe, gBe.t[:], [128, TT])
                        DBG("DT", td, td.t[:], [128, 128]); DBG("SDT", sdt, sdt.t[:], [128, 128], True)
                        DBG("ktok", kt, kt.t[:], [128, 512], True); DBG("ty", ty, ty.t[:], [128, 8])
                        DBG("hc1", hc1, hc1.t[:], [128, 512]); DBG("h2", h2, h2.t[:], [128, 512], True)
                        DBG("vtok", vtok, vtok.t[:], [128, 4, 512], True); DBG("sigo", sigo, sigo.t[:], [128, 4, 512], True)
                        DBG("nwsz", nwsz, nwsz.t[:], [128, 4, 512], True)
                        DBG("qT", qT, qT.t[:], [128, 4, TT], True); DBG("kT", kT, kT.t[:], [128, 4, TT], True)
                        DBG("hT", hT, hT.t[:], [128, 8, TT], True)
                        DBG("zt", zt, zt.t[:], [128, 544])
                    p_tr3 = p_tr.t[:, :].rearrange("p (e t) -> p e t", e=4)
                    for ec in range(4):
                        TR(p_tr, p_tr3[:, ec, 0:L], h2, h2.t[0:L, ec * 128:(ec + 1) * 128], identb, identb.t[0:L, 0:L])
                    ACT(hcT, hcT.t[:, h * 4:(h + 1) * 4, cs], [p_tr], p_tr3[:, :, 0:L], AF.Copy)
                if last_prompt:
                    DMA("sp", nCp_d[l, h].rearrange("(k p) e -> p k e", p=128), Cb[h].t[:], [Cb[h]], [], Cb[h])
            if last_prompt:
                DMA("sp", nnp_d[l], nP.t[:], [nP], [], nP)
                DMA("sp", npp_d[l].rearrange("(c p) t -> p c t", p=128), phist.t[:], [phist], [], phist)
                DMA("sp", ncp_d[l].rearrange("(c p) t -> p c t", p=128), chist.t[:], [chist], [], chist)
            if samp:
                DMA("sp", nns_d[l], nS.t[:], [nS], [], nS)

            for jb in range(4):
                wb_d, wv_d = sq_blk(wmd_d, l, jb * 256, 128, 16)
                wb_d2, wv_d2 = sq_blk(wmd_d, l, jb * 256 + 128, 128, 16)
                wb_g, wv_g = win_blk(l, O_GB + jb * 256, 256)
                for jj in range(2):
                    j = jb * 2 + jj
                    wbx, wvx = (wb_d, wv_d) if jj == 0 else (wb_d2, wv_d2)
                    pbm = proj_fm(wbx, wvx, 0, NT, hcT)
                    pg = proj_fm(wb_g, wv_g, jj, NT, hT)
                    sg = nxt("F", Fb)
                    ACT(sg, sg.t[:, 0:NT], [pg], pg.t[:, 0:NT], AF.Sigmoid)
                    tt = nxt("F", Fb)
                    V("dve", "tensor_tensor", (tt.t[:, 0:NT], pbm.t[:, 0:NT], sg.t[:, 0:NT], ALU.mult), [pbm, sg], [tt])
                    V("dve", "tensor_tensor", (scrA.t[:, j, 0:NT], tt.t[:, 0:NT], gaT.t[:, j, 0:NT], ALU.add), [tt, gaT], [scrA])
            for jb in range(4):
                wb, wv = sq_blk(wout_d, l, jb * 256, 256, 8)
                for jj in range(2):
                    j = jb * 2 + jj
                    pb = proj_fm(wb, wv, jj, NT, scrA)
                    V("dve", "tensor_tensor", (xt.t[:, j, 0:NT], xt.t[:, j, 0:NT], pb.t[:, 0:NT], ALU.add), [xt, pb], [xt])
            rmsnorm_to(hT, NT, C_PNW, prm)
            for jb in range(4):
                wb, wv = sq_blk(wpg_d, l, jb * 256, 256, 8)
                wbp, wvp = sq_blk(wple_d, l, jb * 256, 256, 2)
                for jj in range(2):
                    j = jb * 2 + jj
                    pg = proj_fm(wb, wv, jj, NT, hT)
                    pp = proj_fm(wbp, wvp, jj, NT, peT)
                    sg = nxt("F", Fb)
                    ACT(sg, sg.t[:, 0:NT], [pg], pg.t[:, 0:NT], AF.Sigmoid)
                    tt = nxt("F", Fb)
                    V("dve", "tensor_tensor", (tt.t[:, 0:NT], pp.t[:, 0:NT], sg.t[:, 0:NT], ALU.mult), [pp, sg], [tt])
                    V("dve", "tensor_tensor", (xt.t[:, j, 0:NT], xt.t[:, j, 0:NT], tt.t[:, 0:NT], ALU.add), [xt, tt], [xt])
            if last_layer:
                ACT(scrA, scrA.t[:, :, 0:NT], [xt], xt.t[:, :, 0:NT], AF.Square)
                pb = PR()
                for kc in range(8):
                    MM(pb, pb.t[:, 0:NT], [ones_bf], ones_bf.t[:, :], [scrA], scrA.t[:, kc, 0:NT], kc == 0, kc == 7)
                ACT(rstd, rstd.t[:, 0:NT], [pb, epsf], pb.t[:, 0:NT], AF.Ln, bias=epsf.t[:, 0:1], scale=1.0 / D)
                ACT(rstd, rstd.t[:, 0:NT], [rstd], rstd.t[:, 0:NT], AF.Exp, scale=-0.5)
                for kc in range(8):
                    V("dve", "scalar_tensor_tensor",
                      (xt.t[:, kc, 0:NT], xt.t[:, kc, 0:NT], fnw.t[:, kc:kc + 1], rstd.t[:, 0:NT], ALU.mult, ALU.mult),
                      [xt, fnw, rstd], [xt])
                DMA("sp", yT_d[:, c0:c0 + NT].rearrange("(k p) n -> p k n", p=128), xt.t[:, :, 0:NT], [xt], [], xt)
            else:
                DMA("sp", xs_t[:, c0:c0 + NT].rearrange("(k p) n -> p k n", p=128), xt.t[:, :, 0:NT], [xt], [xs_buf], xt)

    P.emit()
    es.close()
    return nc


cur_cbf = [None]


def make_consts():
    ident = np.eye(128, dtype=np.float32)
    t = np.arange(128)
    maskP = np.where(t[:, None] <= t[None, :], 0.0, NEG).astype(np.float32)
    s64 = np.arange(64)
    same = (s64[:, None] // 4) == (s64[None, :] // 4)
    maskS = np.where(same & (s64[:, None] <= s64[None, :]), 0.0, NEG).astype(np.float32)
    sel = np.zeros((4, 512), np.float32)
    for h in range(4):
        sel[h, h * 128:(h + 1) * 128] = 1.0
    invc = np.zeros((128, 4, 16), np.float32)
    for g in range(4):
        win = 2 << g
        invc[:, g, :] = 1.0 / np.minimum(np.arange(16) + 1, win)
    seg = np.ones((4, 128), np.float32)
    seg[:, 0:64:4] = 0.0
    seg[:, 64:] = 0.0
    seg[:, 64::4] = -1e30
    return dict(c_ident=ident, c_maskP=maskP, c_maskS=maskS, c_sel=sel, c_invc=invc, c_seg=seg,
                c_zero=np.zeros((128, 544), np.float32), c_ones=np.ones((128, 128), np.float32),
                c_eps=np.full((128, 1), EPS, np.float32), c_one1=np.ones((128, 1), np.float32))


def per_partition(v):
    sh = v.shape
    n = sh[-1] // 128
    r = v.reshape(sh[:-1] + (n, 128))
    return np.ascontiguousarray(np.moveaxis(r, -1, 0))


def prepare_inputs(inp, NP, DEPTH, cores):
    consts = make_consts()
    f = lambda a: np.ascontiguousarray(np.asarray(a, dtype=np.float32))
    prm = np.zeros((DEPTH, 128, NPRM), np.float32)
    for l in range(DEPTH):
        prm[l, :, 0:8] = per_partition(f(inp["norm_w"][l]))
        prm[l, :, 8:16] = per_partition(f(inp["ple_norm_w"][l]))
        prm[l, :, 16:24] = per_partition(f(inp["pool_scale"][l]))
        prm[l, :, 24:56] = per_partition(f(inp["conv_b"][l]))
        cw = per_partition(f(inp["conv_w"][l]))
        prm[l, :, 56:184] = np.transpose(cw, (0, 2, 1)).reshape(128, 128)
    fnw = per_partition(f(inp["final_norm_w"]))
    bif = np.ascontiguousarray(np.transpose(f(inp["b_if"])[:DEPTH].reshape(DEPTH, 2, 4), (0, 2, 1)))
    nwB = np.ascontiguousarray(np.broadcast_to(f(inp["mlstm_norm_w"])[:DEPTH, None, :], (DEPTH, 128, 2048)))
    shared = dict(
        w_in=f(inp["w_in"][:DEPTH]), pool_w=f(inp["pool_w"][:DEPTH]), w_pool_down=f(inp["w_pool_down"][:DEPTH]),
        w_mlstm_down=f(inp["w_mlstm_down"][:DEPTH]), w_out=f(inp["w_out"][:DEPTH]), w_ple=f(inp["w_ple"][:DEPTH]),
        w_ple_gate=f(inp["w_ple_gate"][:DEPTH]), prm=prm, fnw=fnw, bif=bif, nwB=nwB, **consts)
    maps = []
    for c in cores:
        bs = slice(NSEQ * c, NSEQ * (c + 1))
        xp = f(inp["x_prompt"][c, :NP])
        xs = f(inp["x_sample"][bs]).reshape(4 * NSEQ, D)
        xT = np.ascontiguousarray(np.concatenate([xp, xs], 0).T)
        pp = f(inp["p_prompt"][:DEPTH, c, :NP])
        psm = f(inp["p_sample"][:DEPTH, bs]).reshape(DEPTH, 4 * NSEQ, 256)
        peT = np.ascontiguousarray(np.transpose(np.concatenate([pp, psm], 1), (0, 2, 1)))
        spT = np.ascontiguousarray(np.transpose(f(inp["state_pool"][:DEPTH, bs]), (0, 3, 1, 2)))
        scT = np.ascontiguousarray(np.transpose(f(inp["state_conv"][:DEPTH, bs]), (0, 3, 1, 2)))
        Cs = f(inp["state_mlstm_C"][:DEPTH, bs])
        n = f(inp["state_mlstm_n"][:DEPTH, bs])
        nT = np.ascontiguousarray(np.transpose(n.reshape(DEPTH, NSEQ, 4, 4, 128), (0, 4, 2, 3, 1)))
        mT = np.ascontiguousarray(np.transpose(f(inp["state_mlstm_m"][:DEPTH, bs]), (0, 2, 1)))
        m = dict(xT=xT, peT=peT, spT=spT, scT=scT, Cs=Cs, nT=nT, mT=mT)
        m.update(shared)
        maps.append(m)
    return maps


def assemble(results, NP, DEPTH, ncores):
    B = ncores
    y_p = np.zeros((B, NP, D), np.float32)
    y_s = np.zeros((B * NSEQ, 4, D), np.float32)
    pool_p = np.zeros((DEPTH, B, 15, D), np.float32)
    pool_s = np.zeros((DEPTH, B * NSEQ, 15, D), np.float32)
    conv_p = np.zeros((DEPTH, B, 3, 4096), np.float32)
    conv_s = np.zeros((DEPTH, B * NSEQ, 3, 4096), np.float32)
    C_p = np.zeros((DEPTH, B, 4, 512, 512), np.float32)
    C_s = np.zeros((DEPTH, B * NSEQ, 4, 512, 512), np.float32)
    n_p = np.zeros((DEPTH, B, 4, 512), np.float32)
    n_s = np.zeros((DEPTH, B * NSEQ, 4, 512), np.float32)
    m_p = np.zeros((DEPTH, B, 4), np.float32)
    m_s = np.zeros((DEPTH, B * NSEQ, 4), np.float32)
    for c, r in enumerate(results):
        bs = slice(NSEQ * c, NSEQ * (c + 1))
        yT = r["yT"]
        y_p[c] = yT[:, :NP].T
        y_s[bs] = yT[:, NP:].T.reshape(NSEQ, 4, D)
        pool_p[:, c] = np.transpose(r["npp"], (0, 2, 1))
        pool_s[:, bs] = np.transpose(r["nps"], (0, 2, 3, 1))
        conv_p[:, c] = np.transpose(r["ncp"], (0, 2, 1))
        conv_s[:, bs] = np.transpose(r["ncs"], (0, 2, 3, 1))
        C_p[:, c] = r["nCp"]
        C_s[:, bs] = r["nCs"]
        n_p[:, c] = np.transpose(r["nnp"], (0, 4, 2, 3, 1)).reshape(DEPTH, 4, 512)
        n_s[:, bs] = np.transpose(r["nns"], (0, 4, 2, 3, 1)).reshape(DEPTH, NSEQ, 4, 512)
        m_p[:, c] = r["nmp"][:, :, 0]
        m_s[:, bs] = np.transpose(r["nms"], (0, 2, 1))
    return (y_p, y_s, pool_p, pool_s, conv_p, conv_s, C_p, C_s, n_p, n_s, m_p, m_s)


def run(inp, NP, DEPTH, cores, trace=False):
    nc = build(NP, DEPTH)
    maps = prepare_inputs(inp, NP, DEPTH, cores)
    res = run_bass_kernel_spmd(nc, maps, core_ids=list(range(len(cores))), trace=trace)
    outs = assemble(res.results, NP, DEPTH, len(cores))
    return outs, res


def kernel(**inputs):
    outs, _ = run(inputs, 2048, 4, list(range(8)))
    return outs
```

```python
import math
from contextlib import ExitStack

import numpy as np
import ml_dtypes

import concourse.bass as bass
import concourse.mybir as mybir
from concourse.bass_utils import run_bass_kernel_spmd

F32 = mybir.dt.float32
BF16 = mybir.dt.bfloat16
AF = mybir.ActivationFunctionType
ALU = mybir.AluOpType
AX = mybir.AxisListType

D = 1024
NIN = 14344
O_PIN, O_PZ, O_QK, O_V, O_O, O_MZ, O_IF, O_GA, O_GB = 0, 1024, 2048, 6144, 8192, 10240, 12288, 12296, 13320
EPS = 1e-6
LNSCALE = math.log(512 ** -0.5)
NEG = -30000.0
NSEQ = 16
TT = 512
NPRM = 8 + 8 + 8 + 32 + 128
DEBUG = False
BIGN = 512
DBG_TILE = 0


class Buf:
    __slots__ = ("name", "t", "w", "r", "sem", "dcount")

    def __init__(self, name, t):
        self.name = name
        self.t = t
        self.w = {}
        self.r = {}
        self.sem = None
        self.dcount = 0


class Op:
    __slots__ = ("eng", "fn", "deps", "dma", "tok", "sig", "idx", "big")


class Prog:
    ENG = ("pe", "act", "dve", "pool", "sp")

    def __init__(self, nc, es):
        self.nc = nc
        self.es = es
        self.ops = []
        self.by_eng = {e: [] for e in self.ENG}
        self.dma_bufs = []

    def op(self, eng, fn, reads=(), writes=(), dma_owner=None, big=False):
        o = Op()
        o.big = big
        o.eng = eng
        o.fn = fn
        o.idx = len(self.ops)
        o.dma = dma_owner
        o.sig = dma_owner is not None
        deps = set()
        for b in reads:
            deps.update(b.w.values())
        for b in writes:
            deps.update(b.w.values())
            deps.update(b.r.values())
        o.deps = deps
        if dma_owner is not None:
            if dma_owner.sem is None:
                dma_owner.sem = self.es.enter_context(self.nc.semaphore("d_" + dma_owner.name))
                self.dma_bufs.append(dma_owner)
            dma_owner.dcount += 1
            o.tok = (dma_owner.sem, 16 * dma_owner.dcount)
            key = ("dma", id(dma_owner))
        else:
            o.tok = None
            key = eng
        for b in reads:
            b.r[key] = o.idx
        for b in writes:
            b.w = {key: o.idx}
            b.r = {}
        self.ops.append(o)
        self.by_eng[eng].append(o)
        return o

    @staticmethod
    def same_ok(p, o):
        if o.dma is not None or p.eng != o.eng:
            return False
        return p.eng == "pe" or (p.big and o.big)

    def emit(self):
        nc = self.nc
        ops = self.ops
        for o in ops:
            for d in o.deps:
                p = ops[d]
                if p.dma is None and not self.same_ok(p, o):
                    p.sig = True
        esem = {e: self.es.enter_context(nc.semaphore("c_" + e)) for e in ("pe", "act", "dve", "pool")}
        for e in ("pe", "act", "dve", "pool"):
            cnt = 0
            for o in self.by_eng[e]:
                if o.dma is None and o.sig:
                    cnt += 1
                    o.tok = (esem[e], cnt)
        block = self.es.enter_context(nc.Block())
        final = [(b.sem, 16 * b.dcount) for b in self.dma_bufs]

        def run(engname, engobj):
            waited = {}
            for o in self.by_eng[engname]:
                for d in sorted(o.deps):
                    p = ops[d]
                    if p.dma is None and self.same_ok(p, o):
                        continue
                    sem, val = p.tok
                    k = id(sem)
                    if waited.get(k, 0) >= val:
                        continue
                    waited[k] = val
                    engobj.wait_ge(sem, val)
                ins = o.fn(engobj)
                if o.sig:
                    ins.then_inc(o.tok[0], 16 if o.dma is not None else 1)
            if engname == "sp":
                for sem, val in final:
                    engobj.wait_ge(sem, val)

        block.tensor(lambda e: run("pe", e))
        block.scalar(lambda e: run("act", e))
        block.vector(lambda e: run("dve", e))
        block.gpsimd(lambda e: run("pool", e))
        block.sync(lambda e: run("sp", e))


def build(NP, DEPTH):
    NTOK = NP + 4 * NSEQ
    nc = bass.Bass("TRN2", target_bir_lowering=False)
    es = ExitStack()
    P = Prog(nc, es)

    def din(name, shape, dt=F32):
        return nc.dram_tensor(name, list(shape), dt, kind="ExternalInput").ap()

    def dout(name, shape):
        return nc.dram_tensor(name, list(shape), F32, kind="ExternalOutput").ap()

    xT_d = din("xT", [D, NTOK])
    peT_d = din("peT", [DEPTH, 256, NTOK])
    spT_d = din("spT", [DEPTH, D, NSEQ, 15])
    scT_d = din("scT", [DEPTH, 4096, NSEQ, 3])
    Cs_d = din("Cs", [DEPTH, NSEQ, 4, 512, 512])
    nT_d = din("nT", [DEPTH, 128, 4, 4, NSEQ])
    mT_d = din("mT", [DEPTH, 4, NSEQ])
    w_in_d = din("w_in", [DEPTH, D, NIN])
    pool_w_d = din("pool_w", [DEPTH, 4, 256, 256])
    wpd_d = din("w_pool_down", [DEPTH, D, D])
    wmd_d = din("w_mlstm_down", [DEPTH, 2048, D])
    wout_d = din("w_out", [DEPTH, D, D])
    wple_d = din("w_ple", [DEPTH, 256, D])
    wpg_d = din("w_ple_gate", [DEPTH, D, D])
    prm_d = din("prm", [DEPTH, 128, NPRM])
    fnw_d = din("fnw", [128, 8])
    bif_d = din("bif", [DEPTH, 4, 2])
    nwB_d = din("nwB", [DEPTH, 128, 2048])
    c_ident = din("c_ident", [128, 128])
    c_maskP = din("c_maskP", [128, 128])
    c_maskS = din("c_maskS", [64, 64])
    c_sel = din("c_sel", [4, 512])
    c_invc = din("c_invc", [128, 4, 16])
    c_seg = din("c_seg", [4, 128])
    c_zero = din("c_zero", [128, 544])
    c_ones = din("c_ones", [128, 128])
    c_eps = din("c_eps", [128, 1])
    c_one1 = din("c_one1", [128, 1])

    yT_d = dout("yT", [D, NTOK])
    npp_d = dout("npp", [DEPTH, D, 15])
    nps_d = dout("nps", [DEPTH, D, NSEQ, 15])
    ncp_d = dout("ncp", [DEPTH, 4096, 3])
    ncs_d = dout("ncs", [DEPTH, 4096, NSEQ, 3])
    nCp_d = dout("nCp", [DEPTH, 4, 512, 512])
    nCs_d = dout("nCs", [DEPTH, NSEQ, 4, 512, 512])
    nnp_d = dout("nnp", [DEPTH, 128, 4, 4, 1])
    nns_d = dout("nns", [DEPTH, 128, 4, 4, NSEQ])
    nmp_d = dout("nmp", [DEPTH, 4, 1])
    nms_d = dout("nms", [DEPTH, 4, NSEQ])
    xs_t = nc.dram_tensor("xs", [D, NTOK], F32, kind="Internal").ap()
    xs_buf = Buf("xs", None)

    def sb(name, shape, dt=F32):
        t = es.enter_context(nc.sbuf_tensor("s_" + name, list(shape), dt))
        return Buf(name, t)

    def ps(name, shape, dt=F32):
        t = es.enter_context(nc.psum_tensor("q_" + name, list(shape), dt))
        return Buf(name, t)

    xt = sb("xt", [128, 8, TT])
    hT = sb("hT", [128, 8, TT], BF16)
    scrA = sb("scrA", [128, 8, TT], BF16)
    gaT = sb("gaT", [128, 8, TT], BF16)
    Fb = [sb(f"F{i}", [128, 528]) for i in range(6)]
    pm = sb("pm", [128, 2, TT], BF16)
    qT = sb("qT", [128, 4, TT], BF16)
    kT = sb("kT", [128, 4, TT], BF16)
    qg = sb("qg", [128, 4, TT], BF16)
    vtok = sb("vtok", [128, 4, 512], BF16)
    sigo = sb("sigo", [128, 4, 512], BF16)
    nwsz = sb("nwsz", [128, 4, 512], BF16)
    g_ig = sb("g_ig", [4, TT])
    g_ls = sb("g_ls", [4, TT])
    g_m = sb("g_m", [4, TT])
    g_B = sb("g_B", [4, TT])
    g_R = sb("g_R", [4, TT])
    g_G = sb("g_G", [4, TT])
    g_small = sb("g_small", [4, 64])
    Atok = sb("Atok", [128, 4, 4])
    emtt = sb("emtt", [128, 4, 4])
    RBs = sb("RBs", [128, TT])
    gBe = sb("gBe", [128, TT])
    tmpD = [sb(f"tmpD{i}", [128, 128]) for i in range(2)]
    SDT = [sb(f"SDT{i}", [128, 128], BF16) for i in range(2)]
    WSBb = [sb(f"WSBb{i}", [128, 16], BF16) for i in range(2)]
    ktok = [sb(f"ktok{i}", [128, 512], BF16) for i in range(2)]
    vw = [sb(f"vw{i}", [128, 512], BF16) for i in range(2)]
    hc1 = sb("hc1", [128, 512])
    hc2 = [sb(f"hc2{i}", [128, 512], BF16) for i in range(2)]
    junk = sb("junk", [128, 512], BF16)
    szb = [sb(f"sz{i}", [128, 512], BF16) for i in range(2)]
    fix15 = sb("fix15", [128, 16])
    onesf = sb("onesf", [128, 1])
    epsf = sb("epsf", [128, 1])
    tiny = [sb(f"tiny{i}", [128, 8]) for i in range(2)]
    hcT = sb("hcT", [128, 16, TT], BF16)
    Cb = [sb(f"C{i}", [128, 4, 512]) for i in range(4)]
    Cbf = [sb(f"Cbf{i}", [128, 4, 512], BF16) for i in range(2)]
    Z = sb("Z", [128, 4, 1088], BF16)
    NSL = 5
    wslot = [sb(f"w{i}", [128, 2048], BF16) for i in range(NSL)]
    ones_bf = sb("ones_bf", [128, 128], BF16)
    maskP = sb("maskP", [128, 128])
    maskS = sb("maskS", [64, 64])
    sel = sb("sel", [4, 512])
    identb = sb("identb", [128, 128], BF16)
    identf = sb("identf", [4, 4])
    invc = sb("invc", [128, 4, 16])
    segc = sb("segc", [4, 128])
    zt = sb("zt", [128, 544])
    prm = sb("prm", [128, NPRM])
    fnw = sb("fnw", [128, 8])
    bif = sb("bif", [4, 2])
    nwh = sb("nwh", [128, 512])
    peT = sb("peT", [128, 2, TT], BF16)
    rstd = sb("rstd", [128, TT])
    phist = sb("phist", [128, 8, 15])
    chist = sb("chist", [128, 32, 3])
    nP = sb("nP", [128, 4, 4, 1])
    nS = sb("nS", [128, 4, 4, NSEQ])
    nbf = sb("nbf", [128, 4, 4, NSEQ], BF16)
    ntmp = sb("ntmp", [128, 4, NSEQ])

    NROT = 4
    prot = [ps(f"prot{i}", [128, 512]) for i in range(NROT)]
    p_st = ps("p_st", [128, 128])
    p_tr = ps("p_tr", [128, 512], BF16)
    p_num = ps("p_num", [128, 512])
    p_sm = ps("p_sm", [128, 512])
    p_sm_at = Buf("p_sm_at", p_sm.t)
    p_sm_den = Buf("p_sm_den", p_sm.t)
    p_sm_n = Buf("p_sm_n", p_sm.t)

    rot_i = [0]

    def PR():
        b = prot[rot_i[0] % NROT]
        rot_i[0] += 1
        return b

    cnt = {"F": 0, "w": 0, "d": 0, "k": 0, "v": 0, "h2": 0, "C": 0, "cbf": 0, "tiny": 0}

    def nxt(key, lst):
        b = lst[cnt[key] % len(lst)]
        cnt[key] += 1
        return b

    def is_big(ap):
        n = 1
        for d in list(ap.shape)[1:]:
            n *= int(d)
        return n >= BIGN

    def MM(out_b, out_ap, lb, l_ap, rb, r_ap, start, stop):
        P.op("pe", lambda e: e.matmul(out_ap, l_ap, r_ap, start=start, stop=stop),
             reads=list(lb) + list(rb), writes=[out_b])

    def TR(out_b, out_ap, ib, in_ap, idb, id_ap):
        P.op("pe", lambda e: e.transpose(out_ap, in_ap, id_ap), reads=[ib, idb], writes=[out_b])

    def ACT(out_b, out_ap, in_bs, in_ap, func, bias=None, scale=None, accum=None, extra_w=()):
        kw = {}
        if bias is not None:
            kw["bias"] = bias
        if scale is not None:
            kw["scale"] = scale
        if accum is not None:
            kw["accum_out"] = accum
        P.op("act", lambda e: e.activation(out_ap, in_ap, func, **kw), reads=list(in_bs),
             writes=[out_b] + list(extra_w), big=is_big(out_ap) and accum is None)

    def V(eng, method, args, reads, writes, kw=None):
        kw = kw or {}
        P.op(eng, lambda e: getattr(e, method)(*args, **kw), reads=reads, writes=writes,
             big=is_big(args[0]) and method != "tensor_tensor_scan")

    def DMA(eng, out_ap, in_ap, reads, writes, owner):
        P.op(eng, lambda e: e.dma_start(out=out_ap, in_=in_ap), reads=reads, writes=writes, dma_owner=owner)

    dbg_names = []

    def DBG(name, buf, ap, shape, bf=False):
        if not DEBUG or name in dbg_names:
            return
        dbg_names.append(name)
        d = nc.dram_tensor("dbg_" + name, list(shape), F32, kind="ExternalOutput").ap()
        DMA("pool" if bf else "sp", d, ap, [buf], [], buf)

    def wload(src_ap, kc, n):
        s = nxt("w", wslot)
        view = s.t[:, 0:kc * n].rearrange("p (k n) -> p k n", k=kc)
        DMA("pool", view, src_ap, [], [s], s)
        return s, view

    def win_blk(l, col0, n):
        return wload(w_in_d[l, :, col0:col0 + n].rearrange("(k p) n -> p k n", p=128), 8, n)

    def sq_blk(wd, l, j0, n, kc):
        return wload(wd[l, :, j0:j0 + n].rearrange("(k p) n -> p k n", p=128), kc, n)

    DMA("sp", identf.t[:], c_ident[0:4, 0:4], [], [identf], identf)
    DMA("pool", identb.t[:], c_ident, [], [identb], identb)
    DMA("sp", maskP.t[:], c_maskP, [], [maskP], maskP)
    DMA("sp", maskS.t[:], c_maskS, [], [maskS], maskS)
    DMA("sp", sel.t[:], c_sel, [], [sel], sel)
    DMA("sp", invc.t[:], c_invc, [], [invc], invc)
    DMA("sp", segc.t[:], c_seg, [], [segc], segc)
    DMA("sp", fnw.t[:], fnw_d, [], [fnw], fnw)
    DMA("sp", zt.t[:], c_zero, [], [zt], zt)
    DMA("pool", ones_bf.t[:], c_ones, [], [ones_bf], ones_bf)
    DMA("sp", onesf.t[:], c_one1, [], [onesf], onesf)
    DMA("sp", epsf.t[:], c_eps, [], [epsf], epsf)
    V("dve", "tensor_copy", (Z.t[:, :, :].rearrange("p c (a u) -> p (c a) u", a=2), zt.t[:, :].unsqueeze(1).to_broadcast([128, 8, 544])), [zt], [Z])
    V("dve", "tensor_copy", (nbf.t[:], zt.t[:, 0:256].rearrange("p (a b c) -> p a b c", a=4, b=4)), [zt], [nbf])

    def rmsnorm_to(dst, NT, wcol0, wbuf):
        ACT(scrA, scrA.t[:, :, 0:NT], [xt], xt.t[:, :, 0:NT], AF.Square)
        pb = PR()
        for kc in range(8):
            MM(pb, pb.t[:, 0:NT], [ones_bf], ones_bf.t[:, :], [scrA], scrA.t[:, kc, 0:NT], kc == 0, kc == 7)
        ACT(rstd, rstd.t[:, 0:NT], [pb, epsf], pb.t[:, 0:NT], AF.Ln, bias=epsf.t[:, 0:1], scale=1.0 / D)
        ACT(rstd, rstd.t[:, 0:NT], [rstd], rstd.t[:, 0:NT], AF.Exp, scale=-0.5)
        for kc in range(8):
            V("dve", "scalar_tensor_tensor",
              (dst.t[:, kc, 0:NT], xt.t[:, kc, 0:NT], wbuf.t[:, wcol0 + kc:wcol0 + kc + 1], rstd.t[:, 0:NT],
               ALU.mult, ALU.mult), [xt, wbuf, rstd], [dst])

    def proj_fm(wb, wv, j, NT, src):
        pb = PR()
        kcn = wv.shape[1]
        for kc in range(kcn):
            MM(pb, pb.t[:, 0:NT], [wb], wv[:, kc, j * 128:(j + 1) * 128], [src], src.t[:, kc, 0:NT],
               kc == 0, kc == kcn - 1)
        return pb

    C_NW, C_PNW, C_PSC, C_CB, C_CW = 0, 8, 16, 24, 56

    for l in range(DEPTH):
        last_layer = l == DEPTH - 1
        DMA("sp", prm.t[:], prm_d[l], [], [prm], prm)
        DMA("sp", bif.t[:], bif_d[l], [], [bif], bif)
        V("dve", "tensor_copy", (g_small.t[:, 0:1], zt.t[0:4, 0:1]), [zt], [g_small])
        V("dve", "tensor_scalar", (g_small.t[:, 1:2], bif.t[:, 1:2], -1.0, None, ALU.mult), [bif], [g_small])
        V("dve", "tensor_copy", (nP.t[:], zt.t[:, 0:16].rearrange("p (a b c) -> p a b c", a=4, b=4)), [zt], [nP])
        DMA("sp", nS.t[:], nT_d[l], [], [nS], nS)
        DMA("sp", g_small.t[:, 8:24], mT_d[l], [], [g_small], g_small)
        V("dve", "tensor_copy", (phist.t[:], zt.t[:, 0:120].rearrange("p (a b) -> p a b", a=8)), [zt], [phist])
        V("dve", "tensor_copy", (chist.t[:], zt.t[:, 0:96].rearrange("p (a b) -> p a b", a=32)), [zt], [chist])

        ntiles = NP // TT
        for ti in range(ntiles + 1):
            samp = ti == ntiles
            NT = 4 * NSEQ if samp else TT
            c0 = ti * TT
            nseg = NSEQ if samp else 1
            Ls = 4 if samp else TT
            L = 64 if samp else 128
            nch = NT // L
            nseq = NSEQ if samp else 1
            ls_ = L // nseq
            first_tile = ti == 0
            last_prompt = ti == ntiles - 1
            mask = maskS if samp else maskP
            xsrc = xT_d if l == 0 else xs_t

            DMA("sp", xt.t[:, :, 0:NT], xsrc[:, c0:c0 + NT].rearrange("(k p) n -> p k n", p=128),
                [xs_buf], [xt], xt)
            DMA("pool", peT.t[:, :, 0:NT], peT_d[l, :, c0:c0 + NT].rearrange("(k p) n -> p k n", p=128),
                [], [peT], peT)

            rmsnorm_to(hT, NT, C_NW, prm)

            def hview(fb, H):
                return fb.t[:, 0:nseg * (H + Ls)].rearrange("p (s w) -> p s w", s=nseg)

            def p3(pb):
                return pb.t[:, 0:NT].rearrange("p (s w) -> p s w", s=nseg)

            for g in range(4):
                win = 2 << g
                wb_u, wv_u = win_blk(l, O_PIN + g * 256, 256)
                wb_z, wv_z = win_blk(l, O_PZ + g * 256, 256)
                wb_p, wv_p = wload(pool_w_d[l, g].rearrange("(k p) n -> p k n", p=128), 2, 256)
                szs = []
                for j in range(2):
                    c = 2 * g + j
                    pb = proj_fm(wb_u, wv_u, j, NT, hT)
                    U = nxt("F", Fb)
                    Uv = hview(U, 15)
                    if samp:
                        DMA("sp", Uv[:, :, 0:15], spT_d[l, c * 128:(c + 1) * 128], [], [U], U)
                    elif first_tile:
                        V("dve", "tensor_copy", (Uv[:, 0, 0:15], zt.t[:, 0:15]), [zt], [U])
                    else:
                        V("dve", "tensor_copy", (Uv[:, 0, 0:15], phist.t[:, c, :]), [phist], [U])
                    ACT(U, Uv[:, :, 15:15 + Ls], [pb], p3(pb), AF.Copy)
                    if samp:
                        DMA("sp", nps_d[l, c * 128:(c + 1) * 128], Uv[:, :, 4:19], [U], [], U)
                    else:
                        V("dve", "tensor_copy", (phist.t[:, c, :], Uv[:, 0, Ls:Ls + 15]), [U], [phist])
                    cur = U
                    sh = 1
                    lo = 0
                    for _ in range(g + 1):
                        nb = nxt("F", Fb)
                        cv = hview(cur, 15)
                        nv = hview(nb, 15)
                        lo += sh
                        V("dve", "tensor_tensor", (nv[:, :, lo:15 + Ls], cv[:, :, lo:15 + Ls],
                                                   cv[:, :, lo - sh:15 + Ls - sh], ALU.add), [cur], [nb])
                        cur = nb
                        sh *= 2
                    sv = hview(cur, 15)
                    pmv = pm.t[:, j, 0:NT].rearrange("p (s w) -> p s w", s=nseg)
                    V("dve", "scalar_tensor_tensor", (pmv, sv[:, :, 15:15 + Ls], 1.0 / win, Uv[:, :, 15:15 + Ls],
                                                      ALU.mult, ALU.subtract), [cur, U], [pm])
                    if first_tile and not samp:
                        V("dve", "tensor_tensor", (fix15.t[:, 0:15], sv[:, 0, 15:30], invc.t[:, g, 0:15], ALU.mult),
                          [cur, invc], [fix15])
                        V("dve", "tensor_tensor", (pm.t[:, j, 0:15], fix15.t[:, 0:15], Uv[:, 0, 15:30], ALU.subtract),
                          [fix15, U], [pm])
                    pz = proj_fm(wb_z, wv_z, j, NT, hT)
                    sz = szb[j]
                    ACT(sz, sz.t[:, 0:NT], [pz], pz.t[:, 0:NT], AF.Silu)
                    szs.append(sz)
                for j in range(2):
                    c = 2 * g + j
                    pb = PR()
                    for ci in range(2):
                        MM(pb, pb.t[:, 0:NT], [wb_p], wv_p[:, ci, j * 128:(j + 1) * 128], [pm], pm.t[:, ci, 0:NT],
                           ci == 0, ci == 1)
                    V("dve", "scalar_tensor_tensor",
                      (scrA.t[:, c, 0:NT], pb.t[:, 0:NT], prm.t[:, C_PSC + c:C_PSC + c + 1], szs[j].t[:, 0:NT],
                       ALU.mult, ALU.mult), [pb, prm, szs[j]], [scrA])
            for jb in range(4):
                wb_d, wv_d = sq_blk(wpd_d, l, jb * 256, 256, 8)
                wb_g, wv_g = win_blk(l, O_GA + jb * 256, 256)
                for jj in range(2):
                    j = jb * 2 + jj
                    pa = proj_fm(wb_d, wv_d, jj, NT, scrA)
                    pg = proj_fm(wb_g, wv_g, jj, NT, hT)
                    sg = nxt("F", Fb)
                    ACT(sg, sg.t[:, 0:NT], [pg], pg.t[:, 0:NT], AF.Sigmoid)
                    V("dve", "tensor_tensor", (gaT.t[:, j, 0:NT], pa.t[:, 0:NT], sg.t[:, 0:NT], ALU.mult),
                      [pa, sg], [gaT])

            wb_if, wv_if = win_blk(l, O_IF, 8)
            pi = PR()
            pf = PR()
            for kc in range(8):
                MM(pi, pi.t[0:4, 0:NT], [wb_if], wv_if[:, kc, 0:4], [hT], hT.t[:, kc, 0:NT], kc == 0, kc == 7)
            for kc in range(8):
                MM(pf, pf.t[0:4, 0:NT], [wb_if], wv_if[:, kc, 4:8], [hT], hT.t[:, kc, 0:NT], kc == 0, kc == 7)
            ACT(g_ig, g_ig.t[:, 0:NT], [pi, bif], pi.t[0:4, 0:NT], AF.Identity, bias=bif.t[:, 0:1])
            ACT(g_ls, g_ls.t[:, 0:NT], [pf, g_small], pf.t[0:4, 0:NT], AF.Exp, bias=g_small.t[:, 1:2], scale=-1.0)
            ACT(g_ls, g_ls.t[:, 0:NT], [g_ls, onesf], g_ls.t[:, 0:NT], AF.Ln, bias=onesf.t[0:4, 0:1])
            V("dve", "tensor_scalar", (g_ls.t[:, 0:NT], g_ls.t[:, 0:NT], -1.0, None, ALU.mult), [g_ls], [g_ls])
            V("dve", "tensor_tensor_scan", (g_B.t[:, 0:TT], g_ls.t[:, 0:TT], zt.t[0:4, 0:TT], 0.0, ALU.add, ALU.add),
              [g_ls, zt], [g_B])
            if samp:
                ls3 = g_ls.t[:, 0:NT].rearrange("p (b t) -> p b t", t=4)
                ig3 = g_ig.t[:, 0:NT].rearrange("p (b t) -> p b t", t=4)
                R3 = g_R.t[:, 0:NT].rearrange("p (b t) -> p b t", t=4)
                G3 = g_G.t[:, 0:NT].rearrange("p (b t) -> p b t", t=4)
                V("dve", "tensor_copy", (g_R.t[:, 0:NT], g_ig.t[:, 0:NT]), [g_ig], [g_R])
                V("dve", "tensor_tensor", (g_small.t[:, 24:40], g_small.t[:, 8:24], ls3[:, :, 0], ALU.add),
                  [g_small, g_ls], [g_small])
                V("dve", "tensor_tensor", (R3[:, :, 0], g_small.t[:, 24:40], ig3[:, :, 0], ALU.max),
                  [g_small, g_ig], [g_R])
                V("dve", "tensor_tensor", (g_G.t[:, 0:NT], g_ls.t[:, 0:NT], segc.t[:, 0:64], ALU.mult), [g_ls, segc], [g_G])
                V("dve", "tensor_tensor", (g_G.t[:, 0:NT], g_G.t[:, 0:NT], segc.t[:, 64:128], ALU.add), [g_G, segc], [g_G])
                V("dve", "tensor_tensor_scan", (g_m.t[:, 0:NT], g_G.t[:, 0:NT], g_R.t[:, 0:NT], 0.0, ALU.add, ALU.max),
                  [g_G, g_R], [g_m])
            else:
                V("dve", "tensor_tensor_scan", (g_m.t[:, 0:NT], g_ls.t[:, 0:NT], g_ig.t[:, 0:NT], g_small.t[:, 0:1],
                                                ALU.add, ALU.max), [g_ls, g_ig, g_small], [g_m])
            V("dve", "tensor_tensor", (g_R.t[:, 0:NT], g_B.t[:, 0:NT], g_m.t[:, 0:NT], ALU.subtract), [g_B, g_m], [g_R])
            V("dve", "tensor_tensor", (g_ig.t[:, 0:NT], g_ig.t[:, 0:NT], g_B.t[:, 0:NT], ALU.subtract), [g_ig, g_B], [g_ig])
            if samp:
                B3 = g_B.t[:, 0:NT].rearrange("p (b t) -> p b t", t=4)
                R3 = g_R.t[:, 0:NT].rearrange("p (b t) -> p b t", t=4)
                G3 = g_G.t[:, 0:NT].rearrange("p (b t) -> p b t", t=4)
                V("dve", "tensor_copy", (g_small.t[:, 24:25], g_small.t[:, 8:9]), [g_small], [g_small])
                V("dve", "tensor_tensor", (g_small.t[:, 25:40], g_small.t[:, 9:24], B3[:, 0:15, 3], ALU.subtract),
                  [g_small, g_B], [g_small])
                V("dve", "tensor_tensor", (G3, R3, g_small.t[:, 24:40].unsqueeze(2).to_broadcast([4, NSEQ, 4]), ALU.add),
                  [g_R, g_small], [g_G])
                m3 = g_m.t[:, 0:NT].rearrange("p (b t) -> p b t", t=4)
                V("dve", "tensor_copy", (g_small.t[:, 40:56], m3[:, :, 3]), [g_m], [g_small])
                DMA("sp", nms_d[l], g_small.t[:, 40:56], [g_small], [], g_small)
            else:
                for c in range(nch):
                    cs = slice(c * L, (c + 1) * L)
                    if c == 0:
                        V("dve", "tensor_scalar", (g_G.t[:, cs], g_R.t[:, cs], g_small.t[:, 0:1], None, ALU.add),
                          [g_R, g_small], [g_G])
                    else:
                        V("dve", "tensor_scalar", (g_G.t[:, cs], g_R.t[:, cs], g_R.t[:, c * L - 1:c * L], None, ALU.subtract),
                          [g_R], [g_G])
                V("dve", "tensor_copy", (g_small.t[:, 0:1], g_m.t[:, NT - 1:NT]), [g_m], [g_small])
                if last_prompt:
                    DMA("sp", nmp_d[l], g_small.t[:, 0:1], [g_small], [], g_small)
            ACT(g_m, g_m.t[:, 0:NT], [g_m], g_m.t[:, 0:NT], AF.Exp, scale=-1.0)
            for c in range(nch):
                cs = slice(c * L, (c + 1) * L)
                TR(p_sm_at, p_sm.t[0:L, c * 4:c * 4 + 4], g_ig, g_ig.t[0:4, cs], identf, identf.t[0:4, 0:4])
                TR(p_sm_at, p_sm.t[0:L, 16 + c * 4:16 + c * 4 + 4], g_m, g_m.t[0:4, cs], identf, identf.t[0:4, 0:4])
            V("dve", "tensor_scalar", (Atok.t[0:L, 0:nch, :], p_sm.t[0:L, 0:nch * 4].rearrange("p (c h) -> p c h", h=4),
                                       LNSCALE, None, ALU.add), [p_sm_at], [Atok])
            V("dve", "tensor_copy", (emtt.t[0:L, 0:nch, :], p_sm.t[0:L, 16:16 + nch * 4].rearrange("p (c h) -> p c h", h=4)),
              [p_sm_at], [emtt])

            if samp:
                V("dve", "tensor_copy", (nbf.t[:], nS.t[:]), [nS], [nbf])
            for h in range(4):
                for qk in range(2):
                    dstb = qT if qk == 0 else kT
                    for half in range(2):
                        wb, wv = win_blk(l, O_QK + qk * 2048 + h * 512 + half * 256, 256)
                        for jj in range(2):
                            dc = half * 2 + jj
                            cch = qk * 16 + h * 4 + dc
                            pb = proj_fm(wb, wv, jj, NT, hT)
                            cb = nxt("F", Fb)
                            cv = hview(cb, 3)
                            if samp:
                                DMA("sp", cv[:, :, 0:3], scT_d[l, cch * 128:(cch + 1) * 128], [], [cb], cb)
                            elif first_tile:
                                V("dve", "tensor_copy", (cv[:, 0, 0:3], zt.t[:, 0:3]), [zt], [cb])
                            else:
                                V("dve", "tensor_copy", (cv[:, 0, 0:3], chist.t[:, cch, :]), [chist], [cb])
                            ACT(cb, cv[:, :, 3:3 + Ls], [pb], p3(pb), AF.Copy)
                            if samp:
                                DMA("sp", ncs_d[l, cch * 128:(cch + 1) * 128], cv[:, :, 4:7], [cb], [], cb)
                            else:
                                V("dve", "tensor_copy", (chist.t[:, cch, :], cv[:, 0, Ls:Ls + 3]), [cb], [chist])
                            acc = nxt("F", Fb)
                            av = acc.t[:, 0:NT].rearrange("p (s w) -> p s w", s=nseg)
                            wc = C_CW + cch * 4
                            V("dve", "tensor_scalar", (av, cv[:, :, 0:Ls], prm.t[:, wc:wc + 1],
                                                       prm.t[:, C_CB + cch:C_CB + cch + 1], ALU.mult, ALU.add),
                              [cb, prm], [acc])
                            for tap in range(1, 4):
                                V("dve", "scalar_tensor_tensor", (av, cv[:, :, tap:tap + Ls], prm.t[:, wc + tap:wc + tap + 1],
                                                                  av, ALU.mult, ALU.add), [cb, prm, acc], [acc])
                            ACT(dstb, dstb.t[:, dc, 0:NT], [acc], acc.t[:, 0:NT], AF.Silu)
                DMA("sp", nwh.t[:], nwB_d[l, :, h * 512:(h + 1) * 512], [], [nwh], nwh)
                for which, off in ((0, O_V), (1, O_O), (2, O_MZ)):
                    blks = [win_blk(l, off + h * 512 + half * 256, 256) for half in range(2)]
                    for c in range(nch):
                        pb = PR()
                        for half in range(2):
                            wb, wv = blks[half]
                            for kc in range(8):
                                MM(pb, pb.t[0:L, half * 256:(half + 1) * 256], [hT], hT.t[:, kc, c * L:(c + 1) * L],
                                   [wb], wv[:, kc, :], kc == 0, kc == 7)
                        if which == 0:
                            ACT(vtok, vtok.t[0:L, c, :], [pb], pb.t[0:L, :], AF.Copy)
                        elif which == 1:
                            ACT(sigo, sigo.t[0:L, c, :], [pb], pb.t[0:L, :], AF.Sigmoid)
                        else:
                            ztmp = nxt("F", Fb)
                            ACT(ztmp, ztmp.t[0:L, 0:512], [pb], pb.t[0:L, :], AF.Silu)
                            V("dve", "tensor_tensor", (nwsz.t[0:L, c, :], ztmp.t[0:L, 0:512], nwh.t[0:L, :], ALU.mult),
                              [ztmp, nwh], [nwsz])
                prb = PR()
                MM(prb, prb.t[:, 0:NT], [sel], sel.t[:, h * 128:(h + 1) * 128], [g_R], g_R.t[:, 0:NT], True, True)
                pgb = PR()
                MM(pgb, pgb.t[:, 0:NT], [sel], sel.t[:, h * 128:(h + 1) * 128], [g_G], g_G.t[:, 0:NT], True, True)
                ACT(RBs, RBs.t[:, 0:NT], [prb], prb.t[:, 0:NT], AF.Copy)
                ACT(gBe, gBe.t[:, 0:NT], [pgb], pgb.t[:, 0:NT], AF.Exp)
                V("dve", "tensor_tensor", (qg.t[:, :, 0:NT], qT.t[:, :, 0:NT],
                                           gBe.t[:, 0:NT].unsqueeze(1).to_broadcast([128, 4, NT]), ALU.mult),
                  [qT, gBe], [qg])
                if samp:
                    zd = Z.t[:, :, :].rearrange("p c (b u) -> p c b u", u=68)[:, :, :, 0:4]
                    V("dve", "tensor_copy", (zd, qg.t[:, :, 0:NT].rearrange("p c (b t) -> p c b t", t=4)), [qg], [Z])
                    nst = nS
                else:
                    nst = nP

                def stage_B(c):
                    cs = slice(c * L, (c + 1) * L)
                    td = nxt("d", tmpD)
                    sdt = SDT[(cnt["d"] - 1) % 2]
                    wsb = WSBb[(cnt["d"] - 1) % 2]
                    V("dve", "tensor_tensor", (td.t[0:L, 0:L], RBs.t[0:L, cs], mask.t[0:L, 0:L], ALU.add), [RBs, mask], [td])
                    ACT(td, td.t[0:L, 0:L], [td, Atok], td.t[0:L, 0:L], AF.Exp, bias=Atok.t[0:L, c, h:h + 1])
                    lastv = td.t[0:L, 0:L].rearrange("p (b t) -> p b t", t=ls_)[:, :, ls_ - 1]
                    V("dve", "tensor_copy", (wsb.t[0:L, 0:nseq], lastv), [td], [wsb])
                    for dc in range(4):
                        MM(p_st, p_st.t[0:L, 0:L], [kT], kT.t[:, dc, cs], [qT], qT.t[:, dc, cs], dc == 0, dc == 3)
                    V("dve", "tensor_tensor", (sdt.t[0:L, 0:L], p_st.t[0:L, 0:L], td.t[0:L, 0:L], ALU.mult), [p_st, td], [sdt])
                    for dc in range(4):
                        TR(p_tr, p_tr.t[0:L, dc * 128:(dc + 1) * 128], kT, kT.t[:, dc, cs], identb, identb.t[:, :])
                    kt = nxt("k", ktok)
                    ACT(kt, kt.t[0:L, :], [p_tr], p_tr.t[0:L, :], AF.Copy)
                    vwb = None
                    if not samp:
                        vwb = nxt("v", vw)
                        V("dve", "tensor_scalar", (vwb.t[0:L, :], vtok.t[0:L, c, :], lastv, None, ALU.mult),
                          [vtok, td], [vwb])
                    return dict(cs=cs, td=td, sdt=sdt, wsb=wsb, lastv=lastv, kt=kt, vwb=vwb)

                def stage_C1(c, X):
                    cs, dt_, sdt, wsb, lastv, kt = X["cs"], X["td"], X["sdt"], X["wsb"], X["lastv"], X["kt"]
                    state_zero = first_tile and c == 0 and not samp
                    den_ap = p_sm.t[0:L, 32:33]
                    MM(p_num, p_num.t[0:L, :], [sdt], sdt.t[0:L, 0:L], [vtok], vtok.t[0:L, c, :], True, state_zero)
                    MM(p_sm_den, den_ap, [sdt], sdt.t[0:L, 0:L], [ones_bf], ones_bf.t[0:L, 0:1], True, state_zero)
                    decay = gBe.t[:, c * L + L - 1:c * L + L]
                    if not samp:
                        Ch = Cb[h]
                        if not state_zero:
                            if c == 0:
                                cbf = Cbf[0]
                                ACT(cbf, cbf.t[:], [Ch], Ch.t[:], AF.Copy)
                                cur_cbf[0] = 0
                            cbf = Cbf[cur_cbf[0]]
                            for dc in range(4):
                                MM(p_num, p_num.t[0:L, :], [qg], qg.t[:, dc, cs], [cbf], cbf.t[:, dc, :], False, dc == 3)
                            for dc in range(4):
                                MM(p_sm_den, den_ap, [qg], qg.t[:, dc, cs], [nbf], nbf.t[:, h, dc, 0:1], False, dc == 3)
                            nxi = 1 - cur_cbf[0]
                        else:
                            nxi = 0
                        vwb = X["vwb"]
                        newcbf = Cbf[nxi]
                        for dc in range(4):
                            pu = PR()
                            MM(pu, pu.t[:, :], [kt], kt.t[0:L, dc * 128:(dc + 1) * 128], [vwb], vwb.t[0:L, :], True, True)
                            if state_zero:
                                V("dve", "tensor_copy", (Ch.t[:, dc, :], pu.t[:, :]), [pu], [Ch])
                            else:
                                V("dve", "scalar_tensor_tensor", (Ch.t[:, dc, :], Ch.t[:, dc, :], decay, pu.t[:, :],
                                                                  ALU.mult, ALU.add), [Ch, gBe, pu], [Ch])
                            ACT(newcbf, newcbf.t[:, dc, :], [Ch], Ch.t[:, dc, :], AF.Copy)
                        cur_cbf[0] = nxi
                    else:
                        def cload(b):
                            cbuf_ = Cb[b % 4]
                            DMA("sp", cbuf_.t[:], Cs_d[l, b, h].rearrange("(k p) e -> p k e", p=128), [], [cbuf_], cbuf_)
                        cload(0)
                        cload(1)
                        for b in range(NSEQ):
                            if b + 2 < NSEQ:
                                cload(b + 2)
                            Cq = Cb[b % 4]
                            cbf = nxt("cbf", Cbf)
                            ACT(cbf, cbf.t[:], [Cq], Cq.t[:], AF.Copy)
                            for dc in range(4):
                                MM(p_num, p_num.t[0:L, :], [Z], Z.t[:, dc, b * 64:(b + 1) * 64], [cbf], cbf.t[:, dc, :],
                                   False, b == NSEQ - 1 and dc == 3)
                            for dc in range(4):
                                MM(p_sm_den, den_ap, [Z], Z.t[:, dc, b * 64:(b + 1) * 64], [nbf], nbf.t[:, h, dc, b:b + 1],
                                   False, b == NSEQ - 1 and dc == 3)
                            vwb = nxt("v", vw)
                            lv_b = dt_.t[0:L, 4 * b + 3:4 * b + 4]
                            V("dve", "tensor_scalar", (vwb.t[0:L, :], vtok.t[0:L, c, :], lv_b, None, ALU.mult),
                              [vtok, dt_], [vwb])
                            dec_b = gBe.t[:, 4 * b + 3:4 * b + 4]
                            for dc in range(4):
                                pu = PR()
                                MM(pu, pu.t[:, :], [kt], kt.t[0:L, dc * 128:(dc + 1) * 128], [vwb], vwb.t[0:L, :], True, True)
                                V("dve", "scalar_tensor_tensor", (Cq.t[:, dc, :], Cq.t[:, dc, :], dec_b, pu.t[:, :],
                                                                  ALU.mult, ALU.add), [Cq, gBe, pu], [Cq])
                            DMA("sp", nCs_d[l, b, h].rearrange("(k p) e -> p k e", p=128), Cq.t[:], [Cq], [], Cq)
                    pn = p_sm.t[:, 64:64 + 4 * nseq].rearrange("p (k b) -> p k b", k=4)
                    for dc in range(4):
                        MM(p_sm_n, pn[:, dc, :], [kt], kt.t[0:L, dc * 128:(dc + 1) * 128], [wsb], wsb.t[0:L, 0:nseq], True, True)
                    decv = gBe.t[:, cs].rearrange("p (b t) -> p b t", t=ls_)[:, :, ls_ - 1]
                    ty = nxt("tiny", tiny)
                    X["ty"] = ty
                    V("dve", "tensor_copy", (ty.t[0:L, 0:1], den_ap), [p_sm_den], [ty])
                    V("dve", "tensor_tensor", (ntmp.t[:, :, 0:nseq], nst.t[:, h, :, :],
                                               decv.unsqueeze(1).to_broadcast([128, 4, nseq]), ALU.mult), [nst, gBe], [ntmp])
                    V("dve", "tensor_tensor", (nst.t[:, h, :, :], ntmp.t[:, :, 0:nseq], pn, ALU.add), [ntmp, p_sm_n], [nst])
                    V("dve", "tensor_copy", (nbf.t[:, h, :, 0:nseq], nst.t[:, h, :, :]), [nst], [nbf])

                def stage_E(c, X):
                    ty = X["ty"]
                    V("dve", "scalar_tensor_tensor", (ty.t[0:L, 1:2], ty.t[0:L, 0:1], -1.0, ty.t[0:L, 0:1], ALU.mult, ALU.max),
                      [ty], [ty])
                    V("dve", "tensor_tensor", (ty.t[0:L, 2:3], ty.t[0:L, 1:2], emtt.t[0:L, c, h:h + 1], ALU.max), [ty, emtt], [ty])
                    V("dve", "reciprocal", (ty.t[0:L, 3:4], ty.t[0:L, 2:3]), [ty], [ty])
                    V("dve", "scalar_tensor_tensor", (hc1.t[0:L, :], p_num.t[0:L, :], ty.t[0:L, 3:4], sigo.t[0:L, c, :],
                                                      ALU.mult, ALU.mult), [p_num, ty, sigo], [hc1])
                    V("dve", "tensor_copy", (ty.t[0:L, 4:5], zt.t[0:L, 0:1]), [zt], [ty])
                    ACT(junk, junk.t[0:L, :], [hc1], hc1.t[0:L, :], AF.Square, accum=ty.t[0:L, 4:5], extra_w=[ty])
                    ACT(ty, ty.t[0:L, 5:6], [ty, epsf], ty.t[0:L, 4:5], AF.Ln, bias=epsf.t[0:L, 0:1], scale=1.0 / 512)
                    ACT(ty, ty.t[0:L, 5:6], [ty], ty.t[0:L, 5:6], AF.Exp, scale=-0.5)
                    h2 = nxt("h2", hc2)
                    X["h2"] = h2
                    V("dve", "scalar_tensor_tensor", (h2.t[0:L, :], hc1.t[0:L, :], ty.t[0:L, 5:6], nwsz.t[0:L, c, :],
                                                      ALU.mult, ALU.mult), [hc1, ty, nwsz], [h2])

                def stage_T(c, X):
                    h2, cs = X["h2"], X["cs"]
                    p_tr3 = p_tr.t[:, :].rearrange("p (e t) -> p e t", e=4)
                    for ec in range(4):
                        TR(p_tr, p_tr3[:, ec, 0:L], h2, h2.t[0:L, ec * 128:(ec + 1) * 128], identb, identb.t[0:L, 0:L])
                    ACT(hcT, hcT.t[:, h * 4:(h + 1) * 4, cs], [p_tr], p_tr3[:, :, 0:L], AF.Copy)

                ctxs = [None] * nch
                ctxs[0] = stage_B(0)
                for c in range(nch):
                    stage_C1(c, ctxs[c])
                    if c > 0:
                        stage_T(c - 1, ctxs[c - 1])
                    if c + 1 < nch:
                        ctxs[c + 1] = stage_B(c + 1)
                    stage_E(c, ctxs[c])
                stage_T(nch - 1, ctxs[nch - 1])
                if last_prompt:
                    DMA("sp", nCp_d[l, h].rearrange("(k p) e -> p k e", p=128), Cb[h].t[:], [Cb[h]], [], Cb[h])
            if last_prompt:
                DMA("sp", nnp_d[l], nP.t[:], [nP], [], nP)
                DMA("sp", npp_d[l].rearrange("(c p) t -> p c t", p=128), phist.t[:], [phist], [], phist)
                DMA("sp", ncp_d[l].rearrange("(c p) t -> p c t", p=128), chist.t[:], [chist], [], chist)
            if samp:
                DMA("sp", nns_d[l], nS.t[:], [nS], [], nS)

            for jb in range(4):
                wb_d, wv_d = sq_blk(wmd_d, l, jb * 256, 128, 16)
                wb_d2, wv_d2 = sq_blk(wmd_d, l, jb * 256 + 128, 128, 16)
                wb_g, wv_g = win_blk(l, O_GB + jb * 256, 256)
                for jj in range(2):
                    j = jb * 2 + jj
                    wbx, wvx = (wb_d, wv_d) if jj == 0 else (wb_d2, wv_d2)
                    pbm = proj_fm(wbx, wvx, 0, NT, hcT)
                    pg = proj_fm(wb_g, wv_g, jj, NT, hT)
                    sg = nxt("F", Fb)
                    ACT(sg, sg.t[:, 0:NT], [pg], pg.t[:, 0:NT], AF.Sigmoid)
                    tt = nxt("F", Fb)
                    V("dve", "tensor_tensor", (tt.t[:, 0:NT], pbm.t[:, 0:NT], sg.t[:, 0:NT], ALU.mult), [pbm, sg], [tt])
                    V("dve", "tensor_tensor", (scrA.t[:, j, 0:NT], tt.t[:, 0:NT], gaT.t[:, j, 0:NT], ALU.add), [tt, gaT], [scrA])
            for jb in range(4):
                wb, wv = sq_blk(wout_d, l, jb * 256, 256, 8)
                for jj in range(2):
                    j = jb * 2 + jj
                    pb = proj_fm(wb, wv, jj, NT, scrA)
                    V("dve", "tensor_tensor", (xt.t[:, j, 0:NT], xt.t[:, j, 0:NT], pb.t[:, 0:NT], ALU.add), [xt, pb], [xt])
            rmsnorm_to(hT, NT, C_PNW, prm)
            for jb in range(4):
                wb, wv = sq_blk(wpg_d, l, jb * 256, 256, 8)
                wbp, wvp = sq_blk(wple_d, l, jb * 256, 256, 2)
                for jj in range(2):
                    j = jb * 2 + jj
                    pg = proj_fm(wb, wv, jj, NT, hT)
                    pp = proj_fm(wbp, wvp, jj, NT, peT)
                    sg = nxt("F", Fb)
                    ACT(sg, sg.t[:, 0:NT], [pg], pg.t[:, 0:NT], AF.Sigmoid)
                    tt = nxt("F", Fb)
                    V("dve", "tensor_tensor", (tt.t[:, 0:NT], pp.t[:, 0:NT], sg.t[:, 0:NT], ALU.mult), [pp, sg], [tt])
                    V("dve", "tensor_tensor", (xt.t[:, j, 0:NT], xt.t[:, j, 0:NT], tt.t[:, 0:NT], ALU.add), [xt, tt], [xt])
            if last_layer:
                ACT(scrA, scrA.t[:, :, 0:NT], [xt], xt.t[:, :, 0:NT], AF.Square)
                pb = PR()
                for kc in range(8):
                    MM(pb, pb.t[:, 0:NT], [ones_bf], ones_bf.t[:, :], [scrA], scrA.t[:, kc, 0:NT], kc == 0, kc == 7)
                ACT(rstd, rstd.t[:, 0:NT], [pb, epsf], pb.t[:, 0:NT], AF.Ln, bias=epsf.t[:, 0:1], scale=1.0 / D)
                ACT(rstd, rstd.t[:, 0:NT], [rstd], rstd.t[:, 0:NT], AF.Exp, scale=-0.5)
                for kc in range(8):
                    V("dve", "scalar_tensor_tensor",
                      (xt.t[:, kc, 0:NT], xt.t[:, kc, 0:NT], fnw.t[:, kc:kc + 1], rstd.t[:, 0:NT], ALU.mult, ALU.mult),
                      [xt, fnw, rstd], [xt])
                DMA("sp", yT_d[:, c0:c0 + NT].rearrange("(k p) n -> p k n", p=128), xt.t[:, :, 0:NT], [xt], [], xt)
            else:
                DMA("sp", xs_t[:, c0:c0 + NT].rearrange("(k p) n -> p k n", p=128), xt.t[:, :, 0:NT], [xt], [xs_buf], xt)

    P.emit()
    es.close()
    return nc


cur_cbf = [None]


def make_consts():
    ident = np.eye(128, dtype=np.float32)
    t = np.arange(128)
    maskP = np.where(t[:, None] <= t[None, :], 0.0, NEG).astype(np.float32)
    s64 = np.arange(64)
    same = (s64[:, None] // 4) == (s64[None, :] // 4)
    maskS = np.where(same & (s64[:, None] <= s64[None, :]), 0.0, NEG).astype(np.float32)
    sel = np.zeros((4, 512), np.float32)
    for h in range(4):
        sel[h, h * 128:(h + 1) * 128] = 1.0
    invc = np.zeros((128, 4, 16), np.float32)
    for g in range(4):
        win = 2 << g
        invc[:, g, :] = 1.0 / np.minimum(np.arange(16) + 1, win)
    seg = np.ones((4, 128), np.float32)
    seg[:, 0:64:4] = 0.0
    seg[:, 64:] = 0.0
    seg[:, 64::4] = -1e30
    return dict(c_ident=ident, c_maskP=maskP, c_maskS=maskS, c_sel=sel, c_invc=invc, c_seg=seg,
                c_zero=np.zeros((128, 544), np.float32), c_ones=np.ones((128, 128), np.float32),
                c_eps=np.full((128, 1), EPS, np.float32), c_one1=np.ones((128, 1), np.float32))


def per_partition(v):
    sh = v.shape
    n = sh[-1] // 128
    r = v.reshape(sh[:-1] + (n, 128))
    return np.ascontiguousarray(np.moveaxis(r, -1, 0))


def prepare_inputs(inp, NP, DEPTH, cores):
    consts = make_consts()
    f = lambda a: np.ascontiguousarray(np.asarray(a, dtype=np.float32))
    prm = np.zeros((DEPTH, 128, NPRM), np.float32)
    for l in range(DEPTH):
        prm[l, :, 0:8] = per_partition(f(inp["norm_w"][l]))
        prm[l, :, 8:16] = per_partition(f(inp["ple_norm_w"][l]))
        prm[l, :, 16:24] = per_partition(f(inp["pool_scale"][l]))
        prm[l, :, 24:56] = per_partition(f(inp["conv_b"][l]))
        cw = per_partition(f(inp["conv_w"][l]))
        prm[l, :, 56:184] = np.transpose(cw, (0, 2, 1)).reshape(128, 128)
    fnw = per_partition(f(inp["final_norm_w"]))
    bif = np.ascontiguousarray(np.transpose(f(inp["b_if"])[:DEPTH].reshape(DEPTH, 2, 4), (0, 2, 1)))
    nwB = np.ascontiguousarray(np.broadcast_to(f(inp["mlstm_norm_w"])[:DEPTH, None, :], (DEPTH, 128, 2048)))
    shared = dict(
        w_in=f(inp["w_in"][:DEPTH]), pool_w=f(inp["pool_w"][:DEPTH]), w_pool_down=f(inp["w_pool_down"][:DEPTH]),
        w_mlstm_down=f(inp["w_mlstm_down"][:DEPTH]), w_out=f(inp["w_out"][:DEPTH]), w_ple=f(inp["w_ple"][:DEPTH]),
        w_ple_gate=f(inp["w_ple_gate"][:DEPTH]), prm=prm, fnw=fnw, bif=bif, nwB=nwB, **consts)
    maps = []
    for c in cores:
        bs = slice(NSEQ * c, NSEQ * (c + 1))
        xp = f(inp["x_prompt"][c, :NP])
        xs = f(inp["x_sample"][bs]).reshape(4 * NSEQ, D)
        xT = np.ascontiguousarray(np.concatenate([xp, xs], 0).T)
        pp = f(inp["p_prompt"][:DEPTH, c, :NP])
        psm = f(inp["p_sample"][:DEPTH, bs]).reshape(DEPTH, 4 * NSEQ, 256)
        peT = np.ascontiguousarray(np.transpose(np.concatenate([pp, psm], 1), (0, 2, 1)))
        spT = np.ascontiguousarray(np.transpose(f(inp["state_pool"][:DEPTH, bs]), (0, 3, 1, 2)))
        scT = np.ascontiguousarray(np.transpose(f(inp["state_conv"][:DEPTH, bs]), (0, 3, 1, 2)))
        Cs = f(inp["state_mlstm_C"][:DEPTH, bs])
        n = f(inp["state_mlstm_n"][:DEPTH, bs])
        nT = np.ascontiguousarray(np.transpose(n.reshape(DEPTH, NSEQ, 4, 4, 128), (0, 4, 2, 3, 1)))
        mT = np.ascontiguousarray(np.transpose(f(inp["state_mlstm_m"][:DEPTH, bs]), (0, 2, 1)))
        m = dict(xT=xT, peT=peT, spT=spT, scT=scT, Cs=Cs, nT=nT, mT=mT)
        m.update(shared)
        maps.append(m)
    return maps


def assemble(results, NP, DEPTH, ncores):
    B = ncores
    y_p = np.zeros((B, NP, D), np.float32)
    y_s = np.zeros((B * NSEQ, 4, D), np.float32)
    pool_p = np.zeros((DEPTH, B, 15, D), np.float32)
    pool_s = np.zeros((DEPTH, B * NSEQ, 15, D), np.float32)
    conv_p = np.zeros((DEPTH, B, 3, 4096), np.float32)
    conv_s = np.zeros((DEPTH, B * NSEQ, 3, 4096), np.float32)
    C_p = np.zeros((DEPTH, B, 4, 512, 512), np.float32)
    C_s = np.zeros((DEPTH, B * NSEQ, 4, 512, 512), np.float32)
    n_p = np.zeros((DEPTH, B, 4, 512), np.float32)
    n_s = np.zeros((DEPTH, B * NSEQ, 4, 512), np.float32)
    m_p = np.zeros((DEPTH, B, 4), np.float32)
    m_s = np.zeros((DEPTH, B * NSEQ, 4), np.float32)
    for c, r in enumerate(results):
        bs = slice(NSEQ * c, NSEQ * (c + 1))
        yT = r["yT"]
        y_p[c] = yT[:, :NP].T
        y_s[bs] = yT[:, NP:].T.reshape(NSEQ, 4, D)
        pool_p[:, c] = np.transpose(r["npp"], (0, 2, 1))
        pool_s[:, bs] = np.transpose(r["nps"], (0, 2, 3, 1))
        conv_p[:, c] = np.transpose(r["ncp"], (0, 2, 1))
        conv_s[:, bs] = np.transpose(r["ncs"], (0, 2, 3, 1))
        C_p[:, c] = r["nCp"]
        C_s[:, bs] = r["nCs"]
        n_p[:, c] = np.transpose(r["nnp"], (0, 4, 2, 3, 1)).reshape(DEPTH, 4, 512)
        n_s[:, bs] = np.transpose(r["nns"], (0, 4, 2, 3, 1)).reshape(DEPTH, NSEQ, 4, 512)
        m_p[:, c] = r["nmp"][:, :, 0]
        m_s[:, bs] = np.transpose(r["nms"], (0, 2, 1))
    return (y_p, y_s, pool_p, pool_s, conv_p, conv_s, C_p, C_s, n_p, n_s, m_p, m_s)


def run(inp, NP, DEPTH, cores, trace=False):
    nc = build(NP, DEPTH)
    maps = prepare_inputs(inp, NP, DEPTH, cores)
    res = run_bass_kernel_spmd(nc, maps, core_ids=list(range(len(cores))), trace=trace)
    outs = assemble(res.results, NP, DEPTH, len(cores))
    return outs, res


def kernel(**inputs):
    outs, _ = run(inputs, 2048, 4, list(range(8)))
    return outs
```

```python
import math
from contextlib import ExitStack

import numpy as np
import ml_dtypes

import concourse.bass as bass
import concourse.mybir as mybir
from concourse.bass_utils import run_bass_kernel_spmd

F32 = mybir.dt.float32
BF16 = mybir.dt.bfloat16
AF = mybir.ActivationFunctionType
ALU = mybir.AluOpType
AX = mybir.AxisListType

D = 1024
NIN = 14344
O_PIN, O_PZ, O_QK, O_V, O_O, O_MZ, O_IF, O_GA, O_GB = 0, 1024, 2048, 6144, 8192, 10240, 12288, 12296, 13320
EPS = 1e-6
LNSCALE = math.log(512 ** -0.5)
NEG = -30000.0
NSEQ = 16
TT = 512
NPRM = 8 + 8 + 8 + 32 + 128
DEBUG = False
BIGN = 512
PUMP_P = 2
NBLK = 96
USE_SCR = True
DBG_TILE = 0


class Buf:
    __slots__ = ("name", "t", "w", "r", "sem", "dcount")

    def __init__(self, name, t):
        self.name = name
        self.t = t
        self.w = {}
        self.r = {}
        self.sem = None
        self.dcount = 0


class Op:
    __slots__ = ("eng", "fn", "deps", "dma", "tok", "sig", "idx", "big")


class Prog:
    ENG = ("pe", "act", "dve", "pool", "sp")

    def __init__(self, nc, es):
        self.nc = nc
        self.es = es
        self.ops = []
        self.by_eng = {e: [] for e in self.ENG}
        self.dma_bufs = []

    def op(self, eng, fn, reads=(), writes=(), dma_owner=None, big=False):
        o = Op()
        o.big = big
        o.eng = eng
        o.fn = fn
        o.idx = len(self.ops)
        o.dma = dma_owner
        o.sig = dma_owner is not None
        deps = set()
        for b in reads:
            deps.update(b.w.values())
        for b in writes:
            deps.update(b.w.values())
            deps.update(b.r.values())
        o.deps = deps
        if dma_owner is not None:
            if dma_owner.sem is None:
                dma_owner.sem = self.es.enter_context(self.nc.semaphore("d_" + dma_owner.name))
                self.dma_bufs.append(dma_owner)
            dma_owner.dcount += 1
            o.tok = (dma_owner.sem, 16 * dma_owner.dcount)
            key = ("dma", id(dma_owner))
        else:
            o.tok = None
            key = eng
        for b in reads:
            b.r[key] = o.idx
        for b in writes:
            b.w = {key: o.idx}
            b.r = {}
        self.ops.append(o)
        self.by_eng[eng].append(o)
        return o

    @staticmethod
    def same_ok(p, o):
        if o.dma is not None or p.eng != o.eng:
            return False
        return p.eng == "pe" or (p.big and o.big)

    def emit(self):
        nc = self.nc
        ops = self.ops
        for o in ops:
            for d in o.deps:
                p = ops[d]
                if p.dma is None and not self.same_ok(p, o):
                    p.sig = True
        esem = {e: self.es.enter_context(nc.semaphore("c_" + e)) for e in ("pe", "act", "dve", "pool")}
        for e in ("pe", "act", "dve", "pool"):
            cnt = 0
            for o in self.by_eng[e]:
                if o.dma is None and o.sig:
                    cnt += 1
                    o.tok = (esem[e], cnt)
        block = self.es.enter_context(nc.Block())
        final = [(b.sem, 16 * b.dcount) for b in self.dma_bufs]

        def run(engname, engobj):
            waited = {}
            for o in self.by_eng[engname]:
                for d in sorted(o.deps):
                    p = ops[d]
                    if p.dma is None and self.same_ok(p, o):
                        continue
                    sem, val = p.tok
                    k = id(sem)
                    if waited.get(k, 0) >= val:
                        continue
                    waited[k] = val
                    engobj.wait_ge(sem, val)
                ins = o.fn(engobj)
                if o.sig:
                    ins.then_inc(o.tok[0], 16 if o.dma is not None else 1)
            if engname == "sp":
                for sem, val in final:
                    engobj.wait_ge(sem, val)

        block.tensor(lambda e: run("pe", e))
        block.scalar(lambda e: run("act", e))
        block.vector(lambda e: run("dve", e))
        block.gpsimd(lambda e: run("pool", e))
        block.sync(lambda e: run("sp", e))


def build(NP, DEPTH):
    NTOK = NP + 4 * NSEQ
    nc = bass.Bass("TRN2", target_bir_lowering=False)
    es = ExitStack()
    P = Prog(nc, es)

    def din(name, shape, dt=F32):
        return nc.dram_tensor(name, list(shape), dt, kind="ExternalInput").ap()

    def dout(name, shape):
        return nc.dram_tensor(name, list(shape), F32, kind="ExternalOutput").ap()

    xT_d = din("xT", [D, NTOK])
    peT_d = din("peT", [DEPTH, 256, NTOK])
    spT_d = din("spT", [DEPTH, D, NSEQ, 15])
    scT_d = din("scT", [DEPTH, 4096, NSEQ, 3])
    Cs_d = din("Cs", [DEPTH, NSEQ, 4, 512, 512])
    nT_d = din("nT", [DEPTH, 128, 4, 4, NSEQ])
    mT_d = din("mT", [DEPTH, 4, NSEQ])
    w_in_d = din("w_in", [DEPTH, D, NIN])
    pool_w_d = din("pool_w", [DEPTH, 4, 256, 256])
    wpd_d = din("w_pool_down", [DEPTH, D, D])
    wmd_d = din("w_mlstm_down", [DEPTH, 2048, D])
    wout_d = din("w_out", [DEPTH, D, D])
    wple_d = din("w_ple", [DEPTH, 256, D])
    wpg_d = din("w_ple_gate", [DEPTH, D, D])
    prm_d = din("prm", [DEPTH, 128, NPRM])
    fnw_d = din("fnw", [128, 8])
    bif_d = din("bif", [DEPTH, 4, 2])
    nwB_d = din("nwB", [DEPTH, 128, 2048])
    c_ident = din("c_ident", [128, 128])
    c_maskP = din("c_maskP", [128, 128])
    c_maskS = din("c_maskS", [64, 64])
    c_sel = din("c_sel", [4, 512])
    c_invc = din("c_invc", [128, 4, 16])
    c_seg = din("c_seg", [4, 128])
    c_zero = din("c_zero", [128, 544])
    c_ones = din("c_ones", [128, 128])
    c_eps = din("c_eps", [128, 1])
    c_one1 = din("c_one1", [128, 1])

    yT_d = dout("yT", [D, NTOK])
    npp_d = dout("npp", [DEPTH, D, 15])
    nps_d = dout("nps", [DEPTH, D, NSEQ, 15])
    ncp_d = dout("ncp", [DEPTH, 4096, 3])
    ncs_d = dout("ncs", [DEPTH, 4096, NSEQ, 3])
    nCp_d = dout("nCp", [DEPTH, 4, 512, 512])
    nCs_d = dout("nCs", [DEPTH, NSEQ, 4, 512, 512])
    nnp_d = dout("nnp", [DEPTH, 128, 4, 4, 1])
    nns_d = dout("nns", [DEPTH, 128, 4, 4, NSEQ])
    nmp_d = dout("nmp", [DEPTH, 4, 1])
    nms_d = dout("nms", [DEPTH, 4, NSEQ])
    xs_t = nc.dram_tensor("xs", [D, NTOK], F32, kind="Internal").ap()
    xs_buf = Buf("xs", None)
    wscr_d = nc.dram_tensor("wscr", [NBLK, 128, 2048], BF16, kind="Internal").ap()

    def sb(name, shape, dt=F32):
        t = es.enter_context(nc.sbuf_tensor("s_" + name, list(shape), dt))
        return Buf(name, t)

    def ps(name, shape, dt=F32):
        t = es.enter_context(nc.psum_tensor("q_" + name, list(shape), dt))
        return Buf(name, t)

    xt = sb("xt", [128, 8, TT])
    hT = sb("hT", [128, 8, TT], BF16)
    scrA = sb("scrA", [128, 8, TT], BF16)
    gaT = sb("gaT", [128, 8, TT], BF16)
    Fb = [sb(f"F{i}", [128, 528]) for i in range(6)]
    pm = sb("pm", [128, 2, TT], BF16)
    qT = sb("qT", [128, 4, TT], BF16)
    kT = sb("kT", [128, 4, TT], BF16)
    qg = sb("qg", [128, 4, TT], BF16)
    vtok = sb("vtok", [128, 4, 512], BF16)
    sigo = sb("sigo", [128, 4, 512], BF16)
    nwsz = sb("nwsz", [128, 4, 512], BF16)
    g_ig = sb("g_ig", [4, TT])
    g_ls = sb("g_ls", [4, TT])
    g_m = sb("g_m", [4, TT])
    g_B = sb("g_B", [4, TT])
    g_R = sb("g_R", [4, TT])
    g_G = sb("g_G", [4, TT])
    g_small = sb("g_small", [4, 64])
    Atok = sb("Atok", [128, 4, 4])
    emtt = sb("emtt", [128, 4, 4])
    RBs = sb("RBs", [128, TT])
    gBe = sb("gBe", [128, TT])
    tmpD = [sb(f"tmpD{i}", [128, 128]) for i in range(2)]
    SDT = [sb(f"SDT{i}", [128, 128], BF16) for i in range(2)]
    WSBb = [sb(f"WSBb{i}", [128, 16], BF16) for i in range(2)]
    ktok = [sb(f"ktok{i}", [128, 512], BF16) for i in range(2)]
    vw = [sb(f"vw{i}", [128, 512], BF16) for i in range(2)]
    hc1 = sb("hc1", [128, 512])
    hc2 = [sb(f"hc2{i}", [128, 512], BF16) for i in range(2)]
    junk = sb("junk", [128, 512], BF16)
    szb = [sb(f"sz{i}", [128, 512], BF16) for i in range(2)]
    fix15 = sb("fix15", [128, 16])
    onesf = sb("onesf", [128, 1])
    epsf = sb("epsf", [128, 1])
    tiny = [sb(f"tiny{i}", [128, 8]) for i in range(2)]
    hcT = sb("hcT", [128, 16, TT], BF16)
    Cb = [sb(f"C{i}", [128, 4, 512]) for i in range(4)]
    Cbf = [sb(f"Cbf{i}", [128, 4, 512], BF16) for i in range(2)]
    Z = sb("Z", [128, 4, 1088], BF16)
    NSL = 5
    wslot = [sb(f"w{i}", [128, 2048], BF16) for i in range(NSL)]
    ones_bf = sb("ones_bf", [128, 128], BF16)
    maskP = sb("maskP", [128, 128])
    maskS = sb("maskS", [64, 64])
    sel = sb("sel", [4, 512])
    identb = sb("identb", [128, 128], BF16)
    identf = sb("identf", [4, 4])
    invc = sb("invc", [128, 4, 16])
    segc = sb("segc", [4, 128])
    zt = sb("zt", [128, 544])
    prm = sb("prm", [128, NPRM])
    fnw = sb("fnw", [128, 8])
    bif = sb("bif", [4, 2])
    nwh = sb("nwh", [128, 512])
    peT = sb("peT", [128, 2, TT], BF16)
    rstd = sb("rstd", [128, TT])
    phist = sb("phist", [128, 8, 15])
    chist = sb("chist", [128, 32, 3])
    nP = sb("nP", [128, 4, 4, 1])
    nS = sb("nS", [128, 4, 4, NSEQ])
    nbf = sb("nbf", [128, 4, 4, NSEQ], BF16)
    ntmp = sb("ntmp", [128, 4, NSEQ])

    NROT = 4
    prot = [ps(f"prot{i}", [128, 512]) for i in range(NROT)]
    p_st = ps("p_st", [128, 128])
    p_tr = ps("p_tr", [128, 512], BF16)
    p_num = ps("p_num", [128, 512])
    p_sm = ps("p_sm", [128, 512])
    p_sm_at = Buf("p_sm_at", p_sm.t)
    p_sm_den = Buf("p_sm_den", p_sm.t)
    p_sm_n = Buf("p_sm_n", p_sm.t)

    rot_i = [0]

    def PR():
        b = prot[rot_i[0] % NROT]
        rot_i[0] += 1
        return b

    cnt = {"F": 0, "w": 0, "d": 0, "k": 0, "v": 0, "h2": 0, "C": 0, "cbf": 0, "tiny": 0}

    def nxt(key, lst):
        b = lst[cnt[key] % len(lst)]
        cnt[key] += 1
        return b

    def is_big(ap):
        n = 1
        for d in list(ap.shape)[1:]:
            n *= int(d)
        return n >= BIGN

    def MM(out_b, out_ap, lb, l_ap, rb, r_ap, start, stop):
        P.op("pe", lambda e: e.matmul(out_ap, l_ap, r_ap, start=start, stop=stop),
             reads=list(lb) + list(rb), writes=[out_b])

    def TR(out_b, out_ap, ib, in_ap, idb, id_ap):
        P.op("pe", lambda e: e.transpose(out_ap, in_ap, id_ap), reads=[ib, idb], writes=[out_b])

    def ACT(out_b, out_ap, in_bs, in_ap, func, bias=None, scale=None, accum=None, extra_w=()):
        kw = {}
        if bias is not None:
            kw["bias"] = bias
        if scale is not None:
            kw["scale"] = scale
        if accum is not None:
            kw["accum_out"] = accum
        P.op("act", lambda e: e.activation(out_ap, in_ap, func, **kw), reads=list(in_bs),
             writes=[out_b] + list(extra_w), big=is_big(out_ap) and accum is None)

    def V(eng, method, args, reads, writes, kw=None):
        kw = kw or {}
        P.op(eng, lambda e: getattr(e, method)(*args, **kw), reads=reads, writes=writes,
             big=is_big(args[0]) and method != "tensor_tensor_scan")

    def DMA(eng, out_ap, in_ap, reads, writes, owner):
        P.op(eng, lambda e: e.dma_start(out=out_ap, in_=in_ap), reads=reads, writes=writes, dma_owner=owner)

    dbg_names = []

    def DBG(name, buf, ap, shape, bf=False):
        if not DEBUG or name in dbg_names:
            return
        dbg_names.append(name)
        d = nc.dram_tensor("dbg_" + name, list(shape), F32, kind="ExternalOutput").ap()
        DMA("pool" if bf else "sp", d, ap, [buf], [], buf)

    wkeys = {}
    wbufs = []
    wmode = ["cast"]

    def wload(src_ap, kc, n, key):
        s_ = nxt("w", wslot)
        view = s_.t[:, 0:kc * n].rearrange("p (k n) -> p k n", k=kc)
        if key not in wkeys:
            wkeys[key] = len(wkeys)
            wbufs.append(Buf("wscr%d" % wkeys[key], None))
        bi = wkeys[key]
        assert bi < NBLK
        if wmode[0] == "cast":
            DMA("pool", view, src_ap, [], [s_], s_)
            if USE_SCR:
                DMA("sp", wscr_d[bi, :, 0:kc * n], s_.t[:, 0:kc * n], [s_], [wbufs[bi]], s_)
        else:
            DMA("pool", s_.t[:, 0:kc * n], wscr_d[bi, :, 0:kc * n], [wbufs[bi]], [s_], s_)
        return s_, view

    def win_blk(l, col0, n):
        return wload(w_in_d[l, :, col0:col0 + n].rearrange("(k p) n -> p k n", p=128), 8, n, ("win", col0))

    def sq_blk(wd, l, j0, n, kc):
        return wload(wd[l, :, j0:j0 + n].rearrange("(k p) n -> p k n", p=128), kc, n, (str(wd.tensor.name) if hasattr(wd, "tensor") else id(wd), j0))

    DMA("sp", identf.t[:], c_ident[0:4, 0:4], [], [identf], identf)
    DMA("pool", identb.t[:], c_ident, [], [identb], identb)
    DMA("sp", maskP.t[:], c_maskP, [], [maskP], maskP)
    DMA("sp", maskS.t[:], c_maskS, [], [maskS], maskS)
    DMA("sp", sel.t[:], c_sel, [], [sel], sel)
    DMA("sp", invc.t[:], c_invc, [], [invc], invc)
    DMA("sp", segc.t[:], c_seg, [], [segc], segc)
    DMA("sp", fnw.t[:], fnw_d, [], [fnw], fnw)
    DMA("sp", zt.t[:], c_zero, [], [zt], zt)
    DMA("pool", ones_bf.t[:], c_ones, [], [ones_bf], ones_bf)
    DMA("sp", onesf.t[:], c_one1, [], [onesf], onesf)
    DMA("sp", epsf.t[:], c_eps, [], [epsf], epsf)
    V("dve", "tensor_copy", (Z.t[:, :, :].rearrange("p c (a u) -> p (c a) u", a=2), zt.t[:, :].unsqueeze(1).to_broadcast([128, 8, 544])), [zt], [Z])
    V("dve", "tensor_copy", (nbf.t[:], zt.t[:, 0:256].rearrange("p (a b c) -> p a b c", a=4, b=4)), [zt], [nbf])

    def rmsnorm_to(dst, NT, wcol0, wbuf):
        ACT(scrA, scrA.t[:, :, 0:NT], [xt], xt.t[:, :, 0:NT], AF.Square)
        pb = PR()
        for kc in range(8):
            MM(pb, pb.t[:, 0:NT], [ones_bf], ones_bf.t[:, :], [scrA], scrA.t[:, kc, 0:NT], kc == 0, kc == 7)
        ACT(rstd, rstd.t[:, 0:NT], [pb, epsf], pb.t[:, 0:NT], AF.Ln, bias=epsf.t[:, 0:1], scale=1.0 / D)
        ACT(rstd, rstd.t[:, 0:NT], [rstd], rstd.t[:, 0:NT], AF.Exp, scale=-0.5)
        for kc in range(8):
            V("dve", "scalar_tensor_tensor",
              (dst.t[:, kc, 0:NT], xt.t[:, kc, 0:NT], wbuf.t[:, wcol0 + kc:wcol0 + kc + 1], rstd.t[:, 0:NT],
               ALU.mult, ALU.mult), [xt, wbuf, rstd], [dst])

    def proj_fm(wb, wv, j, NT, src):
        pb = PR()
        kcn = wv.shape[1]
        for kc in range(kcn):
            MM(pb, pb.t[:, 0:NT], [wb], wv[:, kc, j * 128:(j + 1) * 128], [src], src.t[:, kc, 0:NT],
               kc == 0, kc == kcn - 1)
        return pb

    C_NW, C_PNW, C_PSC, C_CB, C_CW = 0, 8, 16, 24, 56

    for l in range(DEPTH):
        last_layer = l == DEPTH - 1
        DMA("sp", prm.t[:], prm_d[l], [], [prm], prm)
        DMA("sp", bif.t[:], bif_d[l], [], [bif], bif)
        V("dve", "tensor_copy", (g_small.t[:, 0:1], zt.t[0:4, 0:1]), [zt], [g_small])
        V("dve", "tensor_scalar", (g_small.t[:, 1:2], bif.t[:, 1:2], -1.0, None, ALU.mult), [bif], [g_small])
        V("dve", "tensor_copy", (nP.t[:], zt.t[:, 0:16].rearrange("p (a b c) -> p a b c", a=4, b=4)), [zt], [nP])
        DMA("sp", nS.t[:], nT_d[l], [], [nS], nS)
        DMA("sp", g_small.t[:, 8:24], mT_d[l], [], [g_small], g_small)
        V("dve", "tensor_copy", (phist.t[:], zt.t[:, 0:120].rearrange("p (a b) -> p a b", a=8)), [zt], [phist])
        V("dve", "tensor_copy", (chist.t[:], zt.t[:, 0:96].rearrange("p (a b) -> p a b", a=32)), [zt], [chist])

        ntiles = NP // TT
        for ti in range(ntiles + 1):
            samp = ti == ntiles
            NT = 4 * NSEQ if samp else TT
            c0 = ti * TT
            nseg = NSEQ if samp else 1
            Ls = 4 if samp else TT
            L = 64 if samp else 128
            nch = NT // L
            nseq = NSEQ if samp else 1
            ls_ = L // nseq
            first_tile = ti == 0
            last_prompt = ti == ntiles - 1
            mask = maskS if samp else maskP
            xsrc = xT_d if l == 0 else xs_t
            wmode[0] = "cast" if (ti == 0 or not USE_SCR) else "scr"

            DMA("sp", xt.t[:, :, 0:NT], xsrc[:, c0:c0 + NT].rearrange("(k p) n -> p k n", p=128),
                [xs_buf], [xt], xt)
            DMA("pool", peT.t[:, :, 0:NT], peT_d[l, :, c0:c0 + NT].rearrange("(k p) n -> p k n", p=128),
                [], [peT], peT)

            rmsnorm_to(hT, NT, C_NW, prm)

            def hview(fb, H):
                return fb.t[:, 0:nseg * (H + Ls)].rearrange("p (s w) -> p s w", s=nseg)

            def p3(pb):
                return pb.t[:, 0:NT].rearrange("p (s w) -> p s w", s=nseg)

            def step2_gen():
                for g in range(4):
                    win = 2 << g
                    wb_u, wv_u = win_blk(l, O_PIN + g * 256, 256)
                    wb_z, wv_z = win_blk(l, O_PZ + g * 256, 256)
                    wb_p, wv_p = wload(pool_w_d[l, g].rearrange("(k p) n -> p k n", p=128), 2, 256, ("pw", g))
                    szs = []
                    for j in range(2):
                        c = 2 * g + j
                        pb = proj_fm(wb_u, wv_u, j, NT, hT)
                        if samp:
                            hs = nxt("F", Fb)
                        U = nxt("F", Fb)
                        Uv = hview(U, 15)
                        if samp:
                            DMA("sp", hs.t[:, 0:NSEQ * 15], spT_d[l, c * 128:(c + 1) * 128].rearrange("p b t -> p (b t)"),
                                [], [hs], hs)
                            V("dve", "tensor_copy", (Uv[:, :, 0:15], hs.t[:, 0:NSEQ * 15].rearrange("p (b t) -> p b t", t=15)),
                              [hs], [U])
                        elif first_tile:
                            V("dve", "tensor_copy", (Uv[:, 0, 0:15], zt.t[:, 0:15]), [zt], [U])
                        else:
                            V("dve", "tensor_copy", (Uv[:, 0, 0:15], phist.t[:, c, :]), [phist], [U])
                        ACT(U, Uv[:, :, 15:15 + Ls], [pb], p3(pb), AF.Copy)
                        if samp:
                            ho = nxt("F", Fb)
                            V("dve", "tensor_copy", (ho.t[:, 0:NSEQ * 15].rearrange("p (b t) -> p b t", t=15), Uv[:, :, 4:19]),
                              [U], [ho])
                            DMA("sp", nps_d[l, c * 128:(c + 1) * 128].rearrange("p b t -> p (b t)"), ho.t[:, 0:NSEQ * 15],
                                [ho], [], ho)
                        else:
                            V("dve", "tensor_copy", (phist.t[:, c, :], Uv[:, 0, Ls:Ls + 15]), [U], [phist])
                        cur = U
                        sh = 1
                        lo = 0
                        for _ in range(g + 1):
                            nb = nxt("F", Fb)
                            cv = hview(cur, 15)
                            nv = hview(nb, 15)
                            lo += sh
                            V("dve", "tensor_tensor", (nv[:, :, lo:15 + Ls], cv[:, :, lo:15 + Ls],
                                                       cv[:, :, lo - sh:15 + Ls - sh], ALU.add), [cur], [nb])
                            cur = nb
                            sh *= 2
                        sv = hview(cur, 15)
                        pmv = pm.t[:, j, 0:NT].rearrange("p (s w) -> p s w", s=nseg)
                        V("dve", "scalar_tensor_tensor", (pmv, sv[:, :, 15:15 + Ls], 1.0 / win, Uv[:, :, 15:15 + Ls],
                                                          ALU.mult, ALU.subtract), [cur, U], [pm])
                        if first_tile and not samp:
                            V("dve", "tensor_tensor", (fix15.t[:, 0:15], sv[:, 0, 15:30], invc.t[:, g, 0:15], ALU.mult),
                              [cur, invc], [fix15])
                            V("dve", "tensor_tensor", (pm.t[:, j, 0:15], fix15.t[:, 0:15], Uv[:, 0, 15:30], ALU.subtract),
                              [fix15, U], [pm])
                        pz = proj_fm(wb_z, wv_z, j, NT, hT)
                        sz = szb[j]
                        ACT(sz, sz.t[:, 0:NT], [pz], pz.t[:, 0:NT], AF.Silu)
                        szs.append(sz)
                        yield
                    for j in range(2):
                        c = 2 * g + j
                        pb = PR()
                        for ci in range(2):
                            MM(pb, pb.t[:, 0:NT], [wb_p], wv_p[:, ci, j * 128:(j + 1) * 128], [pm], pm.t[:, ci, 0:NT],
                               ci == 0, ci == 1)
                        V("dve", "scalar_tensor_tensor",
                          (scrA.t[:, c, 0:NT], pb.t[:, 0:NT], prm.t[:, C_PSC + c:C_PSC + c + 1], szs[j].t[:, 0:NT],
                           ALU.mult, ALU.mult), [pb, prm, szs[j]], [scrA])
                        yield
                for jb in range(4):
                    wb_d, wv_d = sq_blk(wpd_d, l, jb * 256, 256, 8)
                    wb_g, wv_g = win_blk(l, O_GA + jb * 256, 256)
                    for jj in range(2):
                        j = jb * 2 + jj
                        pa = proj_fm(wb_d, wv_d, jj, NT, scrA)
                        pg = proj_fm(wb_g, wv_g, jj, NT, hT)
                        sg = nxt("F", Fb)
                        ACT(sg, sg.t[:, 0:NT], [pg], pg.t[:, 0:NT], AF.Sigmoid)
                        V("dve", "tensor_tensor", (gaT.t[:, j, 0:NT], pa.t[:, 0:NT], sg.t[:, 0:NT], ALU.mult),
                          [pa, sg], [gaT])
                        yield

            s2 = step2_gen()

            def pump(n):
                for _ in range(n):
                    try:
                        next(s2)
                    except StopIteration:
                        return

            wb_if, wv_if = win_blk(l, O_IF, 8)
            pi = PR()
            pf = PR()
            for kc in range(8):
                MM(pi, pi.t[0:4, 0:NT], [wb_if], wv_if[:, kc, 0:4], [hT], hT.t[:, kc, 0:NT], kc == 0, kc == 7)
            for kc in range(8):
                MM(pf, pf.t[0:4, 0:NT], [wb_if], wv_if[:, kc, 4:8], [hT], hT.t[:, kc, 0:NT], kc == 0, kc == 7)
            ACT(g_ig, g_ig.t[:, 0:NT], [pi, bif], pi.t[0:4, 0:NT], AF.Identity, bias=bif.t[:, 0:1])
            ACT(g_ls, g_ls.t[:, 0:NT], [pf, g_small], pf.t[0:4, 0:NT], AF.Exp, bias=g_small.t[:, 1:2], scale=-1.0)
            ACT(g_ls, g_ls.t[:, 0:NT], [g_ls, onesf], g_ls.t[:, 0:NT], AF.Ln, bias=onesf.t[0:4, 0:1])
            V("dve", "tensor_scalar", (g_ls.t[:, 0:NT], g_ls.t[:, 0:NT], -1.0, None, ALU.mult), [g_ls], [g_ls])
            V("dve", "tensor_tensor_scan", (g_B.t[:, 0:TT], g_ls.t[:, 0:TT], zt.t[0:4, 0:TT], 0.0, ALU.add, ALU.add),
              [g_ls, zt], [g_B])
            if samp:
                ls3 = g_ls.t[:, 0:NT].rearrange("p (b t) -> p b t", t=4)
                ig3 = g_ig.t[:, 0:NT].rearrange("p (b t) -> p b t", t=4)
                R3 = g_R.t[:, 0:NT].rearrange("p (b t) -> p b t", t=4)
                G3 = g_G.t[:, 0:NT].rearrange("p (b t) -> p b t", t=4)
                V("dve", "tensor_copy", (g_R.t[:, 0:NT], g_ig.t[:, 0:NT]), [g_ig], [g_R])
                V("dve", "tensor_tensor", (g_small.t[:, 24:40], g_small.t[:, 8:24], ls3[:, :, 0], ALU.add),
                  [g_small, g_ls], [g_small])
                V("dve", "tensor_tensor", (R3[:, :, 0], g_small.t[:, 24:40], ig3[:, :, 0], ALU.max),
                  [g_small, g_ig], [g_R])
                V("dve", "tensor_tensor", (g_G.t[:, 0:NT], g_ls.t[:, 0:NT], segc.t[:, 0:64], ALU.mult), [g_ls, segc], [g_G])
                V("dve", "tensor_tensor", (g_G.t[:, 0:NT], g_G.t[:, 0:NT], segc.t[:, 64:128], ALU.add), [g_G, segc], [g_G])
                V("dve", "tensor_tensor_scan", (g_m.t[:, 0:NT], g_G.t[:, 0:NT], g_R.t[:, 0:NT], 0.0, ALU.add, ALU.max),
                  [g_G, g_R], [g_m])
            else:
                V("dve", "tensor_tensor_scan", (g_m.t[:, 0:NT], g_ls.t[:, 0:NT], g_ig.t[:, 0:NT], g_small.t[:, 0:1],
                                                ALU.add, ALU.max), [g_ls, g_ig, g_small], [g_m])
            V("dve", "tensor_tensor", (g_R.t[:, 0:NT], g_B.t[:, 0:NT], g_m.t[:, 0:NT], ALU.subtract), [g_B, g_m], [g_R])
            V("dve", "tensor_tensor", (g_ig.t[:, 0:NT], g_ig.t[:, 0:NT], g_B.t[:, 0:NT], ALU.subtract), [g_ig, g_B], [g_ig])
            if samp:
                B3 = g_B.t[:, 0:NT].rearrange("p (b t) -> p b t", t=4)
                R3 = g_R.t[:, 0:NT].rearrange("p (b t) -> p b t", t=4)
                G3 = g_G.t[:, 0:NT].rearrange("p (b t) -> p b t", t=4)
                V("dve", "tensor_copy", (g_small.t[:, 24:25], g_small.t[:, 8:9]), [g_small], [g_small])
                V("dve", "tensor_tensor", (g_small.t[:, 25:40], g_small.t[:, 9:24], B3[:, 0:15, 3], ALU.subtract),
                  [g_small, g_B], [g_small])
                V("dve", "tensor_tensor", (G3, R3, g_small.t[:, 24:40].unsqueeze(2).to_broadcast([4, NSEQ, 4]), ALU.add),
                  [g_R, g_small], [g_G])
                m3 = g_m.t[:, 0:NT].rearrange("p (b t) -> p b t", t=4)
                V("dve", "tensor_copy", (g_small.t[:, 40:56], m3[:, :, 3]), [g_m], [g_small])
                DMA("sp", nms_d[l], g_small.t[:, 40:56], [g_small], [], g_small)
            else:
                for c in range(nch):
                    cs = slice(c * L, (c + 1) * L)
                    if c == 0:
                        V("dve", "tensor_scalar", (g_G.t[:, cs], g_R.t[:, cs], g_small.t[:, 0:1], None, ALU.add),
                          [g_R, g_small], [g_G])
                    else:
                        V("dve", "tensor_scalar", (g_G.t[:, cs], g_R.t[:, cs], g_R.t[:, c * L - 1:c * L], None, ALU.subtract),
                          [g_R], [g_G])
                V("dve", "tensor_copy", (g_small.t[:, 0:1], g_m.t[:, NT - 1:NT]), [g_m], [g_small])
                if last_prompt:
                    DMA("sp", nmp_d[l], g_small.t[:, 0:1], [g_small], [], g_small)
            ACT(g_m, g_m.t[:, 0:NT], [g_m], g_m.t[:, 0:NT], AF.Exp, scale=-1.0)
            for c in range(nch):
                cs = slice(c * L, (c + 1) * L)
                TR(p_sm_at, p_sm.t[0:L, c * 4:c * 4 + 4], g_ig, g_ig.t[0:4, cs], identf, identf.t[0:4, 0:4])
                TR(p_sm_at, p_sm.t[0:L, 16 + c * 4:16 + c * 4 + 4], g_m, g_m.t[0:4, cs], identf, identf.t[0:4, 0:4])
            V("dve", "tensor_scalar", (Atok.t[0:L, 0:nch, :], p_sm.t[0:L, 0:nch * 4].rearrange("p (c h) -> p c h", h=4),
                                       LNSCALE, None, ALU.add), [p_sm_at], [Atok])
            V("dve", "tensor_copy", (emtt.t[0:L, 0:nch, :], p_sm.t[0:L, 16:16 + nch * 4].rearrange("p (c h) -> p c h", h=4)),
              [p_sm_at], [emtt])

            if samp:
                V("dve", "tensor_copy", (nbf.t[:], nS.t[:]), [nS], [nbf])
            for h in range(4):
                for qk in range(2):
                    dstb = qT if qk == 0 else kT
                    for half in range(2):
                        wb, wv = win_blk(l, O_QK + qk * 2048 + h * 512 + half * 256, 256)
                        for jj in range(2):
                            dc = half * 2 + jj
                            cch = qk * 16 + h * 4 + dc
                            pb = proj_fm(wb, wv, jj, NT, hT)
                            cb = nxt("F", Fb)
                            cv = hview(cb, 3)
                            if samp:
                                hs = nxt("F", Fb)
                                DMA("sp", hs.t[:, 0:NSEQ * 3], scT_d[l, cch * 128:(cch + 1) * 128].rearrange("p b t -> p (b t)"),
                                    [], [hs], hs)
                                V("dve", "tensor_copy", (cv[:, :, 0:3], hs.t[:, 0:NSEQ * 3].rearrange("p (b t) -> p b t", t=3)),
                                  [hs], [cb])
                            elif first_tile:
                                V("dve", "tensor_copy", (cv[:, 0, 0:3], zt.t[:, 0:3]), [zt], [cb])
                            else:
                                V("dve", "tensor_copy", (cv[:, 0, 0:3], chist.t[:, cch, :]), [chist], [cb])
                            ACT(cb, cv[:, :, 3:3 + Ls], [pb], p3(pb), AF.Copy)
                            if samp:
                                ho = nxt("F", Fb)
                                V("dve", "tensor_copy", (ho.t[:, 0:NSEQ * 3].rearrange("p (b t) -> p b t", t=3), cv[:, :, 4:7]),
                                  [cb], [ho])
                                DMA("sp", ncs_d[l, cch * 128:(cch + 1) * 128].rearrange("p b t -> p (b t)"), ho.t[:, 0:NSEQ * 3],
                                    [ho], [], ho)
                            else:
                                V("dve", "tensor_copy", (chist.t[:, cch, :], cv[:, 0, Ls:Ls + 3]), [cb], [chist])
                            acc = nxt("F", Fb)
                            av = acc.t[:, 0:NT].rearrange("p (s w) -> p s w", s=nseg)
                            wc = C_CW + cch * 4
                            V("dve", "tensor_scalar", (av, cv[:, :, 0:Ls], prm.t[:, wc:wc + 1],
                                                       prm.t[:, C_CB + cch:C_CB + cch + 1], ALU.mult, ALU.add),
                              [cb, prm], [acc])
                            for tap in range(1, 4):
                                V("dve", "scalar_tensor_tensor", (av, cv[:, :, tap:tap + Ls], prm.t[:, wc + tap:wc + tap + 1],
                                                                  av, ALU.mult, ALU.add), [cb, prm, acc], [acc])
                            ACT(dstb, dstb.t[:, dc, 0:NT], [acc], acc.t[:, 0:NT], AF.Silu)
                DMA("sp", nwh.t[:], nwB_d[l, :, h * 512:(h + 1) * 512], [], [nwh], nwh)
                for which, off in ((0, O_V), (1, O_O), (2, O_MZ)):
                    blks = [win_blk(l, off + h * 512 + half * 256, 256) for half in range(2)]
                    for c in range(nch):
                        pb = PR()
                        for half in range(2):
                            wb, wv = blks[half]
                            for kc in range(8):
                                MM(pb, pb.t[0:L, half * 256:(half + 1) * 256], [hT], hT.t[:, kc, c * L:(c + 1) * L],
                                   [wb], wv[:, kc, :], kc == 0, kc == 7)
                        if which == 0:
                            ACT(vtok, vtok.t[0:L, c, :], [pb], pb.t[0:L, :], AF.Copy)
                        elif which == 1:
                            ACT(sigo, sigo.t[0:L, c, :], [pb], pb.t[0:L, :], AF.Sigmoid)
                        else:
                            ztmp = nxt("F", Fb)
                            ACT(ztmp, ztmp.t[0:L, 0:512], [pb], pb.t[0:L, :], AF.Silu)
                            V("dve", "tensor_tensor", (nwsz.t[0:L, c, :], ztmp.t[0:L, 0:512], nwh.t[0:L, :], ALU.mult),
                              [ztmp, nwh], [nwsz])
                prb = PR()
                MM(prb, prb.t[:, 0:NT], [sel], sel.t[:, h * 128:(h + 1) * 128], [g_R], g_R.t[:, 0:NT], True, True)
                pgb = PR()
                MM(pgb, pgb.t[:, 0:NT], [sel], sel.t[:, h * 128:(h + 1) * 128], [g_G], g_G.t[:, 0:NT], True, True)
                ACT(RBs, RBs.t[:, 0:NT], [prb], prb.t[:, 0:NT], AF.Copy)
                ACT(gBe, gBe.t[:, 0:NT], [pgb], pgb.t[:, 0:NT], AF.Exp)
                V("dve", "tensor_tensor", (qg.t[:, :, 0:NT], qT.t[:, :, 0:NT],
                                           gBe.t[:, 0:NT].unsqueeze(1).to_broadcast([128, 4, NT]), ALU.mult),
                  [qT, gBe], [qg])
                if samp:
                    zd = Z.t[:, :, :].rearrange("p c (b u) -> p c b u", u=68)[:, :, :, 0:4]
                    V("dve", "tensor_copy", (zd, qg.t[:, :, 0:NT].rearrange("p c (b t) -> p c b t", t=4)), [qg], [Z])
                    nst = nS
                else:
                    nst = nP

                def stage_B(c):
                    cs = slice(c * L, (c + 1) * L)
                    td = nxt("d", tmpD)
                    sdt = SDT[(cnt["d"] - 1) % 2]
                    wsb = WSBb[(cnt["d"] - 1) % 2]
                    V("dve", "tensor_tensor", (td.t[0:L, 0:L], RBs.t[0:L, cs], mask.t[0:L, 0:L], ALU.add), [RBs, mask], [td])
                    ACT(td, td.t[0:L, 0:L], [td, Atok], td.t[0:L, 0:L], AF.Exp, bias=Atok.t[0:L, c, h:h + 1])
                    lastv = td.t[0:L, 0:L].rearrange("p (b t) -> p b t", t=ls_)[:, :, ls_ - 1]
                    V("dve", "tensor_copy", (wsb.t[0:L, 0:nseq], lastv), [td], [wsb])
                    for dc in range(4):
                        MM(p_st, p_st.t[0:L, 0:L], [kT], kT.t[:, dc, cs], [qT], qT.t[:, dc, cs], dc == 0, dc == 3)
                    V("dve", "tensor_tensor", (sdt.t[0:L, 0:L], p_st.t[0:L, 0:L], td.t[0:L, 0:L], ALU.mult), [p_st, td], [sdt])
                    for dc in range(4):
                        TR(p_tr, p_tr.t[0:L, dc * 128:(dc + 1) * 128], kT, kT.t[:, dc, cs], identb, identb.t[:, :])
                    kt = nxt("k", ktok)
                    ACT(kt, kt.t[0:L, :], [p_tr], p_tr.t[0:L, :], AF.Copy)
                    vwb = None
                    if not samp:
                        vwb = nxt("v", vw)
                        V("dve", "tensor_scalar", (vwb.t[0:L, :], vtok.t[0:L, c, :], lastv, None, ALU.mult),
                          [vtok, td], [vwb])
                    return dict(cs=cs, td=td, sdt=sdt, wsb=wsb, lastv=lastv, kt=kt, vwb=vwb)

                def stage_C1(c, X):
                    cs, dt_, sdt, wsb, lastv, kt = X["cs"], X["td"], X["sdt"], X["wsb"], X["lastv"], X["kt"]
                    state_zero = first_tile and c == 0 and not samp
                    den_ap = p_sm.t[0:L, 32:33]
                    MM(p_num, p_num.t[0:L, :], [sdt], sdt.t[0:L, 0:L], [vtok], vtok.t[0:L, c, :], True, state_zero)
                    MM(p_sm_den, den_ap, [sdt], sdt.t[0:L, 0:L], [ones_bf], ones_bf.t[0:L, 0:1], True, state_zero)
                    decay = gBe.t[:, c * L + L - 1:c * L + L]
                    if not samp:
                        Ch = Cb[h]
                        if not state_zero:
                            if c == 0:
                                cbf = Cbf[0]
                                ACT(cbf, cbf.t[:], [Ch], Ch.t[:], AF.Copy)
                                cur_cbf[0] = 0
                            cbf = Cbf[cur_cbf[0]]
                            for dc in range(4):
                                MM(p_num, p_num.t[0:L, :], [qg], qg.t[:, dc, cs], [cbf], cbf.t[:, dc, :], False, dc == 3)
                            for dc in range(4):
                                MM(p_sm_den, den_ap, [qg], qg.t[:, dc, cs], [nbf], nbf.t[:, h, dc, 0:1], False, dc == 3)
                            nxi = 1 - cur_cbf[0]
                        else:
                            nxi = 0
                        vwb = X["vwb"]
                        newcbf = Cbf[nxi]
                        for dc in range(4):
                            pu = PR()
                            MM(pu, pu.t[:, :], [kt], kt.t[0:L, dc * 128:(dc + 1) * 128], [vwb], vwb.t[0:L, :], True, True)
                            if state_zero:
                                V("dve", "tensor_copy", (Ch.t[:, dc, :], pu.t[:, :]), [pu], [Ch])
                            else:
                                V("dve", "scalar_tensor_tensor", (Ch.t[:, dc, :], Ch.t[:, dc, :], decay, pu.t[:, :],
                                                                  ALU.mult, ALU.add), [Ch, gBe, pu], [Ch])
                            ACT(newcbf, newcbf.t[:, dc, :], [Ch], Ch.t[:, dc, :], AF.Copy)
                        cur_cbf[0] = nxi
                    else:
                        def cload(b):
                            cbuf_ = Cb[b % 4]
                            DMA("sp", cbuf_.t[:], Cs_d[l, b, h].rearrange("(k p) e -> p k e", p=128), [], [cbuf_], cbuf_)
                        cload(0)
                        cload(1)
                        for b in range(NSEQ):
                            if b + 2 < NSEQ:
                                cload(b + 2)
                            Cq = Cb[b % 4]
                            cbf = nxt("cbf", Cbf)
                            ACT(cbf, cbf.t[:], [Cq], Cq.t[:], AF.Copy)
                            for dc in range(4):
                                MM(p_num, p_num.t[0:L, :], [Z], Z.t[:, dc, b * 64:(b + 1) * 64], [cbf], cbf.t[:, dc, :],
                                   False, b == NSEQ - 1 and dc == 3)
                            for dc in range(4):
                                MM(p_sm_den, den_ap, [Z], Z.t[:, dc, b * 64:(b + 1) * 64], [nbf], nbf.t[:, h, dc, b:b + 1],
                                   False, b == NSEQ - 1 and dc == 3)
                            vwb = nxt("v", vw)
                            lv_b = dt_.t[0:L, 4 * b + 3:4 * b + 4]
                            V("dve", "tensor_scalar", (vwb.t[0:L, :], vtok.t[0:L, c, :], lv_b, None, ALU.mult),
                              [vtok, dt_], [vwb])
                            dec_b = gBe.t[:, 4 * b + 3:4 * b + 4]
                            for dc in range(4):
                                pu = PR()
                                MM(pu, pu.t[:, :], [kt], kt.t[0:L, dc * 128:(dc + 1) * 128], [vwb], vwb.t[0:L, :], True, True)
                                V("dve", "scalar_tensor_tensor", (Cq.t[:, dc, :], Cq.t[:, dc, :], dec_b, pu.t[:, :],
                                                                  ALU.mult, ALU.add), [Cq, gBe, pu], [Cq])
                            DMA("sp", nCs_d[l, b, h].rearrange("(k p) e -> p k e", p=128), Cq.t[:], [Cq], [], Cq)
                            if b % 2 == 1:
                                pump(1)
                    pn = p_sm.t[:, 64:64 + 4 * nseq].rearrange("p (k b) -> p k b", k=4)
                    for dc in range(4):
                        MM(p_sm_n, pn[:, dc, :], [kt], kt.t[0:L, dc * 128:(dc + 1) * 128], [wsb], wsb.t[0:L, 0:nseq], True, True)
                    decv = gBe.t[:, cs].rearrange("p (b t) -> p b t", t=ls_)[:, :, ls_ - 1]
                    ty = nxt("tiny", tiny)
                    X["ty"] = ty
                    V("dve", "tensor_copy", (ty.t[0:L, 0:1], den_ap), [p_sm_den], [ty])
                    V("dve", "tensor_tensor", (ntmp.t[:, :, 0:nseq], nst.t[:, h, :, :],
                                               decv.unsqueeze(1).to_broadcast([128, 4, nseq]), ALU.mult), [nst, gBe], [ntmp])
                    V("dve", "tensor_tensor", (nst.t[:, h, :, :], ntmp.t[:, :, 0:nseq], pn, ALU.add), [ntmp, p_sm_n], [nst])
                    V("dve", "tensor_copy", (nbf.t[:, h, :, 0:nseq], nst.t[:, h, :, :]), [nst], [nbf])

                def stage_E(c, X):
                    ty = X["ty"]
                    V("dve", "scalar_tensor_tensor", (ty.t[0:L, 1:2], ty.t[0:L, 0:1], -1.0, ty.t[0:L, 0:1], ALU.mult, ALU.max),
                      [ty], [ty])
                    V("dve", "tensor_tensor", (ty.t[0:L, 2:3], ty.t[0:L, 1:2], emtt.t[0:L, c, h:h + 1], ALU.max), [ty, emtt], [ty])
                    V("dve", "reciprocal", (ty.t[0:L, 3:4], ty.t[0:L, 2:3]), [ty], [ty])
                    V("dve", "scalar_tensor_tensor", (hc1.t[0:L, :], p_num.t[0:L, :], ty.t[0:L, 3:4], sigo.t[0:L, c, :],
                                                      ALU.mult, ALU.mult), [p_num, ty, sigo], [hc1])
                    V("dve", "tensor_copy", (ty.t[0:L, 4:5], zt.t[0:L, 0:1]), [zt], [ty])
                    ACT(junk, junk.t[0:L, :], [hc1], hc1.t[0:L, :], AF.Square, accum=ty.t[0:L, 4:5], extra_w=[ty])
                    ACT(ty, ty.t[0:L, 5:6], [ty, epsf], ty.t[0:L, 4:5], AF.Ln, bias=epsf.t[0:L, 0:1], scale=1.0 / 512)
                    ACT(ty, ty.t[0:L, 5:6], [ty], ty.t[0:L, 5:6], AF.Exp, scale=-0.5)
                    h2 = nxt("h2", hc2)
                    X["h2"] = h2
                    V("dve", "scalar_tensor_tensor", (h2.t[0:L, :], hc1.t[0:L, :], ty.t[0:L, 5:6], nwsz.t[0:L, c, :],
                                                      ALU.mult, ALU.mult), [hc1, ty, nwsz], [h2])

                def stage_T(c, X):
                    h2, cs = X["h2"], X["cs"]
                    p_tr3 = p_tr.t[:, :].rearrange("p (e t) -> p e t", e=4)
                    for ec in range(4):
                        TR(p_tr, p_tr3[:, ec, 0:L], h2, h2.t[0:L, ec * 128:(ec + 1) * 128], identb, identb.t[0:L, 0:L])
                    ACT(hcT, hcT.t[:, h * 4:(h + 1) * 4, cs], [p_tr], p_tr3[:, :, 0:L], AF.Copy)

                ctxs = [None] * nch
                ctxs[0] = stage_B(0)
                for c in range(nch):
                    stage_C1(c, ctxs[c])
                    if not samp:
                        pump(PUMP_P)
                    if c > 0:
                        stage_T(c - 1, ctxs[c - 1])
                    if c + 1 < nch:
                        ctxs[c + 1] = stage_B(c + 1)
                    stage_E(c, ctxs[c])
                stage_T(nch - 1, ctxs[nch - 1])
                if last_prompt:
                    DMA("sp", nCp_d[l, h].rearrange("(k p) e -> p k e", p=128), Cb[h].t[:], [Cb[h]], [], Cb[h])
            if last_prompt:
                DMA("sp", nnp_d[l], nP.t[:], [nP], [], nP)
                DMA("sp", npp_d[l].rearrange("(c p) t -> p c t", p=128), phist.t[:], [phist], [], phist)
                DMA("sp", ncp_d[l].rearrange("(c p) t -> p c t", p=128), chist.t[:], [chist], [], chist)
            if samp:
                DMA("sp", nns_d[l], nS.t[:], [nS], [], nS)

            pump(1000)
            for jb in range(4):
                wb_d, wv_d = sq_blk(wmd_d, l, jb * 256, 128, 16)
                wb_d2, wv_d2 = sq_blk(wmd_d, l, jb * 256 + 128, 128, 16)
                wb_g, wv_g = win_blk(l, O_GB + jb * 256, 256)
                for jj in range(2):
                    j = jb * 2 + jj
                    wbx, wvx = (wb_d, wv_d) if jj == 0 else (wb_d2, wv_d2)
                    pbm = proj_fm(wbx, wvx, 0, NT, hcT)
                    pg = proj_fm(wb_g, wv_g, jj, NT, hT)
                    sg = nxt("F", Fb)
                    ACT(sg, sg.t[:, 0:NT], [pg], pg.t[:, 0:NT], AF.Sigmoid)
                    tt = nxt("F", Fb)
                    V("dve", "tensor_tensor", (tt.t[:, 0:NT], pbm.t[:, 0:NT], sg.t[:, 0:NT], ALU.mult), [pbm, sg], [tt])
                    V("dve", "tensor_tensor", (scrA.t[:, j, 0:NT], tt.t[:, 0:NT], gaT.t[:, j, 0:NT], ALU.add), [tt, gaT], [scrA])
            for jb in range(4):
                wb, wv = sq_blk(wout_d, l, jb * 256, 256, 8)
                for jj in range(2):
                    j = jb * 2 + jj
                    pb = proj_fm(wb, wv, jj, NT, scrA)
                    V("dve", "tensor_tensor", (xt.t[:, j, 0:NT], xt.t[:, j, 0:NT], pb.t[:, 0:NT], ALU.add), [xt, pb], [xt])
            rmsnorm_to(hT, NT, C_PNW, prm)
            for jb in range(4):
                wb, wv = sq_blk(wpg_d, l, jb * 256, 256, 8)
                wbp, wvp = sq_blk(wple_d, l, jb * 256, 256, 2)
                for jj in range(2):
                    j = jb * 2 + jj
                    pg = proj_fm(wb, wv, jj, NT, hT)
                    pp = proj_fm(wbp, wvp, jj, NT, peT)
                    sg = nxt("F", Fb)
                    ACT(sg, sg.t[:, 0:NT], [pg], pg.t[:, 0:NT], AF.Sigmoid)
                    tt = nxt("F", Fb)
                    V("dve", "tensor_tensor", (tt.t[:, 0:NT], pp.t[:, 0:NT], sg.t[:, 0:NT], ALU.mult), [pp, sg], [tt])
                    V("dve", "tensor_tensor", (xt.t[:, j, 0:NT], xt.t[:, j, 0:NT], tt.t[:, 0:NT], ALU.add), [xt, tt], [xt])
            if last_layer:
                ACT(scrA, scrA.t[:, :, 0:NT], [xt], xt.t[:, :, 0:NT], AF.Square)
                pb = PR()
                for kc in range(8):
                    MM(pb, pb.t[:, 0:NT], [ones_bf], ones_bf.t[:, :], [scrA], scrA.t[:, kc, 0:NT], kc == 0, kc == 7)
                ACT(rstd, rstd.t[:, 0:NT], [pb, epsf], pb.t[:, 0:NT], AF.Ln, bias=epsf.t[:, 0:1], scale=1.0 / D)
                ACT(rstd, rstd.t[:, 0:NT], [rstd], rstd.t[:, 0:NT], AF.Exp, scale=-0.5)
                for kc in range(8):
                    V("dve", "scalar_tensor_tensor",
                      (xt.t[:, kc, 0:NT], xt.t[:, kc, 0:NT], fnw.t[:, kc:kc + 1], rstd.t[:, 0:NT], ALU.mult, ALU.mult),
                      [xt, fnw, rstd], [xt])
                DMA("sp", yT_d[:, c0:c0 + NT].rearrange("(k p) n -> p k n", p=128), xt.t[:, :, 0:NT], [xt], [], xt)
            else:
                DMA("sp", xs_t[:, c0:c0 + NT].rearrange("(k p) n -> p k n", p=128), xt.t[:, :, 0:NT], [xt], [xs_buf], xt)

    P.emit()
    es.close()
    return nc


cur_cbf = [None]


def make_consts():
    ident = np.eye(128, dtype=np.float32)
    t = np.arange(128)
    maskP = np.where(t[:, None] <= t[None, :], 0.0, NEG).astype(np.float32)
    s64 = np.arange(64)
    same = (s64[:, None] // 4) == (s64[None, :] // 4)
    maskS = np.where(same & (s64[:, None] <= s64[None, :]), 0.0, NEG).astype(np.float32)
    sel = np.zeros((4, 512), np.float32)
    for h in range(4):
        sel[h, h * 128:(h + 1) * 128] = 1.0
    invc = np.zeros((128, 4, 16), np.float32)
    for g in range(4):
        win = 2 << g
        invc[:, g, :] = 1.0 / np.minimum(np.arange(16) + 1, win)
    seg = np.ones((4, 128), np.float32)
    seg[:, 0:64:4] = 0.0
    seg[:, 64:] = 0.0
    seg[:, 64::4] = -1e30
    return dict(c_ident=ident, c_maskP=maskP, c_maskS=maskS, c_sel=sel, c_invc=invc, c_seg=seg,
                c_zero=np.zeros((128, 544), np.float32), c_ones=np.ones((128, 128), np.float32),
                c_eps=np.full((128, 1), EPS, np.float32), c_one1=np.ones((128, 1), np.float32))


def per_partition(v):
    sh = v.shape
    n = sh[-1] // 128
    r = v.reshape(sh[:-1] + (n, 128))
    return np.ascontiguousarray(np.moveaxis(r, -1, 0))


def prepare_inputs(inp, NP, DEPTH, cores):
    consts = make_consts()
    f = lambda a: np.ascontiguousarray(np.asarray(a, dtype=np.float32))
    prm = np.zeros((DEPTH, 128, NPRM), np.float32)
    for l in range(DEPTH):
        prm[l, :, 0:8] = per_partition(f(inp["norm_w"][l]))
        prm[l, :, 8:16] = per_partition(f(inp["ple_norm_w"][l]))
        prm[l, :, 16:24] = per_partition(f(inp["pool_scale"][l]))
        prm[l, :, 24:56] = per_partition(f(inp["conv_b"][l]))
        cw = per_partition(f(inp["conv_w"][l]))
        prm[l, :, 56:184] = np.transpose(cw, (0, 2, 1)).reshape(128, 128)
    fnw = per_partition(f(inp["final_norm_w"]))
    bif = np.ascontiguousarray(np.transpose(f(inp["b_if"])[:DEPTH].reshape(DEPTH, 2, 4), (0, 2, 1)))
    nwB = np.ascontiguousarray(np.broadcast_to(f(inp["mlstm_norm_w"])[:DEPTH, None, :], (DEPTH, 128, 2048)))
    shared = dict(
        w_in=f(inp["w_in"][:DEPTH]), pool_w=f(inp["pool_w"][:DEPTH]), w_pool_down=f(inp["w_pool_down"][:DEPTH]),
        w_mlstm_down=f(inp["w_mlstm_down"][:DEPTH]), w_out=f(inp["w_out"][:DEPTH]), w_ple=f(inp["w_ple"][:DEPTH]),
        w_ple_gate=f(inp["w_ple_gate"][:DEPTH]), prm=prm, fnw=fnw, bif=bif, nwB=nwB, **consts)
    maps = []
    for c in cores:
        bs = slice(NSEQ * c, NSEQ * (c + 1))
        xp = f(inp["x_prompt"][c, :NP])
        xs = f(inp["x_sample"][bs]).reshape(4 * NSEQ, D)
        xT = np.ascontiguousarray(np.concatenate([xp, xs], 0).T)
        pp = f(inp["p_prompt"][:DEPTH, c, :NP])
        psm = f(inp["p_sample"][:DEPTH, bs]).reshape(DEPTH, 4 * NSEQ, 256)
        peT = np.ascontiguousarray(np.transpose(np.concatenate([pp, psm], 1), (0, 2, 1)))
        spT = np.ascontiguousarray(np.transpose(f(inp["state_pool"][:DEPTH, bs]), (0, 3, 1, 2)))
        scT = np.ascontiguousarray(np.transpose(f(inp["state_conv"][:DEPTH, bs]), (0, 3, 1, 2)))
        Cs = f(inp["state_mlstm_C"][:DEPTH, bs])
        n = f(inp["state_mlstm_n"][:DEPTH, bs])
        nT = np.ascontiguousarray(np.transpose(n.reshape(DEPTH, NSEQ, 4, 4, 128), (0, 4, 2, 3, 1)))
        mT = np.ascontiguousarray(np.transpose(f(inp["state_mlstm_m"][:DEPTH, bs]), (0, 2, 1)))
        m = dict(xT=xT, peT=peT, spT=spT, scT=scT, Cs=Cs, nT=nT, mT=mT)
        m.update(shared)
        maps.append(m)
    return maps


def assemble(results, NP, DEPTH, ncores):
    B = ncores
    y_p = np.zeros((B, NP, D), np.float32)
    y_s = np.zeros((B * NSEQ, 4, D), np.float32)
    pool_p = np.zeros((DEPTH, B, 15, D), np.float32)
    pool_s = np.zeros((DEPTH, B * NSEQ, 15, D), np.float32)
    conv_p = np.zeros((DEPTH, B, 3, 4096), np.float32)
    conv_s = np.zeros((DEPTH, B * NSEQ, 3, 4096), np.float32)
    C_p = np.zeros((DEPTH, B, 4, 512, 512), np.float32)
    C_s = np.zeros((DEPTH, B * NSEQ, 4, 512, 512), np.float32)
    n_p = np.zeros((DEPTH, B, 4, 512), np.float32)
    n_s = np.zeros((DEPTH, B * NSEQ, 4, 512), np.float32)
    m_p = np.zeros((DEPTH, B, 4), np.float32)
    m_s = np.zeros((DEPTH, B * NSEQ, 4), np.float32)
    for c, r in enumerate(results):
        bs = slice(NSEQ * c, NSEQ * (c + 1))
        yT = r["yT"]
        y_p[c] = yT[:, :NP].T
        y_s[bs] = yT[:, NP:].T.reshape(NSEQ, 4, D)
        pool_p[:, c] = np.transpose(r["npp"], (0, 2, 1))
        pool_s[:, bs] = np.transpose(r["nps"], (0, 2, 3, 1))
        conv_p[:, c] = np.transpose(r["ncp"], (0, 2, 1))
        conv_s[:, bs] = np.transpose(r["ncs"], (0, 2, 3, 1))
        C_p[:, c] = r["nCp"]
        C_s[:, bs] = r["nCs"]
        n_p[:, c] = np.transpose(r["nnp"], (0, 4, 2, 3, 1)).reshape(DEPTH, 4, 512)
        n_s[:, bs] = np.transpose(r["nns"], (0, 4, 2, 3, 1)).reshape(DEPTH, NSEQ, 4, 512)
        m_p[:, c] = r["nmp"][:, :, 0]
        m_s[:, bs] = np.transpose(r["nms"], (0, 2, 1))
    return (y_p, y_s, pool_p, pool_s, conv_p, conv_s, C_p, C_s, n_p, n_s, m_p, m_s)


def run(inp, NP, DEPTH, cores, trace=False):
    nc = build(NP, DEPTH)
    maps = prepare_inputs(inp, NP, DEPTH, cores)
    res = run_bass_kernel_spmd(nc, maps, core_ids=list(range(len(cores))), trace=trace)
    outs = assemble(res.results, NP, DEPTH, len(cores))
    return outs, res


def kernel(**inputs):
    outs, _ = run(inputs, 2048, 4, list(range(8)))
    return outs
```

```python
import math
from contextlib import ExitStack

import numpy as np
import ml_dtypes

import concourse.bass as bass
import concourse.mybir as mybir
from concourse.bass_utils import run_bass_kernel_spmd

F32 = mybir.dt.float32
BF16 = mybir.dt.bfloat16
AF = mybir.ActivationFunctionType
ALU = mybir.AluOpType
AX = mybir.AxisListType

D = 1024
NIN = 14344
O_PIN, O_PZ, O_QK, O_V, O_O, O_MZ, O_IF, O_GA, O_GB = 0, 1024, 2048, 6144, 8192, 10240, 12288, 12296, 13320
EPS = 1e-6
LNSCALE = math.log(512 ** -0.5)
NEG = -30000.0
NSEQ = 16
TT = 512
NPRM = 8 + 8 + 8 + 32 + 128
DEBUG = False
BIGN = 512
PUMP_P = 2
NBLK = 96
USE_SCR = True
INTERLEAVE = False
CAST_ENG = "pool"
PREFETCH_X = True
DBG_TILE = 0


class Buf:
    __slots__ = ("name", "t", "w", "r", "sem", "dcount", "root")

    def __init__(self, name, t):
        self.name = name
        self.t = t
        self.w = {}
        self.r = {}
        self.sem = None
        self.dcount = 0
        self.root = self


class Op:
    __slots__ = ("eng", "fn", "deps", "dma", "tok", "sig", "idx", "big")


class Prog:
    ENG = ("pe", "act", "dve", "pool", "sp")

    def __init__(self, nc, es):
        self.nc = nc
        self.es = es
        self.ops = []
        self.by_eng = {e: [] for e in self.ENG}
        self.dma_bufs = []

    def op(self, eng, fn, reads=(), writes=(), dma_owner=None, big=False):
        o = Op()
        o.big = big
        reads = [b.root for b in reads]
        writes = [b.root for b in writes]
        if dma_owner is not None:
            dma_owner = dma_owner.root
        o.eng = eng
        o.fn = fn
        o.idx = len(self.ops)
        o.dma = dma_owner
        o.sig = dma_owner is not None
        deps = set()
        for b in reads:
            deps.update(b.w.values())
        for b in writes:
            deps.update(b.w.values())
            deps.update(b.r.values())
        o.deps = deps
        if dma_owner is not None:
            if dma_owner.sem is None:
                dma_owner.sem = self.es.enter_context(self.nc.semaphore("d_" + dma_owner.name))
                self.dma_bufs.append(dma_owner)
            dma_owner.dcount += 1
            o.tok = (dma_owner.sem, 16 * dma_owner.dcount)
            key = ("dma", id(dma_owner))
        else:
            o.tok = None
            key = eng
        for b in reads:
            b.r[key] = o.idx
        for b in writes:
            b.w = {key: o.idx}
            b.r = {}
        self.ops.append(o)
        self.by_eng[eng].append(o)
        return o

    @staticmethod
    def same_ok(p, o):
        if o.dma is not None or p.eng != o.eng:
            return False
        return p.eng == "pe" or (p.big and o.big)

    def emit(self):
        nc = self.nc
        ops = self.ops
        for o in ops:
            for d in o.deps:
                p = ops[d]
                if p.dma is None and not self.same_ok(p, o):
                    p.sig = True
        esem = {e: self.es.enter_context(nc.semaphore("c_" + e)) for e in ("pe", "act", "dve", "pool")}
        for e in ("pe", "act", "dve", "pool"):
            cnt = 0
            for o in self.by_eng[e]:
                if o.dma is None and o.sig:
                    cnt += 1
                    o.tok = (esem[e], cnt)
        block = self.es.enter_context(nc.Block())
        final = [(b.sem, 16 * b.dcount) for b in self.dma_bufs]

        def run(engname, engobj):
            waited = {}
            for o in self.by_eng[engname]:
                for d in sorted(o.deps):
                    p = ops[d]
                    if p.dma is None and self.same_ok(p, o):
                        continue
                    sem, val = p.tok
                    k = id(sem)
                    if waited.get(k, 0) >= val:
                        continue
                    waited[k] = val
                    engobj.wait_ge(sem, val)
                ins = o.fn(engobj)
                if o.sig:
                    ins.then_inc(o.tok[0], 16 if o.dma is not None else 1)
            if engname == "sp":
                for sem, val in final:
                    engobj.wait_ge(sem, val)

        block.tensor(lambda e: run("pe", e))
        block.scalar(lambda e: run("act", e))
        block.vector(lambda e: run("dve", e))
        block.gpsimd(lambda e: run("pool", e))
        block.sync(lambda e: run("sp", e))


def build(NP, DEPTH):
    NTOK = NP + 4 * NSEQ
    nc = bass.Bass("TRN2", target_bir_lowering=False)
    es = ExitStack()
    P = Prog(nc, es)

    def din(name, shape, dt=F32):
        return nc.dram_tensor(name, list(shape), dt, kind="ExternalInput").ap()

    def dout(name, shape):
        return nc.dram_tensor(name, list(shape), F32, kind="ExternalOutput").ap()

    xT_d = din("xT", [D, NTOK])
    peT_d = din("peT", [DEPTH, 256, NTOK])
    spT_d = din("spT", [DEPTH, D, NSEQ, 15])
    scT_d = din("scT", [DEPTH, 4096, NSEQ, 3])
    Cs_d = din("Cs", [DEPTH, NSEQ, 4, 512, 512])
    nT_d = din("nT", [DEPTH, 128, 4, 4, NSEQ])
    mT_d = din("mT", [DEPTH, 4, NSEQ])
    w_in_d = din("w_in", [DEPTH, D, NIN])
    pool_w_d = din("pool_w", [DEPTH, 4, 256, 256])
    wpd_d = din("w_pool_down", [DEPTH, D, D])
    wmd_d = din("w_mlstm_down", [DEPTH, 2048, D])
    wout_d = din("w_out", [DEPTH, D, D])
    wple_d = din("w_ple", [DEPTH, 256, D])
    wpg_d = din("w_ple_gate", [DEPTH, D, D])
    prm_d = din("prm", [DEPTH, 128, NPRM])
    fnw_d = din("fnw", [128, 8])
    bif_d = din("bif", [DEPTH, 4, 2])
    nwB_d = din("nwB", [DEPTH, 128, 2048])
    c_ident = din("c_ident", [128, 128])
    c_maskP = din("c_maskP", [128, 128])
    c_maskS = din("c_maskS", [64, 64])
    c_sel = din("c_sel", [4, 512])
    c_invc = din("c_invc", [128, 4, 16])
    c_seg = din("c_seg", [4, 128])
    c_zero = din("c_zero", [128, 544])
    c_ones = din("c_ones", [128, 128])
    c_eps = din("c_eps", [128, 1])
    c_one1 = din("c_one1", [128, 1])

    yT_d = dout("yT", [D, NTOK])
    npp_d = dout("npp", [DEPTH, D, 15])
    nps_d = dout("nps", [DEPTH, D, NSEQ, 15])
    ncp_d = dout("ncp", [DEPTH, 4096, 3])
    ncs_d = dout("ncs", [DEPTH, 4096, NSEQ, 3])
    nCp_d = dout("nCp", [DEPTH, 4, 512, 512])
    nCs_d = dout("nCs", [DEPTH, NSEQ, 4, 512, 512])
    nnp_d = dout("nnp", [DEPTH, 128, 4, 4, 1])
    nns_d = dout("nns", [DEPTH, 128, 4, 4, NSEQ])
    nmp_d = dout("nmp", [DEPTH, 4, 1])
    nms_d = dout("nms", [DEPTH, 4, NSEQ])
    xs_t = nc.dram_tensor("xs", [D, NTOK], F32, kind="Internal").ap()
    xs_buf = Buf("xs", None)
    wscr_d = nc.dram_tensor("wscr", [NBLK, 128, 2048], BF16, kind="Internal").ap()

    def sb(name, shape, dt=F32):
        t = es.enter_context(nc.sbuf_tensor("s_" + name, list(shape), dt))
        return Buf(name, t)

    def ps(name, shape, dt=F32):
        t = es.enter_context(nc.psum_tensor("q_" + name, list(shape), dt))
        return Buf(name, t)

    RG = [sb("rgA", [128, 8, TT]), sb("rgB", [128, 8, TT])]
    XV, HV = [], []
    for i_, rg_ in enumerate(RG):
        xv_ = Buf("xv%d" % i_, rg_.t[:, :, :]); xv_.root = rg_
        hv_ = Buf("hv%d" % i_, rg_.t[:, :, :].bitcast(BF16).rearrange("p k (a n) -> p (k a) n", a=2)); hv_.root = rg_
        XV.append(xv_); HV.append(hv_)
    tile_no = [0]
    prefetched = [False]
    hT = sb("hT", [128, 8, TT], BF16)
    scrA = sb("scrA", [128, 8, TT], BF16)
    gaT = sb("gaT", [128, 8, TT], BF16)
    Fb = [sb(f"F{i}", [128, 528]) for i in range(6)]
    pm = sb("pm", [128, 2, TT], BF16)
    qT = sb("qT", [128, 4, TT], BF16)
    kT = sb("kT", [128, 4, TT], BF16)
    qg = sb("qg", [128, 4, TT], BF16)
    vtok = sb("vtok", [128, 4, 512], BF16)
    sigo = sb("sigo", [128, 4, 512], BF16)
    nwsz = sb("nwsz", [128, 4, 512], BF16)
    g_ig = sb("g_ig", [4, TT])
    g_ls = sb("g_ls", [4, TT])
    g_m = sb("g_m", [4, TT])
    g_B = sb("g_B", [4, TT])
    g_R = sb("g_R", [4, TT])
    g_G = sb("g_G", [4, TT])
    g_small = sb("g_small", [4, 64])
    Atok = sb("Atok", [128, 4, 4])
    emtt = sb("emtt", [128, 4, 4])
    RBs = sb("RBs", [128, TT])
    gBe = sb("gBe", [128, TT])
    tmpD = [sb(f"tmpD{i}", [128, 128]) for i in range(2)]
    SDT = [sb(f"SDT{i}", [128, 128], BF16) for i in range(2)]
    WSBb = [sb(f"WSBb{i}", [128, 16], BF16) for i in range(2)]
    ktok = [sb(f"ktok{i}", [128, 512], BF16) for i in range(2)]
    vw = [sb(f"vw{i}", [128, 512], BF16) for i in range(2)]
    hc1 = sb("hc1", [128, 512])
    hc2 = [sb(f"hc2{i}", [128, 512], BF16) for i in range(2)]
    junk = sb("junk", [128, 512], BF16)
    szb = [sb(f"sz{i}", [128, 512], BF16) for i in range(2)]
    fix15 = sb("fix15", [128, 16])
    onesf = sb("onesf", [128, 1])
    epsf = sb("epsf", [128, 1])
    tiny = [sb(f"tiny{i}", [128, 8]) for i in range(2)]
    Cb = [sb(f"C{i}", [128, 4, 512]) for i in range(4)]
    Cbf = [sb(f"Cbf{i}", [128, 4, 512], BF16) for i in range(2)]
    Z = sb("Z", [128, 4, 1088], BF16)
    NSL = 5
    wslot = [sb(f"w{i}", [128, 2048], BF16) for i in range(NSL)]
    ones_bf = sb("ones_bf", [128, 128], BF16)
    maskP = sb("maskP", [128, 128])
    maskS = sb("maskS", [64, 64])
    sel = sb("sel", [4, 512])
    identb = sb("identb", [128, 128], BF16)
    identf = sb("identf", [4, 4])
    invc = sb("invc", [128, 4, 16])
    segc = sb("segc", [4, 128])
    zt = sb("zt", [128, 544])
    prm = sb("prm", [128, NPRM])
    fnw = sb("fnw", [128, 8])
    bif = sb("bif", [4, 2])
    nwh = sb("nwh", [128, 512])
    peT = sb("peT", [128, 2, TT], BF16)
    rstd = sb("rstd", [128, TT])
    phist = sb("phist", [128, 8, 15])
    chist = sb("chist", [128, 32, 3])
    nP = sb("nP", [128, 4, 4, 1])
    nS = sb("nS", [128, 4, 4, NSEQ])
    nbf = sb("nbf", [128, 4, 4, NSEQ], BF16)
    ntmp = sb("ntmp", [128, 4, NSEQ])

    NROT = 5
    p_tr = ps("p_tr", [128, 512], BF16)
    p_sm = ps("p_sm", [128, 128])
    p_nums = [ps(f"p_num{i}", [128, 512]) for i in range(1)]
    prot = [ps(f"prot{i}", [128, 512]) for i in range(NROT)]
    _unused = [128, 128]
    p_sm_at = Buf("p_sm_at", p_sm.t)
    p_sm_den = Buf("p_sm_den", p_sm.t)
    p_sm_n = Buf("p_sm_n", p_sm.t)

    rot_i = [0]

    def PR():
        b = prot[rot_i[0] % NROT]
        rot_i[0] += 1
        return b

    cnt = {"pn": 0, "F": 0, "w": 0, "d": 0, "k": 0, "v": 0, "h2": 0, "C": 0, "cbf": 0, "tiny": 0}

    def nxt(key, lst):
        b = lst[cnt[key] % len(lst)]
        cnt[key] += 1
        return b

    def is_big(ap):
        n = 1
        for d in list(ap.shape)[1:]:
            n *= int(d)
        return n >= BIGN

    def MM(out_b, out_ap, lb, l_ap, rb, r_ap, start, stop):
        P.op("pe", lambda e: e.matmul(out_ap, l_ap, r_ap, start=start, stop=stop),
             reads=list(lb) + list(rb), writes=[out_b])

    def TR(out_b, out_ap, ib, in_ap, idb, id_ap):
        P.op("pe", lambda e: e.transpose(out_ap, in_ap, id_ap), reads=[ib, idb], writes=[out_b])

    def ACT(out_b, out_ap, in_bs, in_ap, func, bias=None, scale=None, accum=None, extra_w=()):
        kw = {}
        if bias is not None:
            kw["bias"] = bias
        if scale is not None:
            kw["scale"] = scale
        if accum is not None:
            kw["accum_out"] = accum
        P.op("act", lambda e: e.activation(out_ap, in_ap, func, **kw), reads=list(in_bs),
             writes=[out_b] + list(extra_w), big=is_big(out_ap) and accum is None)

    def V(eng, method, args, reads, writes, kw=None):
        kw = kw or {}
        P.op(eng, lambda e: getattr(e, method)(*args, **kw), reads=reads, writes=writes,
             big=is_big(args[0]) and method != "tensor_tensor_scan")

    def DMA(eng, out_ap, in_ap, reads, writes, owner):
        P.op(eng, lambda e: e.dma_start(out=out_ap, in_=in_ap), reads=reads, writes=writes, dma_owner=owner)

    dbg_names = []

    def DBG(name, buf, ap, shape, bf=False):
        if not DEBUG or name in dbg_names:
            return
        dbg_names.append(name)
        d = nc.dram_tensor("dbg_" + name, list(shape), F32, kind="ExternalOutput").ap()
        DMA("pool" if bf else "sp", d, ap, [buf], [], buf)

    wkeys = {}
    wbufs = []
    wmode = ["cast"]

    def wload(src_ap, kc, n, key):
        s_ = nxt("w", wslot)
        view = s_.t[:, 0:kc * n].rearrange("p (k n) -> p k n", k=kc)
        if key not in wkeys:
            wkeys[key] = len(wkeys)
            wbufs.append(Buf("wscr%d" % wkeys[key], None))
        bi = wkeys[key]
        assert bi < NBLK
        if wmode[0] == "cast":
            DMA("pool", view, src_ap, [], [s_], s_)
            if USE_SCR:
                DMA("sp", wscr_d[bi, :, 0:kc * n], s_.t[:, 0:kc * n], [s_], [wbufs[bi]], s_)
        else:
            DMA("pool", s_.t[:, 0:kc * n], wscr_d[bi, :, 0:kc * n], [wbufs[bi]], [s_], s_)
        return s_, view

    def win_blk(l, col0, n):
        return wload(w_in_d[l, :, col0:col0 + n].rearrange("(k p) n -> p k n", p=128), 8, n, ("win", col0))

    def sq_blk(wd, l, j0, n, kc):
        return wload(wd[l, :, j0:j0 + n].rearrange("(k p) n -> p k n", p=128), kc, n, (str(wd.tensor.name) if hasattr(wd, "tensor") else id(wd), j0))

    DMA("sp", identf.t[:], c_ident[0:4, 0:4], [], [identf], identf)
    DMA("pool", identb.t[:], c_ident, [], [identb], identb)
    DMA("sp", maskP.t[:], c_maskP, [], [maskP], maskP)
    DMA("sp", maskS.t[:], c_maskS, [], [maskS], maskS)
    DMA("sp", sel.t[:], c_sel, [], [sel], sel)
    DMA("sp", invc.t[:], c_invc, [], [invc], invc)
    DMA("sp", segc.t[:], c_seg, [], [segc], segc)
    DMA("sp", fnw.t[:], fnw_d, [], [fnw], fnw)
    DMA("sp", zt.t[:], c_zero, [], [zt], zt)
    DMA("pool", ones_bf.t[:], c_ones, [], [ones_bf], ones_bf)
    DMA("sp", onesf.t[:], c_one1, [], [onesf], onesf)
    DMA("sp", epsf.t[:], c_eps, [], [epsf], epsf)
    V("dve", "tensor_copy", (Z.t[:, :, :].rearrange("p c (a u) -> p (c a) u", a=2), zt.t[:, :].unsqueeze(1).to_broadcast([128, 8, 544])), [zt], [Z])
    V("dve", "tensor_copy", (nbf.t[:], zt.t[:, 0:256].rearrange("p (a b c) -> p a b c", a=4, b=4)), [zt], [nbf])

    def rmsnorm_to(xt, dst, NT, wcol0, wbuf):
        ACT(scrA, scrA.t[:, :, 0:NT], [xt], xt.t[:, :, 0:NT], AF.Square)
        pb = PR()
        for kc in range(8):
            MM(pb, pb.t[:, 0:NT], [ones_bf], ones_bf.t[:, :], [scrA], scrA.t[:, kc, 0:NT], kc == 0, kc == 7)
        ACT(rstd, rstd.t[:, 0:NT], [pb, epsf], pb.t[:, 0:NT], AF.Ln, bias=epsf.t[:, 0:1], scale=1.0 / D)
        ACT(rstd, rstd.t[:, 0:NT], [rstd], rstd.t[:, 0:NT], AF.Exp, scale=-0.5)
        for kc in range(8):
            V("dve", "scalar_tensor_tensor",
              (dst.t[:, kc, 0:NT], xt.t[:, kc, 0:NT], wbuf.t[:, wcol0 + kc:wcol0 + kc + 1], rstd.t[:, 0:NT],
               ALU.mult, ALU.mult), [xt, wbuf, rstd], [dst])

    def proj_fm(wb, wv, j, NT, src):
        pb = PR()
        kcn = wv.shape[1]
        for kc in range(kcn):
            MM(pb, pb.t[:, 0:NT], [wb], wv[:, kc, j * 128:(j + 1) * 128], [src], src.t[:, kc, 0:NT],
               kc == 0, kc == kcn - 1)
        return pb

    C_NW, C_PNW, C_PSC, C_CB, C_CW = 0, 8, 16, 24, 56

    for l in range(DEPTH):
        last_layer = l == DEPTH - 1
        DMA("sp", prm.t[:], prm_d[l], [], [prm], prm)
        DMA("sp", bif.t[:], bif_d[l], [], [bif], bif)
        V("dve", "tensor_copy", (g_small.t[:, 0:1], zt.t[0:4, 0:1]), [zt], [g_small])
        V("dve", "tensor_scalar", (g_small.t[:, 1:2], bif.t[:, 1:2], -1.0, None, ALU.mult), [bif], [g_small])
        V("dve", "tensor_copy", (nP.t[:], zt.t[:, 0:16].rearrange("p (a b c) -> p a b c", a=4, b=4)), [zt], [nP])
        DMA("sp", nS.t[:], nT_d[l], [], [nS], nS)
        DMA("sp", g_small.t[:, 8:24], mT_d[l], [], [g_small], g_small)
        V("dve", "tensor_copy", (phist.t[:], zt.t[:, 0:120].rearrange("p (a b) -> p a b", a=8)), [zt], [phist])
        V("dve", "tensor_copy", (chist.t[:], zt.t[:, 0:96].rearrange("p (a b) -> p a b", a=32)), [zt], [chist])

        ntiles = NP // TT
        for ti in range(ntiles + 1):
            samp = ti == ntiles
            NT = 4 * NSEQ if samp else TT
            c0 = ti * TT
            nseg = NSEQ if samp else 1
            Ls = 4 if samp else TT
            L = 64 if samp else 128
            nch = NT // L
            nseq = NSEQ if samp else 1
            ls_ = L // nseq
            first_tile = ti == 0
            last_prompt = ti == ntiles - 1
            mask = maskS if samp else maskP
            xsrc = xT_d if l == 0 else xs_t
            wmode[0] = "cast" if (ti == 0 or not USE_SCR) else "scr"

            xt = XV[tile_no[0] % 2]
            hcT = HV[1 - tile_no[0] % 2]
            if not prefetched[0]:
                DMA("sp", xt.t[:, :, 0:NT], xsrc[:, c0:c0 + NT].rearrange("(k p) n -> p k n", p=128),
                    [xs_buf], [xt], xt)
            prefetched[0] = False
            DMA("pool", peT.t[:, :, 0:NT], peT_d[l, :, c0:c0 + NT].rearrange("(k p) n -> p k n", p=128),
                [], [peT], peT)

            rmsnorm_to(xt, hT, NT, C_NW, prm)

            def hview(fb, H):
                return fb.t[:, 0:nseg * (H + Ls)].rearrange("p (s w) -> p s w", s=nseg)

            def p3(pb):
                return pb.t[:, 0:NT].rearrange("p (s w) -> p s w", s=nseg)

            def step2_gen():
                for g in range(4):
                    win = 2 << g
                    wb_u, wv_u = win_blk(l, O_PIN + g * 256, 256)
                    wb_z, wv_z = win_blk(l, O_PZ + g * 256, 256)
                    wb_p, wv_p = wload(pool_w_d[l, g].rearrange("(k p) n -> p k n", p=128), 2, 256, ("pw", g))
                    szs = []
                    for j in range(2):
                        c = 2 * g + j
                        pb = proj_fm(wb_u, wv_u, j, NT, hT)
                        if samp:
                            hs = nxt("F", Fb)
                        U = nxt("F", Fb)
                        Uv = hview(U, 15)
                        if samp:
                            DMA("sp", hs.t[:, 0:NSEQ * 15], spT_d[l, c * 128:(c + 1) * 128].rearrange("p b t -> p (b t)"),
                                [], [hs], hs)
                            V("dve", "tensor_copy", (Uv[:, :, 0:15], hs.t[:, 0:NSEQ * 15].rearrange("p (b t) -> p b t", t=15)),
                              [hs], [U])
                        elif first_tile:
                            V("dve", "tensor_copy", (Uv[:, 0, 0:15], zt.t[:, 0:15]), [zt], [U])
                        else:
                            V("dve", "tensor_copy", (Uv[:, 0, 0:15], phist.t[:, c, :]), [phist], [U])
                        ACT(U, Uv[:, :, 15:15 + Ls], [pb], p3(pb), AF.Copy)
                        if samp:
                            ho = nxt("F", Fb)
                            V("dve", "tensor_copy", (ho.t[:, 0:NSEQ * 15].rearrange("p (b t) -> p b t", t=15), Uv[:, :, 4:19]),
                              [U], [ho])
                            DMA("sp", nps_d[l, c * 128:(c + 1) * 128].rearrange("p b t -> p (b t)"), ho.t[:, 0:NSEQ * 15],
                                [ho], [], ho)
                        else:
                            V("dve", "tensor_copy", (phist.t[:, c, :], Uv[:, 0, Ls:Ls + 15]), [U], [phist])
                        cur = U
                        sh = 1
                        lo = 0
                        for _ in range(g + 1):
                            nb = nxt("F", Fb)
                            cv = hview(cur, 15)
                            nv = hview(nb, 15)
                            lo += sh
                            V("dve", "tensor_tensor", (nv[:, :, lo:15 + Ls], cv[:, :, lo:15 + Ls],
                                                       cv[:, :, lo - sh:15 + Ls - sh], ALU.add), [cur], [nb])
                            cur = nb
                            sh *= 2
                        sv = hview(cur, 15)
                        pmv = pm.t[:, j, 0:NT].rearrange("p (s w) -> p s w", s=nseg)
                        V("dve", "scalar_tensor_tensor", (pmv, sv[:, :, 15:15 + Ls], 1.0 / win, Uv[:, :, 15:15 + Ls],
                                                          ALU.mult, ALU.subtract), [cur, U], [pm])
                        if first_tile and not samp:
                            V("dve", "tensor_tensor", (fix15.t[:, 0:15], sv[:, 0, 15:30], invc.t[:, g, 0:15], ALU.mult),
                              [cur, invc], [fix15])
                            V("dve", "tensor_tensor", (pm.t[:, j, 0:15], fix15.t[:, 0:15], Uv[:, 0, 15:30], ALU.subtract),
                              [fix15, U], [pm])
                        pz = proj_fm(wb_z, wv_z, j, NT, hT)
                        sz = szb[j]
                        ACT(sz, sz.t[:, 0:NT], [pz], pz.t[:, 0:NT], AF.Silu)
                        szs.append(sz)
                        yield
                    for j in range(2):
                        c = 2 * g + j
                        pb = PR()
                        for ci in range(2):
                            MM(pb, pb.t[:, 0:NT], [wb_p], wv_p[:, ci, j * 128:(j + 1) * 128], [pm], pm.t[:, ci, 0:NT],
                               ci == 0, ci == 1)
                        V("dve", "scalar_tensor_tensor",
                          (scrA.t[:, c, 0:NT], pb.t[:, 0:NT], prm.t[:, C_PSC + c:C_PSC + c + 1], szs[j].t[:, 0:NT],
                           ALU.mult, ALU.mult), [pb, prm, szs[j]], [scrA])
                        yield
                for jb in range(4):
                    wb_d, wv_d = sq_blk(wpd_d, l, jb * 256, 256, 8)
                    wb_g, wv_g = win_blk(l, O_GA + jb * 256, 256)
                    for jj in range(2):
                        j = jb * 2 + jj
                        pa = proj_fm(wb_d, wv_d, jj, NT, scrA)
                        pg = proj_fm(wb_g, wv_g, jj, NT, hT)
                        sg = nxt("F", Fb)
                        ACT(sg, sg.t[:, 0:NT], [pg], pg.t[:, 0:NT], AF.Sigmoid)
                        V("dve", "tensor_tensor", (gaT.t[:, j, 0:NT], pa.t[:, 0:NT], sg.t[:, 0:NT], ALU.mult),
                          [pa, sg], [gaT])
                        yield

            s2 = step2_gen()
            if not INTERLEAVE:
                for _ in s2:
                    pass

            def pump(n):
                for _ in range(n):
                    try:
                        next(s2)
                    except StopIteration:
                        return

            wb_if, wv_if = win_blk(l, O_IF, 8)
            pi = PR()
            pf = PR()
            for kc in range(8):
                MM(pi, pi.t[0:4, 0:NT], [wb_if], wv_if[:, kc, 0:4], [hT], hT.t[:, kc, 0:NT], kc == 0, kc == 7)
            for kc in range(8):
                MM(pf, pf.t[0:4, 0:NT], [wb_if], wv_if[:, kc, 4:8], [hT], hT.t[:, kc, 0:NT], kc == 0, kc == 7)
            ACT(g_ig, g_ig.t[:, 0:NT], [pi, bif], pi.t[0:4, 0:NT], AF.Identity, bias=bif.t[:, 0:1])
            ACT(g_ls, g_ls.t[:, 0:NT], [pf, g_small], pf.t[0:4, 0:NT], AF.Exp, bias=g_small.t[:, 1:2], scale=-1.0)
            ACT(g_ls, g_ls.t[:, 0:NT], [g_ls, onesf], g_ls.t[:, 0:NT], AF.Ln, bias=onesf.t[0:4, 0:1])
            V("dve", "tensor_scalar", (g_ls.t[:, 0:NT], g_ls.t[:, 0:NT], -1.0, None, ALU.mult), [g_ls], [g_ls])
            V("dve", "tensor_tensor_scan", (g_B.t[:, 0:TT], g_ls.t[:, 0:TT], zt.t[0:4, 0:TT], 0.0, ALU.add, ALU.add),
              [g_ls, zt], [g_B])
            if samp:
                ls3 = g_ls.t[:, 0:NT].rearrange("p (b t) -> p b t", t=4)
                ig3 = g_ig.t[:, 0:NT].rearrange("p (b t) -> p b t", t=4)
                R3 = g_R.t[:, 0:NT].rearrange("p (b t) -> p b t", t=4)
                G3 = g_G.t[:, 0:NT].rearrange("p (b t) -> p b t", t=4)
                V("dve", "tensor_copy", (g_R.t[:, 0:NT], g_ig.t[:, 0:NT]), [g_ig], [g_R])
                V("dve", "tensor_tensor", (g_small.t[:, 24:40], g_small.t[:, 8:24], ls3[:, :, 0], ALU.add),
                  [g_small, g_ls], [g_small])
                V("dve", "tensor_tensor", (R3[:, :, 0], g_small.t[:, 24:40], ig3[:, :, 0], ALU.max),
                  [g_small, g_ig], [g_R])
                V("dve", "tensor_tensor", (g_G.t[:, 0:NT], g_ls.t[:, 0:NT], segc.t[:, 0:64], ALU.mult), [g_ls, segc], [g_G])
                V("dve", "tensor_tensor", (g_G.t[:, 0:NT], g_G.t[:, 0:NT], segc.t[:, 64:128], ALU.add), [g_G, segc], [g_G])
                V("dve", "tensor_tensor_scan", (g_m.t[:, 0:NT], g_G.t[:, 0:NT], g_R.t[:, 0:NT], 0.0, ALU.add, ALU.max),
                  [g_G, g_R], [g_m])
            else:
                V("dve", "tensor_tensor_scan", (g_m.t[:, 0:NT], g_ls.t[:, 0:NT], g_ig.t[:, 0:NT], g_small.t[:, 0:1],
                                                ALU.add, ALU.max), [g_ls, g_ig, g_small], [g_m])
            V("dve", "tensor_tensor", (g_R.t[:, 0:NT], g_B.t[:, 0:NT], g_m.t[:, 0:NT], ALU.subtract), [g_B, g_m], [g_R])
            V("dve", "tensor_tensor", (g_ig.t[:, 0:NT], g_ig.t[:, 0:NT], g_B.t[:, 0:NT], ALU.subtract), [g_ig, g_B], [g_ig])
            if samp:
                B3 = g_B.t[:, 0:NT].rearrange("p (b t) -> p b t", t=4)
                R3 = g_R.t[:, 0:NT].rearrange("p (b t) -> p b t", t=4)
                G3 = g_G.t[:, 0:NT].rearrange("p (b t) -> p b t", t=4)
                V("dve", "tensor_copy", (g_small.t[:, 24:25], g_small.t[:, 8:9]), [g_small], [g_small])
                V("dve", "tensor_tensor", (g_small.t[:, 25:40], g_small.t[:, 9:24], B3[:, 0:15, 3], ALU.subtract),
                  [g_small, g_B], [g_small])
                V("dve", "tensor_tensor", (G3, R3, g_small.t[:, 24:40].unsqueeze(2).to_broadcast([4, NSEQ, 4]), ALU.add),
                  [g_R, g_small], [g_G])
                m3 = g_m.t[:, 0:NT].rearrange("p (b t) -> p b t", t=4)
                V("dve", "tensor_copy", (g_small.t[:, 40:56], m3[:, :, 3]), [g_m], [g_small])
                DMA("sp", nms_d[l], g_small.t[:, 40:56], [g_small], [], g_small)
            else:
                for c in range(nch):
                    cs = slice(c * L, (c + 1) * L)
                    if c == 0:
                        V("dve", "tensor_scalar", (g_G.t[:, cs], g_R.t[:, cs], g_small.t[:, 0:1], None, ALU.add),
                          [g_R, g_small], [g_G])
                    else:
                        V("dve", "tensor_scalar", (g_G.t[:, cs], g_R.t[:, cs], g_R.t[:, c * L - 1:c * L], None, ALU.subtract),
                          [g_R], [g_G])
                V("dve", "tensor_copy", (g_small.t[:, 0:1], g_m.t[:, NT - 1:NT]), [g_m], [g_small])
                if last_prompt:
                    DMA("sp", nmp_d[l], g_small.t[:, 0:1], [g_small], [], g_small)
            ACT(g_m, g_m.t[:, 0:NT], [g_m], g_m.t[:, 0:NT], AF.Exp, scale=-1.0)
            for c in range(nch):
                cs = slice(c * L, (c + 1) * L)
                TR(p_sm_at, p_sm.t[0:L, c * 4:c * 4 + 4], g_ig, g_ig.t[0:4, cs], identf, identf.t[0:4, 0:4])
                TR(p_sm_at, p_sm.t[0:L, 16 + c * 4:16 + c * 4 + 4], g_m, g_m.t[0:4, cs], identf, identf.t[0:4, 0:4])
            V("dve", "tensor_scalar", (Atok.t[0:L, 0:nch, :], p_sm.t[0:L, 0:nch * 4].rearrange("p (c h) -> p c h", h=4),
                                       LNSCALE, None, ALU.add), [p_sm_at], [Atok])
            V("dve", "tensor_copy", (emtt.t[0:L, 0:nch, :], p_sm.t[0:L, 16:16 + nch * 4].rearrange("p (c h) -> p c h", h=4)),
              [p_sm_at], [emtt])

            if samp:
                V("dve", "tensor_copy", (nbf.t[:], nS.t[:]), [nS], [nbf])
            for h in range(4):
                for qk in range(2):
                    dstb = qT if qk == 0 else kT
                    for half in range(2):
                        wb, wv = win_blk(l, O_QK + qk * 2048 + h * 512 + half * 256, 256)
                        for jj in range(2):
                            dc = half * 2 + jj
                            cch = qk * 16 + h * 4 + dc
                            pb = proj_fm(wb, wv, jj, NT, hT)
                            cb = nxt("F", Fb)
                            cv = hview(cb, 3)
                            if samp:
                                hs = nxt("F", Fb)
                                DMA("sp", hs.t[:, 0:NSEQ * 3], scT_d[l, cch * 128:(cch + 1) * 128].rearrange("p b t -> p (b t)"),
                                    [], [hs], hs)
                                V("dve", "tensor_copy", (cv[:, :, 0:3], hs.t[:, 0:NSEQ * 3].rearrange("p (b t) -> p b t", t=3)),
                                  [hs], [cb])
                            elif first_tile:
                                V("dve", "tensor_copy", (cv[:, 0, 0:3], zt.t[:, 0:3]), [zt], [cb])
                            else:
                                V("dve", "tensor_copy", (cv[:, 0, 0:3], chist.t[:, cch, :]), [chist], [cb])
                            ACT(cb, cv[:, :, 3:3 + Ls], [pb], p3(pb), AF.Copy)
                            if samp:
                                ho = nxt("F", Fb)
                                V("dve", "tensor_copy", (ho.t[:, 0:NSEQ * 3].rearrange("p (b t) -> p b t", t=3), cv[:, :, 4:7]),
                                  [cb], [ho])
                                DMA("sp", ncs_d[l, cch * 128:(cch + 1) * 128].rearrange("p b t -> p (b t)"), ho.t[:, 0:NSEQ * 3],
                                    [ho], [], ho)
                            else:
                                V("dve", "tensor_copy", (chist.t[:, cch, :], cv[:, 0, Ls:Ls + 3]), [cb], [chist])
                            acc = nxt("F", Fb)
                            av = acc.t[:, 0:NT].rearrange("p (s w) -> p s w", s=nseg)
                            wc = C_CW + cch * 4
                            V("dve", "tensor_scalar", (av, cv[:, :, 0:Ls], prm.t[:, wc:wc + 1],
                                                       prm.t[:, C_CB + cch:C_CB + cch + 1], ALU.mult, ALU.add),
                              [cb, prm], [acc])
                            for tap in range(1, 4):
                                V("dve", "scalar_tensor_tensor", (av, cv[:, :, tap:tap + Ls], prm.t[:, wc + tap:wc + tap + 1],
                                                                  av, ALU.mult, ALU.add), [cb, prm, acc], [acc])
                            ACT(dstb, dstb.t[:, dc, 0:NT], [acc], acc.t[:, 0:NT], AF.Silu)
                DMA("sp", nwh.t[:], nwB_d[l, :, h * 512:(h + 1) * 512], [], [nwh], nwh)
                for which, off in ((0, O_V), (2, O_MZ), (1, O_O)):
                    blks = [win_blk(l, off + h * 512 + half * 256, 256) for half in range(2)]
                    for c in range(nch):
                        pb = PR()
                        for half in range(2):
                            wb, wv = blks[half]
                            for kc in range(8):
                                MM(pb, pb.t[0:L, half * 256:(half + 1) * 256], [hT], hT.t[:, kc, c * L:(c + 1) * L],
                                   [wb], wv[:, kc, :], kc == 0, kc == 7)
                        if which == 0:
                            ACT(vtok, vtok.t[0:L, c, :], [pb], pb.t[0:L, :], AF.Copy)
                        elif which == 1:
                            ACT(sigo, sigo.t[0:L, c, :], [pb], pb.t[0:L, :], AF.Sigmoid)
                        else:
                            ztmp = nxt("F", Fb)
                            ACT(ztmp, ztmp.t[0:L, 0:512], [pb], pb.t[0:L, :], AF.Silu)
                            V("dve", "tensor_tensor", (nwsz.t[0:L, c, :], ztmp.t[0:L, 0:512], nwh.t[0:L, :], ALU.mult),
                              [ztmp, nwh], [nwsz])
                prb = PR()
                MM(prb, prb.t[:, 0:NT], [sel], sel.t[:, h * 128:(h + 1) * 128], [g_R], g_R.t[:, 0:NT], True, True)
                pgb = PR()
                MM(pgb, pgb.t[:, 0:NT], [sel], sel.t[:, h * 128:(h + 1) * 128], [g_G], g_G.t[:, 0:NT], True, True)
                ACT(RBs, RBs.t[:, 0:NT], [prb], prb.t[:, 0:NT], AF.Copy)
                ACT(gBe, gBe.t[:, 0:NT], [pgb], pgb.t[:, 0:NT], AF.Exp)
                V("dve", "tensor_tensor", (qg.t[:, :, 0:NT], qT.t[:, :, 0:NT],
                                           gBe.t[:, 0:NT].unsqueeze(1).to_broadcast([128, 4, NT]), ALU.mult),
                  [qT, gBe], [qg])
                if samp:
                    zd = Z.t[:, :, :].rearrange("p c (b u) -> p c b u", u=68)[:, :, :, 0:4]
                    V("dve", "tensor_copy", (zd, qg.t[:, :, 0:NT].rearrange("p c (b t) -> p c b t", t=4)), [qg], [Z])
                    nst = nS
                else:
                    nst = nP

                def stage_B(c):
                    cs = slice(c * L, (c + 1) * L)
                    td = nxt("d", tmpD)
                    sdt = SDT[(cnt["d"] - 1) % 2]
                    wsb = WSBb[(cnt["d"] - 1) % 2]
                    V("dve", "tensor_tensor", (td.t[0:L, 0:L], RBs.t[0:L, cs], mask.t[0:L, 0:L], ALU.add), [RBs, mask], [td])
                    ACT(td, td.t[0:L, 0:L], [td, Atok], td.t[0:L, 0:L], AF.Exp, bias=Atok.t[0:L, c, h:h + 1])
                    lastv = td.t[0:L, 0:L].rearrange("p (b t) -> p b t", t=ls_)[:, :, ls_ - 1]
                    V("dve", "tensor_copy", (wsb.t[0:L, 0:nseq], lastv), [td], [wsb])
                    p_st = PR()
                    for dc in range(4):
                        MM(p_st, p_st.t[0:L, 0:L], [kT], kT.t[:, dc, cs], [qT], qT.t[:, dc, cs], dc == 0, dc == 3)
                    V("dve", "tensor_tensor", (sdt.t[0:L, 0:L], p_st.t[0:L, 0:L], td.t[0:L, 0:L], ALU.mult), [p_st, td], [sdt])
                    for dc in range(4):
                        TR(p_tr, p_tr.t[0:L, dc * 128:(dc + 1) * 128], kT, kT.t[:, dc, cs], identb, identb.t[:, :])
                    kt = nxt("k", ktok)
                    ACT(kt, kt.t[0:L, :], [p_tr], p_tr.t[0:L, :], AF.Copy)
                    vwb = None
                    if not samp:
                        vwb = nxt("v", vw)
                        V("dve", "tensor_scalar", (vwb.t[0:L, :], vtok.t[0:L, c, :], lastv, None, ALU.mult),
                          [vtok, td], [vwb])
                    return dict(cs=cs, td=td, sdt=sdt, wsb=wsb, lastv=lastv, kt=kt, vwb=vwb)

                def stage_C1(c, X):
                    cs, dt_, sdt, wsb, lastv, kt = X["cs"], X["td"], X["sdt"], X["wsb"], X["lastv"], X["kt"]
                    state_zero = first_tile and c == 0 and not samp
                    den_ap = p_sm.t[0:L, 32:33]
                    p_num = nxt("pn", p_nums)
                    X["pnum"] = p_num
                    MM(p_num, p_num.t[0:L, :], [sdt], sdt.t[0:L, 0:L], [vtok], vtok.t[0:L, c, :], True, state_zero)
                    MM(p_sm_den, den_ap, [sdt], sdt.t[0:L, 0:L], [ones_bf], ones_bf.t[0:L, 0:1], True, state_zero)
                    decay = gBe.t[:, c * L + L - 1:c * L + L]
                    if not samp:
                        Ch = Cb[h]
                        if not state_zero:
                            if c == 0:
                                cbf = Cbf[0]
                                ACT(cbf, cbf.t[:], [Ch], Ch.t[:], AF.Copy)
                                cur_cbf[0] = 0
                            cbf = Cbf[cur_cbf[0]]
                            for dc in range(4):
                                MM(p_num, p_num.t[0:L, :], [qg], qg.t[:, dc, cs], [cbf], cbf.t[:, dc, :], False, dc == 3)
                            for dc in range(4):
                                MM(p_sm_den, den_ap, [qg], qg.t[:, dc, cs], [nbf], nbf.t[:, h, dc, 0:1], False, dc == 3)
                            nxi = 1 - cur_cbf[0]
                        else:
                            nxi = 0
                        vwb = X["vwb"]
                        newcbf = Cbf[nxi]
                        for dc in range(4):
                            pu = PR()
                            MM(pu, pu.t[:, :], [kt], kt.t[0:L, dc * 128:(dc + 1) * 128], [vwb], vwb.t[0:L, :], True, True)
                            if state_zero:
                                V("dve", "tensor_copy", (Ch.t[:, dc, :], pu.t[:, :]), [pu], [Ch])
                            else:
                                V("dve", "scalar_tensor_tensor", (Ch.t[:, dc, :], Ch.t[:, dc, :], decay, pu.t[:, :],
                                                                  ALU.mult, ALU.add), [Ch, gBe, pu], [Ch])
                            ACT(newcbf, newcbf.t[:, dc, :], [Ch], Ch.t[:, dc, :], AF.Copy)
                        cur_cbf[0] = nxi
                    else:
                        def cload(b):
                            cbuf_ = Cb[b % 4]
                            DMA("sp", cbuf_.t[:], Cs_d[l, b, h].rearrange("(k p) e -> p k e", p=128), [], [cbuf_], cbuf_)
                        cload(0)
                        cload(1)
                        for b in range(NSEQ):
                            if b + 2 < NSEQ:
                                cload(b + 2)
                            Cq = Cb[b % 4]
                            cbf = nxt("cbf", Cbf)
                            ACT(cbf, cbf.t[:], [Cq], Cq.t[:], AF.Copy)
                            for dc in range(4):
                                MM(p_num, p_num.t[0:L, :], [Z], Z.t[:, dc, b * 64:(b + 1) * 64], [cbf], cbf.t[:, dc, :],
                                   False, b == NSEQ - 1 and dc == 3)
                            for dc in range(4):
                                MM(p_sm_den, den_ap, [Z], Z.t[:, dc, b * 64:(b + 1) * 64], [nbf], nbf.t[:, h, dc, b:b + 1],
                                   False, b == NSEQ - 1 and dc == 3)
                            vwb = nxt("v", vw)
                            lv_b = dt_.t[0:L, 4 * b + 3:4 * b + 4]
                            V("dve", "tensor_scalar", (vwb.t[0:L, :], vtok.t[0:L, c, :], lv_b, None, ALU.mult),
                              [vtok, dt_], [vwb])
                            dec_b = gBe.t[:, 4 * b + 3:4 * b + 4]
                            for dc in range(4):
                                pu = PR()
                                MM(pu, pu.t[:, :], [kt], kt.t[0:L, dc * 128:(dc + 1) * 128], [vwb], vwb.t[0:L, :], True, True)
                                V("dve", "scalar_tensor_tensor", (Cq.t[:, dc, :], Cq.t[:, dc, :], dec_b, pu.t[:, :],
                                                                  ALU.mult, ALU.add), [Cq, gBe, pu], [Cq])
                            DMA("sp", nCs_d[l, b, h].rearrange("(k p) e -> p k e", p=128), Cq.t[:], [Cq], [], Cq)
                            if b % 2 == 1:
                                pump(1)
                    pn = p_sm.t[:, 64:64 + 4 * nseq].rearrange("p (k b) -> p k b", k=4)
                    for dc in range(4):
                        MM(p_sm_n, pn[:, dc, :], [kt], kt.t[0:L, dc * 128:(dc + 1) * 128], [wsb], wsb.t[0:L, 0:nseq], True, True)
                    decv = gBe.t[:, cs].rearrange("p (b t) -> p b t", t=ls_)[:, :, ls_ - 1]
                    ty = nxt("tiny", tiny)
                    X["ty"] = ty
                    V("dve", "tensor_copy", (ty.t[0:L, 0:1], den_ap), [p_sm_den], [ty])
                    V("dve", "tensor_tensor", (ntmp.t[:, :, 0:nseq], nst.t[:, h, :, :],
                                               decv.unsqueeze(1).to_broadcast([128, 4, nseq]), ALU.mult), [nst, gBe], [ntmp])
                    V("dve", "tensor_tensor", (nst.t[:, h, :, :], ntmp.t[:, :, 0:nseq], pn, ALU.add), [ntmp, p_sm_n], [nst])
                    V("dve", "tensor_copy", (nbf.t[:, h, :, 0:nseq], nst.t[:, h, :, :]), [nst], [nbf])

                def stage_E(c, X):
                    ty = X["ty"]
                    p_num = X["pnum"]
                    V("dve", "scalar_tensor_tensor", (ty.t[0:L, 1:2], ty.t[0:L, 0:1], -1.0, ty.t[0:L, 0:1], ALU.mult, ALU.max),
                      [ty], [ty])
                    V("dve", "tensor_tensor", (ty.t[0:L, 2:3], ty.t[0:L, 1:2], emtt.t[0:L, c, h:h + 1], ALU.max), [ty, emtt], [ty])
                    V("dve", "reciprocal", (ty.t[0:L, 3:4], ty.t[0:L, 2:3]), [ty], [ty])
                    V("dve", "scalar_tensor_tensor", (hc1.t[0:L, :], p_num.t[0:L, :], ty.t[0:L, 3:4], sigo.t[0:L, c, :],
                                                      ALU.mult, ALU.mult), [p_num, ty, sigo], [hc1])
                    V("dve", "tensor_copy", (ty.t[0:L, 4:5], zt.t[0:L, 0:1]), [zt], [ty])
                    ACT(junk, junk.t[0:L, :], [hc1], hc1.t[0:L, :], AF.Square, accum=ty.t[0:L, 4:5], extra_w=[ty])
                    ACT(ty, ty.t[0:L, 5:6], [ty, epsf], ty.t[0:L, 4:5], AF.Ln, bias=epsf.t[0:L, 0:1], scale=1.0 / 512)
                    ACT(ty, ty.t[0:L, 5:6], [ty], ty.t[0:L, 5:6], AF.Exp, scale=-0.5)
                    h2 = nxt("h2", hc2)
                    X["h2"] = h2
                    V("dve", "scalar_tensor_tensor", (h2.t[0:L, :], hc1.t[0:L, :], ty.t[0:L, 5:6], nwsz.t[0:L, c, :],
                                                      ALU.mult, ALU.mult), [hc1, ty, nwsz], [h2])

                def stage_T(c, X):
                    h2, cs = X["h2"], X["cs"]
                    p_tr3 = p_tr.t[:, :].rearrange("p (e t) -> p e t", e=4)
                    for ec in range(4):
                        TR(p_tr, p_tr3[:, ec, 0:L], h2, h2.t[0:L, ec * 128:(ec + 1) * 128], identb, identb.t[0:L, 0:L])
                    ACT(hcT, hcT.t[:, h * 4:(h + 1) * 4, cs], [p_tr], p_tr3[:, :, 0:L], AF.Copy)

                ctxs = [None] * nch
                ctxs[0] = stage_B(0)
                for c in range(nch):
                    stage_C1(c, ctxs[c])
                    if not samp:
                        pump(PUMP_P)
                    if c > 0:
                        stage_T(c - 1, ctxs[c - 1])
                    if c + 1 < nch:
                        ctxs[c + 1] = stage_B(c + 1)
                    stage_E(c, ctxs[c])
                stage_T(nch - 1, ctxs[nch - 1])
                if last_prompt:
                    DMA("sp", nCp_d[l, h].rearrange("(k p) e -> p k e", p=128), Cb[h].t[:], [Cb[h]], [], Cb[h])
            if last_prompt:
                DMA("sp", nnp_d[l], nP.t[:], [nP], [], nP)
                DMA("sp", npp_d[l].rearrange("(c p) t -> p c t", p=128), phist.t[:], [phist], [], phist)
                DMA("sp", ncp_d[l].rearrange("(c p) t -> p c t", p=128), chist.t[:], [chist], [], chist)
            if samp:
                DMA("sp", nns_d[l], nS.t[:], [nS], [], nS)

            pump(1000)
            for jb in range(4):
                wb_d, wv_d = sq_blk(wmd_d, l, jb * 256, 128, 16)
                wb_d2, wv_d2 = sq_blk(wmd_d, l, jb * 256 + 128, 128, 16)
                wb_g, wv_g = win_blk(l, O_GB + jb * 256, 256)
                for jj in range(2):
                    j = jb * 2 + jj
                    wbx, wvx = (wb_d, wv_d) if jj == 0 else (wb_d2, wv_d2)
                    pbm = proj_fm(wbx, wvx, 0, NT, hcT)
                    pg = proj_fm(wb_g, wv_g, jj, NT, hT)
                    sg = nxt("F", Fb)
                    ACT(sg, sg.t[:, 0:NT], [pg], pg.t[:, 0:NT], AF.Sigmoid)
                    tt = nxt("F", Fb)
                    V("dve", "tensor_tensor", (tt.t[:, 0:NT], pbm.t[:, 0:NT], sg.t[:, 0:NT], ALU.mult), [pbm, sg], [tt])
                    V("dve", "tensor_tensor", (scrA.t[:, j, 0:NT], tt.t[:, 0:NT], gaT.t[:, j, 0:NT], ALU.add), [tt, gaT], [scrA])
            nti, nl = ti + 1, l
            if nti > ntiles:
                nti, nl = 0, l + 1
            if PREFETCH_X and nl < DEPTH:
                n_samp = nti == ntiles
                nNT = 4 * NSEQ if n_samp else TT
                nc0 = nti * TT
                nsrc = xT_d if nl == 0 else xs_t
                nxt_x = XV[(tile_no[0] + 1) % 2]
                DMA("sp", nxt_x.t[:, :, 0:nNT], nsrc[:, nc0:nc0 + nNT].rearrange("(k p) n -> p k n", p=128),
                    [xs_buf], [nxt_x], nxt_x)
                prefetched[0] = True
            for jb in range(4):
                wb, wv = sq_blk(wout_d, l, jb * 256, 256, 8)
                for jj in range(2):
                    j = jb * 2 + jj
                    pb = proj_fm(wb, wv, jj, NT, scrA)
                    V("dve", "tensor_tensor", (xt.t[:, j, 0:NT], xt.t[:, j, 0:NT], pb.t[:, 0:NT], ALU.add), [xt, pb], [xt])
            rmsnorm_to(xt, hT, NT, C_PNW, prm)
            for jb in range(4):
                wb, wv = sq_blk(wpg_d, l, jb * 256, 256, 8)
                wbp, wvp = sq_blk(wple_d, l, jb * 256, 256, 2)
                for jj in range(2):
                    j = jb * 2 + jj
                    pg = proj_fm(wb, wv, jj, NT, hT)
                    pp = proj_fm(wbp, wvp, jj, NT, peT)
                    sg = nxt("F", Fb)
                    ACT(sg, sg.t[:, 0:NT], [pg], pg.t[:, 0:NT], AF.Sigmoid)
                    tt = nxt("F", Fb)
                    V("dve", "tensor_tensor", (tt.t[:, 0:NT], pp.t[:, 0:NT], sg.t[:, 0:NT], ALU.mult), [pp, sg], [tt])
                    V("dve", "tensor_tensor", (xt.t[:, j, 0:NT], xt.t[:, j, 0:NT], tt.t[:, 0:NT], ALU.add), [xt, tt], [xt])
            if last_layer:
                ACT(scrA, scrA.t[:, :, 0:NT], [xt], xt.t[:, :, 0:NT], AF.Square)
                pb = PR()
                for kc in range(8):
                    MM(pb, pb.t[:, 0:NT], [ones_bf], ones_bf.t[:, :], [scrA], scrA.t[:, kc, 0:NT], kc == 0, kc == 7)
                ACT(rstd, rstd.t[:, 0:NT], [pb, epsf], pb.t[:, 0:NT], AF.Ln, bias=epsf.t[:, 0:1], scale=1.0 / D)
                ACT(rstd, rstd.t[:, 0:NT], [rstd], rstd.t[:, 0:NT], AF.Exp, scale=-0.5)
                for kc in range(8):
                    V("dve", "scalar_tensor_tensor",
                      (xt.t[:, kc, 0:NT], xt.t[:, kc, 0:NT], fnw.t[:, kc:kc + 1], rstd.t[:, 0:NT], ALU.mult, ALU.mult),
                      [xt, fnw, rstd], [xt])
                DMA("sp", yT_d[:, c0:c0 + NT].rearrange("(k p) n -> p k n", p=128), xt.t[:, :, 0:NT], [xt], [], xt)
            else:
                DMA("sp", xs_t[:, c0:c0 + NT].rearrange("(k p) n -> p k n", p=128), xt.t[:, :, 0:NT], [xt], [xs_buf], xt)
            tile_no[0] += 1

    P.emit()
    es.close()
    return nc


cur_cbf = [None]


def make_consts():
    ident = np.eye(128, dtype=np.float32)
    t = np.arange(128)
    maskP = np.where(t[:, None] <= t[None, :], 0.0, NEG).astype(np.float32)
    s64 = np.arange(64)
    same = (s64[:, None] // 4) == (s64[None, :] // 4)
    maskS = np.where(same & (s64[:, None] <= s64[None, :]), 0.0, NEG).astype(np.float32)
    sel = np.zeros((4, 512), np.float32)
    for h in range(4):
        sel[h, h * 128:(h + 1) * 128] = 1.0
    invc = np.zeros((128, 4, 16), np.float32)
    for g in range(4):
        win = 2 << g
        invc[:, g, :] = 1.0 / np.minimum(np.arange(16) + 1, win)
    seg = np.ones((4, 128), np.float32)
    seg[:, 0:64:4] = 0.0
    seg[:, 64:] = 0.0
    seg[:, 64::4] = -1e30
    return dict(c_ident=ident, c_maskP=maskP, c_maskS=maskS, c_sel=sel, c_invc=invc, c_seg=seg,
                c_zero=np.zeros((128, 544), np.float32), c_ones=np.ones((128, 128), np.float32),
                c_eps=np.full((128, 1), EPS, np.float32), c_one1=np.ones((128, 1), np.float32))


def per_partition(v):
    sh = v.shape
    n = sh[-1] // 128
    r = v.reshape(sh[:-1] + (n, 128))
    return np.ascontiguousarray(np.moveaxis(r, -1, 0))


def prepare_inputs(inp, NP, DEPTH, cores):
    consts = make_consts()
    f = lambda a: np.ascontiguousarray(np.asarray(a, dtype=np.float32))
    prm = np.zeros((DEPTH, 128, NPRM), np.float32)
    for l in range(DEPTH):
        prm[l, :, 0:8] = per_partition(f(inp["norm_w"][l]))
        prm[l, :, 8:16] = per_partition(f(inp["ple_norm_w"][l]))
        prm[l, :, 16:24] = per_partition(f(inp["pool_scale"][l]))
        prm[l, :, 24:56] = per_partition(f(inp["conv_b"][l]))
        cw = per_partition(f(inp["conv_w"][l]))
        prm[l, :, 56:184] = np.transpose(cw, (0, 2, 1)).reshape(128, 128)
    fnw = per_partition(f(inp["final_norm_w"]))
    bif = np.ascontiguousarray(np.transpose(f(inp["b_if"])[:DEPTH].reshape(DEPTH, 2, 4), (0, 2, 1)))
    nwB = np.ascontiguousarray(np.broadcast_to(f(inp["mlstm_norm_w"])[:DEPTH, None, :], (DEPTH, 128, 2048)))
    shared = dict(
        w_in=f(inp["w_in"][:DEPTH]), pool_w=f(inp["pool_w"][:DEPTH]), w_pool_down=f(inp["w_pool_down"][:DEPTH]),
        w_mlstm_down=f(inp["w_mlstm_down"][:DEPTH]), w_out=f(inp["w_out"][:DEPTH]), w_ple=f(inp["w_ple"][:DEPTH]),
        w_ple_gate=f(inp["w_ple_gate"][:DEPTH]), prm=prm, fnw=fnw, bif=bif, nwB=nwB, **consts)
    maps = []
    for c in cores:
        bs = slice(NSEQ * c, NSEQ * (c + 1))
        xp = f(inp["x_prompt"][c, :NP])
        xs = f(inp["x_sample"][bs]).reshape(4 * NSEQ, D)
        xT = np.ascontiguousarray(np.concatenate([xp, xs], 0).T)
        pp = f(inp["p_prompt"][:DEPTH, c, :NP])
        psm = f(inp["p_sample"][:DEPTH, bs]).reshape(DEPTH, 4 * NSEQ, 256)
        peT = np.ascontiguousarray(np.transpose(np.concatenate([pp, psm], 1), (0, 2, 1)))
        spT = np.ascontiguousarray(np.transpose(f(inp["state_pool"][:DEPTH, bs]), (0, 3, 1, 2)))
        scT = np.ascontiguousarray(np.transpose(f(inp["state_conv"][:DEPTH, bs]), (0, 3, 1, 2)))
        Cs = f(inp["state_mlstm_C"][:DEPTH, bs])
        n = f(inp["state_mlstm_n"][:DEPTH, bs])
        nT = np.ascontiguousarray(np.transpose(n.reshape(DEPTH, NSEQ, 4, 4, 128), (0, 4, 2, 3, 1)))
        mT = np.ascontiguousarray(np.transpose(f(inp["state_mlstm_m"][:DEPTH, bs]), (0, 2, 1)))
        m = dict(xT=xT, peT=peT, spT=spT, scT=scT, Cs=Cs, nT=nT, mT=mT)
        m.update(shared)
        maps.append(m)
    return maps


def assemble(results, NP, DEPTH, ncores):
    B = ncores
    y_p = np.zeros((B, NP, D), np.float32)
    y_s = np.zeros((B * NSEQ, 4, D), np.float32)
    pool_p = np.zeros((DEPTH, B, 15, D), np.float32)
    pool_s = np.zeros((DEPTH, B * NSEQ, 15, D), np.float32)
    conv_p = np.zeros((DEPTH, B, 3, 4096), np.float32)
    conv_s = np.zeros((DEPTH, B * NSEQ, 3, 4096), np.float32)
    C_p = np.zeros((DEPTH, B, 4, 512, 512), np.float32)
    C_s = np.zeros((DEPTH, B * NSEQ, 4, 512, 512), np.float32)
    n_p = np.zeros((DEPTH, B, 4, 512), np.float32)
    n_s = np.zeros((DEPTH, B * NSEQ, 4, 512), np.float32)
    m_p = np.zeros((DEPTH, B, 4), np.float32)
    m_s = np.zeros((DEPTH, B * NSEQ, 4), np.float32)
    for c, r in enumerate(results):
        bs = slice(NSEQ * c, NSEQ * (c + 1))
        yT = r["yT"]
        y_p[c] = yT[:, :NP].T
        y_s[bs] = yT[:, NP:].T.reshape(NSEQ, 4, D)
        pool_p[:, c] = np.transpose(r["npp"], (0, 2, 1))
        pool_s[:, bs] = np.transpose(r["nps"], (0, 2, 3, 1))
        conv_p[:, c] = np.transpose(r["ncp"], (0, 2, 1))
        conv_s[:, bs] = np.transpose(r["ncs"], (0, 2, 3, 1))
        C_p[:, c] = r["nCp"]
        C_s[:, bs] = r["nCs"]
        n_p[:, c] = np.transpose(r["nnp"], (0, 4, 2, 3, 1)).reshape(DEPTH, 4, 512)
        n_s[:, bs] = np.transpose(r["nns"], (0, 4, 2, 3, 1)).reshape(DEPTH, NSEQ, 4, 512)
        m_p[:, c] = r["nmp"][:, :, 0]
        m_s[:, bs] = np.transpose(r["nms"], (0, 2, 1))
    return (y_p, y_s, pool_p, pool_s, conv_p, conv_s, C_p, C_s, n_p, n_s, m_p, m_s)


def run(inp, NP, DEPTH, cores, trace=False):
    nc = build(NP, DEPTH)
    maps = prepare_inputs(inp, NP, DEPTH, cores)
    res = run_bass_kernel_spmd(nc, maps, core_ids=list(range(len(cores))), trace=trace)
    outs = assemble(res.results, NP, DEPTH, len(cores))
    return outs, res


def kernel(**inputs):
    outs, _ = run(inputs, 2048, 4, list(range(8)))
    return outs
```
